# Optimizing a Trainium2 kernel written in Bass

```python
import math
import jax, jax.numpy as jnp
from jax import lax
import numpy as np

D_MODEL = 1024
BATCH = 8
SEQ = 2048
DEPTH = 1
DEC_BATCH = 128
DEC_SEQ = 4
PAST_LEN = 16384
PAGE_SIZE = 128

W_A = D_MODEL // 2
H_A = 8
BW_A = W_A // H_A
CONV_A = 4
C_RG = 8.0
W_B = D_MODEL - W_A
H_B = 4
HD_B = W_B // H_B
CHUNK = 128
MIX_W = W_A + W_B
D_FF = 3 * D_MODEL
CONV_F = 3
PLE_DIM = 256
EPS = 1e-6

kernel_name = "hymba_style_rglru_sgu_convffn_step"


def _rmsnorm(x, g):
    x32 = x.astype(jnp.float32)
    y = x32 * lax.rsqrt(jnp.mean(x32 * x32, axis=-1, keepdims=True) + EPS)
    return y.astype(x.dtype) * g


def _layernorm(x, g, b):
    x32 = x.astype(jnp.float32)
    mu = jnp.mean(x32, axis=-1, keepdims=True)
    xc = x32 - mu
    y = xc * lax.rsqrt(jnp.mean(xc * xc, axis=-1, keepdims=True) + EPS)
    return y.astype(x.dtype) * g + b


def _causal_dwconv(x, buf, w, b):
    k = w.shape[0]
    L = x.shape[1]
    xp = jnp.concatenate([buf.astype(x.dtype), x], axis=1)
    out = b
    for j in range(k):
        out = out + xp[:, j:j + L] * w[j]
    return out, xp[:, xp.shape[1] - (k - 1):]


def _rglru(x, h0, wa, ba, wx, bx, a_param, reset_first):
    B, L, W = x.shape
    xb = x.reshape(B, L, H_A, BW_A)
    r = jax.nn.sigmoid(jnp.einsum('blhi,hij->blhj', xb, wa).reshape(B, L, W) + ba)
    i = jax.nn.sigmoid(jnp.einsum('blhi,hij->blhj', xb, wx).reshape(B, L, W) + bx)
    log_a = (-C_RG * r.astype(jnp.float32)) * jax.nn.softplus(-a_param.astype(jnp.float32))
    a = jnp.exp(log_a)
    mult = jnp.sqrt(-jnp.expm1(2.0 * log_a))
    if reset_first:
        mult = mult.at[:, 0].set(1.0)
    u = x.astype(jnp.float32) * i.astype(jnp.float32) * mult

    def step(h, inp):
        a_t, u_t = inp
        h = a_t * h + u_t
        return h, h

    hT, hs = lax.scan(step, h0.astype(jnp.float32), (jnp.swapaxes(a, 0, 1), jnp.swapaxes(u, 0, 1)))
    return jnp.swapaxes(hs, 0, 1).astype(x.dtype), hT


def _chunk_sgu(u, v, w_s, b_s):
    B, L, _ = v.shape
    cl = min(L, CHUNK)
    nc = L // cl
    mask = jnp.tril(jnp.ones((cl, cl), dtype=bool))
    w = jnp.where(mask, w_s[:, :cl, :cl], 0.0)
    vc = v.reshape(B, nc, cl, H_B, HD_B)
    mixed = jnp.einsum('hts,bcshd->bcthd', w, vc) + jnp.transpose(b_s[:, :cl])[None, None, :, :, None]
    return u * mixed.reshape(B, L, W_B)


def _layer(h, p, h0, conv_buf, ffn_buf, reset_first, lw):
    n1 = _rmsnorm(h, lw['g_mix_norm'])
    z = n1 @ lw['w_in']
    xa = z[..., :W_A]
    ga = z[..., W_A:2 * W_A]
    ub = z[..., 2 * W_A:2 * W_A + W_B]
    vb = z[..., 2 * W_A + W_B:]
    xa_c, conv_new = _causal_dwconv(xa, conv_buf, lw['conv_a_w'], lw['conv_a_b'])
    ya, hT = _rglru(xa_c, h0, lw['lru_wa'], lw['lru_ba'], lw['lru_wx'], lw['lru_bx'],
                    lw['lru_a_param'], reset_first)
    ya = _rmsnorm(ya * jax.nn.gelu(ga), lw['g_out_a'])
    vn = _layernorm(jax.nn.gelu(vb), lw['ln_v_g'], lw['ln_v_b'])
    yb = _rmsnorm(_chunk_sgu(jax.nn.gelu(ub), vn, lw['sgu_w'], lw['sgu_b']), lw['g_out_b'])
    h = h + jnp.concatenate([ya, yb], axis=-1) @ lw['w_out']
    n2 = _rmsnorm(h, lw['g_ffn_norm'])
    up = n2 @ lw['w_up']
    up_c, ffn_new = _causal_dwconv(up, ffn_buf, lw['ffn_conv_w'], lw['ffn_conv_b'])
    h = h + (jax.nn.gelu(up_c[..., :D_FF]) * up_c[..., D_FF:]) @ lw['w_down']
    gate = jax.nn.sigmoid(_rmsnorm(h, lw['g_ple_norm']) @ lw['w_ple_gate'])
    h = h + (p @ lw['w_ple']) * gate
    return h, hT, conv_new, vn, ffn_new


def setup_inputs(seed: int = 0) -> dict:
    key = jax.random.key(seed)
    ks = jax.random.split(key, 40)
    f32 = jnp.float32
    nrm = lambda k, s, sc: jax.random.normal(k, s, f32) * sc
    gain = lambda k, s: 1.0 + 0.02 * jax.random.normal(k, s, f32)
    u = jax.random.uniform(ks[12], (DEPTH, W_A), f32, minval=0.9, maxval=0.999)
    return {
        "x_prompt": nrm(ks[0], (BATCH, SEQ, D_MODEL), 1.0),
        "x_sample": nrm(ks[1], (DEC_BATCH, DEC_SEQ, D_MODEL), 1.0),
        "p_prompt": nrm(ks[2], (DEPTH, BATCH, SEQ, PLE_DIM), 1.0),
        "p_sample": nrm(ks[3], (DEPTH, DEC_BATCH, DEC_SEQ, PLE_DIM), 1.0),
        "state_rglru_h": nrm(ks[4], (DEPTH, DEC_BATCH, W_A), 0.5),
        "state_rglru_conv": nrm(ks[5], (DEPTH, DEC_BATCH, CONV_A - 1, W_A), 1.0),
        "state_ffn_conv": nrm(ks[6], (DEPTH, DEC_BATCH, CONV_F - 1, 2 * D_FF), 1.0),
        "g_mix_norm": gain(ks[7], (DEPTH, D_MODEL)),
        "w_in": nrm(ks[8], (DEPTH, D_MODEL, 2 * W_A + 2 * W_B), D_MODEL ** -0.5),
        "conv_a_w": nrm(ks[9], (DEPTH, CONV_A, W_A), CONV_A ** -0.5),
        "conv_a_b": nrm(ks[10], (DEPTH, W_A), 0.02),
        "lru_wa": nrm(ks[11], (DEPTH, H_A, BW_A, BW_A), BW_A ** -0.5),
        "lru_ba": nrm(ks[13], (DEPTH, W_A), 0.02),
        "lru_wx": nrm(ks[14], (DEPTH, H_A, BW_A, BW_A), BW_A ** -0.5),
        "lru_bx": nrm(ks[15], (DEPTH, W_A), 0.02),
        "lru_a_param": jnp.log(u) - jnp.log1p(-u),
        "g_out_a": gain(ks[16], (DEPTH, W_A)),
        "ln_v_g": gain(ks[17], (DEPTH, W_B)),
        "ln_v_b": nrm(ks[18], (DEPTH, W_B), 0.02),
        "sgu_w": nrm(ks[19], (DEPTH, H_B, CHUNK, CHUNK), CHUNK ** -0.5),
        "sgu_b": gain(ks[20], (DEPTH, H_B, CHUNK)),
        "g_out_b": gain(ks[21], (DEPTH, W_B)),
        "w_out": nrm(ks[22], (DEPTH, MIX_W, D_MODEL), MIX_W ** -0.5),
        "g_ffn_norm": gain(ks[23], (DEPTH, D_MODEL)),
        "w_up": nrm(ks[24], (DEPTH, D_MODEL, 2 * D_FF), D_MODEL ** -0.5),
        "ffn_conv_w": nrm(ks[25], (DEPTH, CONV_F, 2 * D_FF), CONV_F ** -0.5),
        "ffn_conv_b": nrm(ks[26], (DEPTH, 2 * D_FF), 0.02),
        "w_down": nrm(ks[27], (DEPTH, D_FF, D_MODEL), D_FF ** -0.5),
        "g_ple_norm": gain(ks[28], (DEPTH, D_MODEL)),
        "w_ple_gate": nrm(ks[29], (DEPTH, D_MODEL, D_MODEL), D_MODEL ** -0.5),
        "w_ple": nrm(ks[30], (DEPTH, PLE_DIM, D_MODEL), PLE_DIM ** -0.5),
        "g_final": gain(ks[31], (D_MODEL,)),
    }


def reference(x_prompt, x_sample, p_prompt, p_sample, state_rglru_h, state_rglru_conv,
              state_ffn_conv, g_mix_norm, w_in, conv_a_w, conv_a_b, lru_wa, lru_ba, lru_wx,
              lru_bx, lru_a_param, g_out_a, ln_v_g, ln_v_b, sgu_w, sgu_b, g_out_b, w_out,
              g_ffn_norm, w_up, ffn_conv_w, ffn_conv_b, w_down, g_ple_norm, w_ple_gate,
              w_ple, g_final):
    hp = x_prompt
    hs = x_sample
    Bp = x_prompt.shape[0]
    hP, hS, cP, cS, vS, fP, fS = [], [], [], [], [], [], []
    for i in range(DEPTH):
        lw = {
            'g_mix_norm': g_mix_norm[i], 'w_in': w_in[i], 'conv_a_w': conv_a_w[i],
            'conv_a_b': conv_a_b[i], 'lru_wa': lru_wa[i], 'lru_ba': lru_ba[i],
            'lru_wx': lru_wx[i], 'lru_bx': lru_bx[i], 'lru_a_param': lru_a_param[i],
            'g_out_a': g_out_a[i], 'ln_v_g': ln_v_g[i], 'ln_v_b': ln_v_b[i],
            'sgu_w': sgu_w[i], 'sgu_b': sgu_b[i], 'g_out_b': g_out_b[i], 'w_out': w_out[i],
            'g_ffn_norm': g_ffn_norm[i], 'w_up': w_up[i], 'ffn_conv_w': ffn_conv_w[i],
            'ffn_conv_b': ffn_conv_b[i], 'w_down': w_down[i], 'g_ple_norm': g_ple_norm[i],
            'w_ple_gate': w_ple_gate[i], 'w_ple': w_ple[i],
        }
        z_h = jnp.zeros((Bp, W_A), jnp.float32)
        z_c = jnp.zeros((Bp, CONV_A - 1, W_A), hp.dtype)
        z_f = jnp.zeros((Bp, CONV_F - 1, 2 * D_FF), hp.dtype)
        hp, h_p, c_p, _, f_p = _layer(hp, p_prompt[i], z_h, z_c, z_f, True, lw)
        hs, h_s, c_s, v_s, f_s = _layer(hs, p_sample[i], state_rglru_h[i], state_rglru_conv[i],
                                        state_ffn_conv[i], False, lw)
        hP.append(h_p); hS.append(h_s); cP.append(c_p); cS.append(c_s)
        vS.append(v_s); fP.append(f_p); fS.append(f_s)
    y_prompt = _rmsnorm(hp, g_final)
    y_sample = _rmsnorm(hs, g_final)
    new_rglru_h_prompt = jnp.stack(hP)
    new_rglru_h_sample = jnp.stack(hS)
    new_rglru_conv_prompt = jnp.stack(cP)
    new_rglru_conv_sample = jnp.stack(cS)
    new_sgu_v_sample = jnp.stack(vS)
    new_ffn_conv_prompt = jnp.stack(fP)
    new_ffn_conv_sample = jnp.stack(fS)
    return (y_prompt, y_sample, new_rglru_h_prompt, new_rglru_h_sample, new_rglru_conv_prompt,
            new_rglru_conv_sample, new_sgu_v_sample, new_ffn_conv_prompt, new_ffn_conv_sample)
```

```python
import numpy as np
import concourse.bass as bass
import concourse.mybir as mybir
from concourse.bass_utils import run_bass_kernel_spmd

F32 = mybir.dt.float32
BF16 = mybir.dt.bfloat16
AF = mybir.ActivationFunctionType
ALU = mybir.AluOpType

NCORES = 8
D = 1024
WA = 512
DFF = 3072
EPS = 1e-6
NSLOT = 27
RING = 5
SAME_ENGINE_SYNC = True

C_GMIX, C_GFFN, C_GPLE, C_CAW, C_CAB, C_BA, C_BX, C_AP, C_GOA, C_GOB, C_FW, C_FB = 0, 8, 16, 24, 40, 44, 48, 52, 56, 60, 64, 208
NCV = 256


import types


def _freeze(fn):
    if fn is None or fn.__closure__ is None:
        return fn
    cells = []
    for c in fn.__closure__:
        try:
            cells.append(types.CellType(c.cell_contents))
        except ValueError:
            cells.append(c)
    return types.FunctionType(fn.__code__, fn.__globals__, fn.__name__, fn.__defaults__, tuple(cells))


class Sem:
    def __init__(self, nc, stack, name):
        self.h = stack.enter_context(nc.semaphore(name))
        self.name = name


class Buf:
    __slots__ = ("name", "w", "r")

    def __init__(self, name):
        self.name = name
        self.w = None
        self.r = {}


class Chan:
    def __init__(self, sem):
        self.sem = sem
        self.cnt = 0


class Eng:
    def __init__(self, name, sem, same_sync):
        self.name = name
        self.sem = sem
        self.cnt = 0
        self.waited = {}
        self.ops = []
        self.same_sync = same_sync


class Sched:
    def __init__(self, nc, stack):
        self.nc = nc
        self.stack = stack
        self.nsem = 0
        self.eng = {}
        for n in ("pe", "act", "dve", "pool", "sp"):
            self.eng[n] = Eng(n, self.newsem("e_" + n), SAME_ENGINE_SYNC and n not in ("pe", "sp"))

    def newsem(self, name):
        self.nsem += 1
        return Sem(self.nc, self.stack, name)

    def chan(self, name):
        return Chan(self.newsem("c_" + name))

    def _deps(self, e, reads, writes):
        deps = {}

        def add(s, v):
            if deps.get(s, 0) < v:
                deps[s] = v

        for b in reads:
            if b.w is not None:
                add(*b.w)
        for b in writes:
            if b.w is not None:
                add(*b.w)
            for s, v in b.r.items():
                add(s, v)
        waits = []
        for s, v in deps.items():
            if s is e.sem and not e.same_sync:
                continue
            if e.waited.get(s, 0) >= v:
                continue
            e.waited[s] = v
            waits.append((s, v))
        return waits

    def _commit(self, tok, reads, writes):
        for b in writes:
            b.w = tok
            b.r = {}
        for b in reads:
            if b.r.get(tok[0], 0) < tok[1]:
                b.r[tok[0]] = tok[1]

    def op(self, en, fn, reads=(), writes=(), inc=True):
        e = self.eng[en]
        waits = self._deps(e, reads, writes)
        tok = (e.sem, e.cnt + 1)
        if inc:
            e.cnt += 1
        e.ops.append((waits, _freeze(fn), (e.sem, 1) if inc else None))
        self._commit(tok, reads, writes)
        return tok

    def dma(self, en, ch, fn, reads=(), writes=()):
        e = self.eng[en]
        waits = self._deps(e, reads, writes)
        if en == "pool" and ch.cnt > 0 and e.waited.get(ch.sem, 0) < ch.cnt:
            e.waited[ch.sem] = ch.cnt
            waits.append((ch.sem, ch.cnt))
        ch.cnt += 16
        tok = (ch.sem, ch.cnt)
        e.ops.append((waits, _freeze(fn), (ch.sem, 16)))
        self._commit(tok, reads, writes)
        return tok

    def wait(self, en, tok):
        e = self.eng[en]
        if e.waited.get(tok[0], 0) >= tok[1]:
            return
        e.waited[tok[0]] = tok[1]
        e.ops.append(([tok], None, None))

    def replay(self):
        nc = self.nc
        with nc.Block() as block:
            def run(eobj, e):
                for waits, fn, inc in e.ops:
                    for s, v in waits:
                        eobj.wait_ge(s.h, v)
                    if fn is None:
                        continue
                    ins = fn(eobj)
                    if inc is not None:
                        ins.then_inc(inc[0].h, inc[1])

            @block.tensor
            def _(x):
                run(x, self.eng["pe"])

            @block.scalar
            def _(x):
                run(x, self.eng["act"])

            @block.vector
            def _(x):
                run(x, self.eng["dve"])

            @block.gpsimd
            def _(x):
                run(x, self.eng["pool"])

            @block.sync
            def _(x):
                run(x, self.eng["sp"])


def build_nc(ntiles=5):
    from contextlib import ExitStack
    nc = bass.Bass("TRN2", target_bir_lowering=False)
    stack = ExitStack()

    def din(name, shape, dt=F32):
        return nc.dram_tensor(name, list(shape), dt, kind="ExternalInput").ap()

    def dout(name, shape, dt=F32):
        return nc.dram_tensor(name, list(shape), dt, kind="ExternalOutput").ap()

    xp = din("xp", [2048, D]); xs = din("xs", [64, D])
    ppT = din("ppT", [256, 2048]); psT = din("psT", [256, 64])
    wall = din("wall", [NSLOT, 128, 4096])
    cvec_d = din("cvec", [128, NCV]); gfin_d = din("gfin_b", [128, D])
    lng_d = din("lng_b", [128, 512]); lnb_d = din("lnb_b", [128, 512])
    ident_d = din("ident", [128, 128]); maskT_d = din("maskT", [128, 128])
    sguT_d = din("sguT", [128, 512]); wabd_d = din("wabd", [128, 512]); wxbd_d = din("wxbd", [128, 512])
    bsrow_d = din("bsrow", [1, 512]); Rs_d = din("Rs", [64, 256]); masks_d = din("mask_s", [64, 64])
    bsrow_s_d = din("bsrow_s", [1, 256])
    sth_d = din("st_hT", [128, 64]); stc_d = din("st_cT", [128, 192]); stf_d = din("st_fT", [128, 1536])

    y_p = dout("y_p", [2048, D]); y_s = dout("y_s", [64, D])
    o_hp = dout("o_hp", [128, 4]); o_hs = dout("o_hs", [128, 64])
    o_cp = dout("o_cp", [128, 12]); o_cs = dout("o_cs", [128, 192])
    o_vs = dout("o_vs", [64, 512]); o_fp = dout("o_fp", [128, 96]); o_fs = dout("o_fs", [128, 1536])
    wbf = nc.dram_tensor("wbf", [NSLOT, 128, 4096], BF16, kind="Internal").ap()

    def sb(name, shape, dt=F32):
        return stack.enter_context(nc.sbuf_tensor(name, list(shape), dt))

    hbuf = sb("hbuf", [128, 2, 4, D])
    nscr = sb("nscr", [128, 2, D])
    junk = sb("junk", [128, D], BF16)
    nT = sb("nT", [128, 8, 512], BF16)
    xa_ext = sb("xa_ext", [128, 4, 520])
    xc = sb("xc", [128, 4, 512]); xcb = sb("xcb", [128, 4, 512], BF16)
    e4 = sb("e4", [128, 4, 512]); ig4 = sb("ig4", [128, 4, 512]); rr4 = sb("rr4", [128, 4, 512])
    shared = sb("shared", [128, 6144])
    vnb = sb("vnb", [128, 4, 512], BF16)
    hs = sb("hs", [128, 2, 512])
    yabT = sb("yabT", [128, 8, 512], BF16)
    corr = sb("corr", [128, 48, 2])
    pTb = sb("pTb", [128, 2, 2, 512], BF16)
    cvec = sb("cvec_s", [128, NCV]); gfin_b = sb("gfin_s", [128, D])
    lng_b = sb("lng_s", [128, 512]); lnb_b = sb("lnb_s", [128, 512])
    ident = sb("ident_s", [128, 128])
    wsgu = sb("wsgu", [128, 4, 128], BF16); wabd = sb("wabd_s", [128, 4, 128], BF16); wxbd = sb("wxbd_s", [128, 4, 128], BF16)
    wsgu_s = sb("wsgu_ss", [128, 4, 64], BF16)
    bsh = sb("bsh", [1, 768], BF16); bsl = sb("bsl", [1, 768], BF16)
    bsf = hbuf[0:1, 1, 0, 0:768]
    ones_bf = sb("ones_bf", [1, 128], BF16); ones_f = sb("ones_f", [128, 2]); ones_b2 = sb("ones_b2", [128, 2], BF16); mhalf = sb("mhalf", [128, 8])
    st_hT = sb("st_hT_s", [128, 4, 16]); st_cT = sb("st_cT_s", [128, 4, 16, 3]); st_fT = sb("st_fT_s", [128, 48, 16, 2])
    fext = sb("fext", [128, 2, 16, 6])
    hsout = sb("hsout", [128, 4, 16]); tmp16 = sb("tmp16", [128, 16]); cs_out = sb("cs_out", [128, 4, 16, 3])
    xhist = sb("xhist", [128, 4, 3]); hlast = sb("hlast", [128, 4]); fhist = sb("fhist", [128, 48, 2])
    cA = sb("cA", [128, 4]); cA2 = sb("cA2", [128, 4]); spt = sb("spt", [128, 4])
    ss = sb("ss", [128, 4]); ms = sb("ms", [128, 4]); rstd = sb("rstd", [128, 4])
    ssn = sb("ssn", [128, 4]); msn = sb("msn", [128, 4]); rstdn = sb("rstdn", [128, 4])
    ms8 = sb("ms8", [128, 8]); rstd8 = sb("rstd8", [128, 8])
    bst = sb("bst", [128, 2, 6]); mv = sb("mv", [128, 2, 2]); rsv = sb("rsv", [128, 2])
    ring = sb("ring", [128, RING, 8, 512], BF16)
    psum = [stack.enter_context(nc.psum_tensor("ps%d" % i, [128, 512], F32)) for i in range(8)]

    actT = shared[:, :].bitcast(BF16).rearrange("p (j t) -> p j t", t=512)
    sh3 = shared[:, :].rearrange("p (a t) -> p a t", t=512)

    S = Sched(nc, stack)
    U = [Buf("U%d" % i) for i in range(24)]
    B_gga = [[U[2 * c], U[2 * c + 1]] for c in range(4)]
    B_gub = [[U[8 + 2 * c], U[9 + 2 * c]] for c in range(4)]
    B_gvb = [[U[16 + 2 * k], U[17 + 2 * k]] for k in range(2)]
    B_rr4 = [Buf("rr%d" % c) for c in range(4)]; B_e4 = [Buf("e4_%d" % c) for c in range(4)]; B_ig4 = [Buf("ig4_%d" % c) for c in range(4)]
    B_act = [[U[j]] for j in range(24)]
    B_h = [[Buf("h%d_%d" % (a, t)) for t in range(4)] for a in range(2)]
    B_nscr = [Buf("nscr0"), Buf("nscr1")]; B_junk = Buf("junk"); B_corr = Buf("corr")
    B_nT = [Buf("nT%d" % k) for k in range(8)]
    B_xa = [Buf("xa%d" % c) for c in range(4)]
    B_xc = [Buf("xc%d" % k) for k in range(4)]; B_xcb = [Buf("xcb%d" % k) for k in range(4)]
    B_vnb = [Buf("vnb%d" % t) for t in range(4)]
    B_hs = [Buf("hs%d" % c) for c in range(2)]
    B_yab = [Buf("yab%d" % c) for c in range(8)]
    B_fc = [[B_e4[2 * s_ + p_] for p_ in range(2)] for s_ in range(2)]
    B_fG = [B_ig4[0], B_ig4[1]]
    B_gt = B_fG; B_pt = [B_fc[0][0], B_fc[1][0]]
    B_pTb = [Buf("pTb%d" % k) for k in range(2)]
    B_ps = [Buf("ps%d" % k) for k in range(8)]
    B_ring = [Buf("ring%d" % k) for k in range(RING)]
    B_wbf = [Buf("wbf%d" % k) for k in range(NSLOT)]
    B_xhist = [Buf("xhist%d" % c) for c in range(4)]
    B_hlast = [Buf("hlast%d" % c) for c in range(4)]
    B_fhist = [Buf("fhist%d" % f) for f in range(48)]
    B_stf = [Buf("stf%d" % f) for f in range(48)]
    B_fext = [Buf("fext%d" % k) for k in range(2)]
    B_hsout = Buf("hsout"); B_tmp16 = Buf("tmp16"); B_csout = Buf("csout")
    B_ssn = [Buf("ssn%d" % t) for t in range(4)]; B_msn = [Buf("msn%d" % t) for t in range(4)]; B_rstdn = [Buf("rstdn%d" % t) for t in range(4)]
    B_ss = Buf("ss"); B_ms = Buf("ms"); B_rstd = Buf("rstd"); B_ms8 = Buf("ms8"); B_rstd8 = Buf("rstd8")
    B_bst = [Buf("bst%d" % k) for k in range(2)]; B_mv = [Buf("mv%d" % k) for k in range(2)]; B_rsv = [Buf("rsv%d" % k) for k in range(2)]
    B_const = Buf("const")

    ch_setup = S.chan("setup")
    setup_loads = [
        (cvec[:, :], cvec_d), (gfin_b[:, :], gfin_d), (lng_b[:, :], lng_d), (lnb_b[:, :], lnb_d),
        (ident[:, :], ident_d), (nscr[:, 0, 0:512], sguT_d), (nscr[:, 0, 512:640], maskT_d),
        (nscr[0:64, 0, 640:896], Rs_d), (nscr[0:64, 0, 896:960], masks_d),
        (bsf[:, 0:512], bsrow_d), (bsf[:, 512:768], bsrow_s_d),
        (st_hT[:, :, :].rearrange("p a b -> p (a b)"), sth_d),
        (st_cT[:, :, :, :].rearrange("p a b c -> p (a b c)"), stc_d),
        (st_fT[:, :, :, :].rearrange("p a b c -> p (a b c)"), stf_d),
    ]
    for o_, i_ in setup_loads:
        S.dma("sp", ch_setup, lambda e, o_=o_, i_=i_: e.dma_start(out=o_, in_=i_), writes=[])
    B_gw = Buf("gatew")
    ch_gw = [S.chan("gw0"), S.chan("gw1")]
    S.dma("pool", ch_gw[0], lambda e: e.dma_start(out=wabd[:, :, :].rearrange("p a b -> p (a b)"), in_=wabd_d), writes=[B_gw])
    tgw = S.dma("pool", ch_gw[1], lambda e: e.dma_start(out=wxbd[:, :, :].rearrange("p a b -> p (a b)"), in_=wxbd_d), writes=[])
    tok_setup = (ch_setup.sem, ch_setup.cnt)
    B_const.w = tok_setup
    B_nscr[0].w = tok_setup
    B_h[1][0].w = tok_setup

    def DV(fn, r=(), w=()):
        return S.op("dve", fn, reads=r, writes=w)

    def AC(fn, r=(), w=()):
        return S.op("act", fn, reads=r, writes=w)

    for h_ in range(4):
        DV(lambda e, h_=h_: e.tensor_tensor(out=wsgu[:, h_, :], in0=nscr[:, 0, h_ * 128:(h_ + 1) * 128], in1=nscr[:, 0, 512:640], op=ALU.mult),
           r=[B_nscr[0]], w=[B_const])
        DV(lambda e, h_=h_: e.tensor_tensor(out=wsgu_s[0:64, h_, :], in0=nscr[0:64, 0, 640 + h_ * 64:640 + (h_ + 1) * 64], in1=nscr[0:64, 0, 896:960], op=ALU.mult),
           r=[B_nscr[0]], w=[B_const])
    DV(lambda e: e.tensor_copy(out=bsh[0:1, :], in_=bsf), r=[B_const, B_h[1][0]], w=[B_const])
    bsg_v = nscr[0:1, 1, 0:768]
    DV(lambda e: e.tensor_copy(out=bsg_v, in_=bsh[0:1, :]), r=[B_const], w=[B_const, B_nscr[1]])
    DV(lambda e: e.tensor_tensor(out=bsl[0:1, :], in0=bsf, in1=bsg_v, op=ALU.subtract), r=[B_const, B_nscr[1], B_h[1][0]], w=[B_const])
    DV(lambda e: e.memset(ones_bf[0:1, :], 1.0), w=[B_const])
    DV(lambda e: e.memset(ones_f[:, :], 1.0), w=[B_const])
    DV(lambda e: e.memset(ones_b2[:, :], 1.0), w=[B_const])
    DV(lambda e: e.memset(mhalf[:, :], -0.5), w=[B_const])
    DV(lambda e: e.memset(xhist[:, :, :], 0.0), w=B_xhist)
    DV(lambda e: e.memset(hlast[:, :], 0.0), w=B_hlast)
    DV(lambda e: e.memset(fhist[:, :, :], 0.0), w=B_fhist)
    AC(lambda e: e.activation(out=spt[:, :], in_=cvec[:, C_AP:C_AP + 4], func=AF.Exp, scale=-1.0), r=[B_const], w=[B_const])
    AC(lambda e: e.activation(out=spt[:, :], in_=spt[:, :], func=AF.Ln, bias=1.0), r=[B_const], w=[B_const])
    DV(lambda e: e.tensor_scalar(out=cA[:, :], in0=spt[:, :], scalar1=-8.0, scalar2=None, op0=ALU.mult), r=[B_const], w=[B_const])
    DV(lambda e: e.tensor_scalar(out=cA2[:, :], in0=spt[:, :], scalar1=-16.0, scalar2=None, op0=ALU.mult), r=[B_const], w=[B_const])

    ch_wbf = [S.chan("wbf%d" % s_) for s_ in range(NSLOT)]
    ch_ring = [S.chan("ring%d" % k) for k in range(RING)]
    total_loads = NSLOT * ntiles
    st = {"next_load": 0, "released": [False] * total_loads, "prefix": 0, "cast": 0, "bank": 0}

    def emit_cast(s_):
        src = wall[s_, :, :].rearrange("p (a b) -> p a b", b=2048)
        dst = wbf[s_, :, :].rearrange("p (a b) -> p a b", b=2048)
        S.dma("pool", ch_wbf[s_], lambda e: e.dma_start(out=dst, in_=src), writes=[B_wbf[s_]])

    def pump():
        while st["next_load"] < total_loads and st["next_load"] - RING < st["prefix"]:
            g = st["next_load"]
            s_ = g % NSLOT
            k = g % RING
            S.dma("sp", ch_ring[k],
                  lambda e, s_=s_, k=k: e.dma_start(out=ring[:, k, :, :].rearrange("p a b -> p (a b)"), in_=wbf[s_, :, :]),
                  reads=[B_wbf[s_]], writes=[B_ring[k]])
            st["next_load"] += 1

    def wget(g):
        pump()
        assert g < st["next_load"], "weight slot %d not loaded (ring deadlock)" % g
        return g % RING

    def wrel(g):
        st["released"][g] = True
        while st["prefix"] < total_loads and st["released"][st["prefix"]]:
            st["prefix"] += 1
        pump()

    reserved = set()

    def nbank():
        b = st["bank"]
        while b in reserved:
            b = (b + 1) % 8
        st["bank"] = (b + 1) % 8
        return b

    def mm_group(bank, out_ap, pairs, reads, first_start=True, skip=False):
        n = len(pairs)
        for i_, (l_, r_) in enumerate(pairs):
            S.op("pe", lambda e, l_=l_, r_=r_, i_=i_: e.matmul(out_ap, l_, r_, start=(first_start and i_ == 0), stop=(i_ == n - 1),
                                                         skip_group_check=skip),
                 reads=reads, writes=[B_ps[bank]], inc=(i_ == n - 1))

    ch_x = [S.chan("x%d" % k) for k in range(2)]
    ch_p = [S.chan("p%d" % k) for k in range(2)]
    ch_y = [S.chan("y%d" % k) for k in range(2)]
    ch_out = S.chan("out")
    out_toks = []
    pool_toks = []

    def tile_info(i):
        if i < 4:
            return dict(T=512, NTS=4, TP=128, nb=1, L=512, sample=False)
        return dict(T=64, NTS=1, TP=64, nb=16, L=4, sample=True)

    def emit_xload(i):
        hb = i % 2
        ti = tile_info(i)
        if not ti["sample"]:
            src = xp[i * 512:(i + 1) * 512, :].rearrange("(t p) f -> p t f", p=128)
            S.dma("sp", ch_x[hb], lambda e: e.dma_start(out=hbuf[:, hb, :, :], in_=src), writes=B_h[hb])
            psrc = ppT[:, i * 512:(i + 1) * 512].rearrange("(k p) t -> p k t", p=128)
            S.dma("pool", ch_p[hb], lambda e: e.dma_start(out=pTb[:, hb, :, :], in_=psrc), writes=[B_pTb[hb]])
        else:
            S.dma("sp", ch_x[hb], lambda e: e.dma_start(out=hbuf[0:64, hb, 0, :], in_=xs[:, :]), writes=[B_h[hb][0]])
            psrc = psT[:, :].rearrange("(k p) t -> p k t", p=128)
            S.dma("pool", ch_p[hb], lambda e: e.dma_start(out=pTb[:, hb, :, 0:64], in_=psrc), writes=[B_pTb[hb]])

    def rstd_chain(src_ap, ms_ap, rstd_ap, scale, Bsrc, Bms, Brstd):
        DV(lambda e: e.tensor_scalar(out=ms_ap, in0=src_ap, scalar1=scale, scalar2=EPS, op0=ALU.mult, op1=ALU.add), r=Bsrc, w=[Bms])
        npart, ncol = ms_ap.shape[0], ms_ap.shape[1]
        S.op("pool", lambda e: e.tensor_tensor(out=rstd_ap, in0=ms_ap, in1=mhalf[:npart, 0:ncol], op=ALU.pow), reads=[Bms, B_const], writes=[Brstd])

    def norm(hb, gcol, ti, phase="all"):
        T, NTS, TP = ti["T"], ti["NTS"], ti["TP"]
        banks = list(range(8))
        NPRE = min(2, NTS)

        def stat(ts):
            AC(lambda e: e.activation(out=junk[:TP, :], in_=hbuf[:TP, hb, ts, :], func=AF.Square, accum_out=ssn[:TP, ts:ts + 1]),
               r=[B_h[hb][ts]], w=[B_junk, B_ssn[ts]])
            DV(lambda e: e.tensor_scalar(out=msn[:TP, ts:ts + 1], in0=ssn[:TP, ts:ts + 1], scalar1=1.0 / D, scalar2=EPS, op0=ALU.mult, op1=ALU.add),
               r=[B_ssn[ts]], w=[B_msn[ts]])
            S.op("pool", lambda e: e.tensor_tensor(out=rstdn[:TP, ts:ts + 1], in0=msn[:TP, ts:ts + 1], in1=mhalf[:TP, 0:1], op=ALU.pow),
                 reads=[B_msn[ts], B_const], writes=[B_rstdn[ts]])

        def scale(ts):
            nk = ts % 2
            DV(lambda e: e.tensor_scalar(out=nscr[:TP, nk, :], in0=hbuf[:TP, hb, ts, :], scalar1=rstdn[:TP, ts:ts + 1], scalar2=None, op0=ALU.mult),
               r=[B_h[hb][ts], B_rstdn[ts]], w=[B_nscr[nk]])

        def transp(ts):
            nk = ts % 2
            for kt in range(8):
                S.op("pe", lambda e, kt=kt: e.transpose(out=psum[banks[kt]][:, ts * 128:ts * 128 + TP], in_=nscr[:TP, nk, kt * 128:(kt + 1) * 128],
                                                        identity=ident[:TP, :TP]),
                     reads=[B_nscr[nk], B_const], writes=[B_ps[banks[kt]]], inc=(kt == 7))

        if phase == "pre":
            for ts in range(NTS):
                stat(ts)
            for ts in range(NPRE):
                scale(ts)
            return
        if phase == "post":
            for ts in range(NPRE):
                transp(ts)
            for ts in range(NPRE, NTS):
                scale(ts)
                transp(ts)
        else:
            order = []
            for ts in range(NTS):
                order.append(("stat", ts))
                if ts >= 1:
                    order.append(("scale", ts - 1))
            order.append(("scale", NTS - 1))
            for kind, ts in order:
                if kind == "stat":
                    stat(ts)
                else:
                    scale(ts)
                    transp(ts)
        for kt in range(8):
            if kt % 2 == 0:
                AC(lambda e, kt=kt: e.activation(out=nT[:, kt, 0:T], in_=psum[banks[kt]][:, 0:T], func=AF.Identity, scale=cvec[:, gcol + kt:gcol + kt + 1]),
                   r=[B_ps[banks[kt]], B_const], w=[B_nT[kt]])
            else:
                DV(lambda e, kt=kt: e.tensor_scalar(out=nT[:, kt, 0:T], in0=psum[banks[kt]][:, 0:T], scalar1=cvec[:, gcol + kt:gcol + kt + 1], scalar2=None, op0=ALU.mult),
                   r=[B_ps[banks[kt]], B_const], w=[B_nT[kt]])
        st["bank"] = 0

    def outdma(dst, src, reads):
        pool_toks.append(S.dma("pool", S.chan("o%d" % len(pool_toks)), lambda e: e.dma_start(out=dst, in_=src), reads=reads))

    emit_xload(0)
    S.wait("pool", tok_setup)
    S.wait("pool", (ch_x[0].sem, ch_x[0].cnt))
    for s_ in range(8):
        emit_cast(s_)
    for i in range(ntiles):
        ti = tile_info(i)
        T, NTS, TP, nb, L, sample = ti["T"], ti["NTS"], ti["TP"], ti["nb"], ti["L"], ti["sample"]
        hb = i % 2
        g0 = i * NSLOT
        def v3(ap, l=L):
            return ap.rearrange("p (b l) -> p b l", l=l) if sample else ap

        if i == 0:
            norm(hb, C_GMIX, ti)
            for s_ in range(8, NSLOT):
                emit_cast(s_)

        EXT = 3 + L
        def phaseA1():
            for ct in range(4):
                xav = xa_ext[:, ct, 0:nb * EXT].rearrange("p (b l) -> p b l", l=EXT)
                xcv = xc[:, ct, 0:T].rearrange("p (b l) -> p b l", l=L)
                cw = C_CAW + ct * 4
                DV(lambda e, xav=xav, xcv=xcv, cw=cw, ct=ct: e.tensor_scalar(out=xcv, in0=xav[:, :, 0:L], scalar1=cvec[:, cw:cw + 1], scalar2=cvec[:, C_CAB + ct:C_CAB + ct + 1],
                                                                        op0=ALU.mult, op1=ALU.add), r=[B_xa[ct], B_const], w=[B_xc[ct]])
                for j in range(1, 4):
                    DV(lambda e, xav=xav, xcv=xcv, cw=cw, j=j: e.scalar_tensor_tensor(out=xcv, in0=xav[:, :, j:j + L], scalar=cvec[:, cw + j:cw + j + 1], in1=xcv,
                                                                                 op0=ALU.mult, op1=ALU.add), r=[B_xa[ct], B_const, B_xc[ct]], w=[B_xc[ct]])
                DV(lambda e, ct=ct: e.tensor_copy(out=xcb[:, ct, 0:T], in_=xc[:, ct, 0:T]), r=[B_xc[ct]], w=[B_xcb[ct]])


        sbank = nbank()
        reserved.add(sbank)
        first_stat = [True]

        def stats_mm(sq_ap, Bsq, base):
            for ts in range(NTS):
                fs = first_stat[0]
                first_stat[0] = False
                S.op("pe", lambda e, ts=ts, fs=fs: e.matmul(psum[sbank][:TP, base + 2 * ts:base + 2 * ts + 2], sq_ap[:, ts * 128:ts * 128 + TP], ones_b2[:, 0:2],
                                                        start=fs, stop=True, skip_group_check=True),
                     reads=[Bsq, B_const], writes=[B_ps[sbank]], inc=(ts == NTS - 1))

        def gatesA(ct):
            b1 = nbank()
            S.wait("pe", tgw)
            mm_group(b1, psum[b1][:, 0:T], [(wabd[:, ct, :], xcb[:, ct, 0:T])], reads=[B_const, B_gw, B_xcb[ct]])
            b2 = nbank()
            mm_group(b2, psum[b2][:, 0:T], [(wxbd[:, ct, :], xcb[:, ct, 0:T])], reads=[B_const, B_xcb[ct]])
            return b1, b2

        def sigA(ct, b1, b2):
            AC(lambda e: e.activation(out=rr4[:, ct, 0:T], in_=psum[b1][:, 0:T], func=AF.Sigmoid, bias=cvec[:, C_BA + ct:C_BA + ct + 1]),
               r=[B_ps[b1], B_const], w=[B_rr4[ct]])
            AC(lambda e: e.activation(out=ig4[:, ct, 0:T], in_=psum[b2][:, 0:T], func=AF.Sigmoid, bias=cvec[:, C_BX + ct:C_BX + ct + 1]),
               r=[B_ps[b2], B_const], w=[B_ig4[ct]])

        def expA(ct):
            AC(lambda e: e.activation(out=e4[:, ct, 0:T], in_=rr4[:, ct, 0:T], func=AF.Exp, scale=cA2[:, ct:ct + 1]), r=[B_rr4[ct], B_const], w=[B_e4[ct]])
            AC(lambda e: e.activation(out=rr4[:, ct, 0:T], in_=rr4[:, ct, 0:T], func=AF.Exp, scale=cA[:, ct:ct + 1]), r=[B_rr4[ct], B_const], w=[B_rr4[ct]])

        def sqrtA(ct):
            AC(lambda e: e.activation(out=e4[:, ct, 0:T], in_=e4[:, ct, 0:T], func=AF.Sqrt, scale=-1.0, bias=1.0), r=[B_e4[ct]], w=[B_e4[ct]])

        def postA(ct):
            hk = ct % 2
            rr = rr4[:, ct, 0:T]
            ee = e4[:, ct, 0:T]
            ig = ig4[:, ct, 0:T]
            xcc = xc[:, ct, 0:T]
            if i == 0:
                DV(lambda e: e.memset(e4[:, ct, 0:1], 1.0), w=[B_e4[ct]])
            DV(lambda e: e.tensor_tensor(out=ig, in0=ig, in1=ee, op=ALU.mult), r=[B_ig4[ct], B_e4[ct]], w=[B_ig4[ct]])
            DV(lambda e: e.tensor_tensor(out=xcc, in0=xcc, in1=ig, op=ALU.mult), r=[B_ig4[ct], B_xc[ct]], w=[B_xc[ct]])
            if sample:
                rv = rr.rearrange("p (b l) -> p b l", l=L)
                uv = xcc.rearrange("p (b l) -> p b l", l=L)
                DV(lambda e: e.tensor_tensor(out=tmp16[:, :], in0=rv[:, :, 0], in1=st_hT[:, ct, :], op=ALU.mult), r=[B_rr4[ct], B_const], w=[B_tmp16])
                DV(lambda e: e.tensor_tensor(out=uv[:, :, 0], in0=uv[:, :, 0], in1=tmp16[:, :], op=ALU.add), r=[B_tmp16, B_xc[ct]], w=[B_xc[ct]])
                DV(lambda e: e.memset(rv[:, :, 0], 0.0), w=[B_rr4[ct]])
                DV(lambda e: e.tensor_tensor_scan(out=hs[:, hk, 0:T], data0=rr, data1=xcc, initial=0.0, op0=ALU.mult, op1=ALU.add),
                   r=[B_rr4[ct], B_xc[ct]], w=[B_hs[hk]])
                DV(lambda e: e.tensor_copy(out=hsout[:, ct, :], in_=hs[:, hk, 0:T].rearrange("p (b l) -> p b l", l=L)[:, :, L - 1]), r=[B_hs[hk]], w=[B_hsout])
            else:
                DV(lambda e: e.tensor_tensor_scan(out=hs[:, hk, 0:T], data0=rr, data1=xcc, initial=hlast[:, ct:ct + 1], op0=ALU.mult, op1=ALU.add),
                   r=[B_rr4[ct], B_xc[ct], B_hlast[ct]], w=[B_hs[hk]])
                DV(lambda e: e.tensor_copy(out=hlast[:, ct:ct + 1], in_=hs[:, hk, L - 1:L]), r=[B_hs[hk]], w=[B_hlast[ct]])
            DV(lambda e: e.tensor_tensor(out=ee, in0=hs[:, hk, 0:T], in1=sh3[:, ct, 0:T], op=ALU.mult), r=[B_hs[hk]] + B_gga[ct], w=[B_e4[ct]])

        def postA_act(ct):
            ee = e4[:, ct, 0:T]
            sqv = ig4[:, ct, :].bitcast(BF16)[:, 0:T]
            AC(lambda e: e.activation(out=sqv, in_=ee, func=AF.Square), r=[B_e4[ct]], w=[B_ig4[ct]])
            stats_mm(sqv, B_ig4[ct], 0)
            AC(lambda e: e.activation(out=yabT[:, ct, 0:T], in_=ee, func=AF.Identity, scale=cvec[:, C_GOA + ct:C_GOA + ct + 1]),
               r=[B_e4[ct], B_const], w=[B_yab[ct]])

        for c in range(3):
            k = wget(g0 + c)
            for ct in range(4):
                bank = nbank()
                mm_group(bank, psum[bank][:, 0:T], [(ring[:, k, kt, ct * 128:(ct + 1) * 128], nT[:, kt, 0:T]) for kt in range(8)],
                         reads=[B_ring[k]] + B_nT)
                if c == 0:
                    xav = xa_ext[:, ct, 0:nb * EXT].rearrange("p (b l) -> p b l", l=EXT)
                    if sample:
                        DV(lambda e, xav=xav, ct=ct: e.tensor_copy(out=xav[:, :, 0:3], in_=st_cT[:, ct, :, :]), r=[B_const], w=[B_xa[ct]])
                    else:
                        DV(lambda e, xav=xav, ct=ct: e.tensor_copy(out=xav[:, :, 0:3], in_=xhist[:, ct:ct + 1, :]), r=[B_xhist[ct]], w=[B_xa[ct]])
                    DV(lambda e, xav=xav, bank=bank: e.tensor_copy(out=xav[:, :, 3:3 + L], in_=psum[bank][:, 0:T].rearrange("p (b l) -> p b l", l=L)),
                       r=[B_ps[bank]], w=[B_xa[ct]])
                    if not sample:
                        DV(lambda e, xav=xav, ct=ct: e.tensor_copy(out=xhist[:, ct:ct + 1, :], in_=xav[:, :, L:L + 3]), r=[B_xa[ct]], w=[B_xhist[ct]])
                    else:
                        DV(lambda e, xav=xav, ct=ct: e.tensor_copy(out=cs_out[:, ct, :, :], in_=xav[:, :, 4:7]), r=[B_xa[ct]], w=[B_csout])
                        if ct == 3:
                            outdma(o_cs[:, :], cs_out[:, :, :, :].rearrange("p a b c -> p (a b c)"), [B_csout])
                elif c == 1:
                    AC(lambda e, bank=bank, ct=ct: e.activation(out=sh3[:, ct, 0:T], in_=psum[bank][:, 0:T], func=AF.Gelu_apprx_tanh),
                       r=[B_ps[bank]], w=B_gga[ct])
                else:
                    AC(lambda e, bank=bank, ct=ct: e.activation(out=sh3[:, 4 + ct, 0:T], in_=psum[bank][:, 0:T], func=AF.Gelu_apprx_tanh),
                       r=[B_ps[bank]], w=B_gub[ct])
            wrel(g0 + c)
            if c == 0:
                phaseA1()
        k = wget(g0 + 3)

        def vb_mm(ts):
            bank = nbank()
            kk = ts % 2
            mm_group(bank, psum[bank][:TP, :], [(nT[:, kt, ts * 128:ts * 128 + TP], ring[:, k, kt, :]) for kt in range(8)],
                     reads=[B_ring[k]] + B_nT)
            gv = sh3[:TP, 8 + kk, :]
            AC(lambda e, bank=bank, gv=gv: e.activation(out=gv, in_=psum[bank][:TP, :], func=AF.Gelu_apprx_tanh), r=[B_ps[bank]], w=B_gvb[kk])

        def vb_ln(ts):
            kk = ts % 2
            gv = sh3[:TP, 8 + kk, :]
            DV(lambda e, gv=gv, kk=kk: e.bn_stats(out=bst[:TP, kk, :], in_=gv), r=B_gvb[kk], w=[B_bst[kk]])
            DV(lambda e, kk=kk: e.bn_aggr(out=mv[:TP, kk, :], in_=bst[:TP, kk, :]), r=[B_bst[kk]], w=[B_mv[kk]])
            DV(lambda e, kk=kk: e.tensor_scalar(out=rsv[:TP, kk:kk + 1], in0=mv[:TP, kk, 1:2], scalar1=1.0, scalar2=EPS, op0=ALU.mult, op1=ALU.add),
               r=[B_mv[kk]], w=[B_rsv[kk]])
            S.op("pool", lambda e, kk=kk: e.tensor_tensor(out=rsv[:TP, kk:kk + 1], in0=rsv[:TP, kk:kk + 1], in1=mhalf[:TP, 0:1], op=ALU.pow),
                 reads=[B_rsv[kk], B_const], writes=[B_rsv[kk]])
            DV(lambda e, gv=gv, kk=kk: e.tensor_scalar(out=gv, in0=gv, scalar1=mv[:TP, kk, 0:1], scalar2=rsv[:TP, kk:kk + 1], op0=ALU.subtract, op1=ALU.mult),
               r=B_gvb[kk] + [B_mv[kk], B_rsv[kk]], w=B_gvb[kk])
            S.op("pool", lambda e, gv=gv: e.tensor_tensor(out=gv, in0=gv, in1=lng_b[:TP, :], op=ALU.mult), reads=B_gvb[kk] + [B_const], writes=B_gvb[kk])
            if not sample:
                S.op("pool", lambda e, gv=gv, ts=ts: e.tensor_tensor(out=vnb[:TP, ts, :], in0=gv, in1=lnb_b[:TP, :], op=ALU.add), reads=B_gvb[kk] + [B_const], writes=[B_vnb[ts]])
            else:
                S.op("pool", lambda e, gv=gv: e.tensor_tensor(out=gv, in0=gv, in1=lnb_b[:TP, :], op=ALU.add), reads=B_gvb[kk] + [B_const], writes=B_gvb[kk])
                AC(lambda e, gv=gv, ts=ts: e.activation(out=vnb[:TP, ts, :], in_=gv, func=AF.Copy), r=B_gvb[kk], w=[B_vnb[ts]])
                pool_toks.append(S.dma("pool", S.chan("ovs"), lambda e, gv=gv: e.dma_start(out=o_vs[:, :], in_=gv), reads=B_gvb[kk]))
        for ts in range(min(2, NTS)):
            vb_mm(ts)
            vb_ln(ts)
        for ct in range(4):
            sigA(ct, *gatesA(ct))
        for ts in range(2, NTS):
            vb_mm(ts)
            vb_ln(ts)
        wrel(g0 + 3)
        sgu_banks = []
        for h_ in range(4):
            k2 = h_ % 2
            bank = nbank()
            sgu_banks.append(bank)
            if not sample:
                nmm = NTS * 3
                idx = 0
                for ts in range(NTS):
                    oc = psum[bank][:, ts * 128:(ts + 1) * 128]
                    trip = [(vnb[:, ts, h_ * 128:(h_ + 1) * 128], wsgu[:, h_, :]),
                            (ones_bf[0:1, :], bsh[0:1, h_ * 128:(h_ + 1) * 128]),
                            (ones_bf[0:1, :], bsl[0:1, h_ * 128:(h_ + 1) * 128])]
                    for l_, r_ in trip:
                        S.op("pe", lambda e, oc=oc, l_=l_, r_=r_, idx=idx: e.matmul(oc, l_, r_, start=(idx == 0), stop=(idx == nmm - 1), skip_group_check=True),
                             reads=[B_vnb[ts], B_const], writes=[B_ps[bank]], inc=(idx == nmm - 1))
                        idx += 1
            else:
                oc = psum[bank][:, 0:64]
                trip = [(vnb[0:64, 0, h_ * 128:(h_ + 1) * 128], wsgu_s[0:64, h_, :]),
                        (ones_bf[0:1, :], bsh[0:1, 512 + h_ * 64:512 + (h_ + 1) * 64]),
                        (ones_bf[0:1, :], bsl[0:1, 512 + h_ * 64:512 + (h_ + 1) * 64])]
                for idx, (l_, r_) in enumerate(trip):
                    S.op("pe", lambda e, oc=oc, l_=l_, r_=r_, idx=idx: e.matmul(oc, l_, r_, start=(idx == 0), stop=(idx == 2), skip_group_check=True),
                         reads=[B_vnb[0], B_const], writes=[B_ps[bank]], inc=(idx == 2))
        for ct in range(4):
            expA(ct)
        for ct in range(4):
            sqrtA(ct)
        for ct in range(2):
            postA(ct)
            postA_act(ct)
        for h_ in range(4):
            k2 = h_ % 2
            bank = sgu_banks[h_]
            ee = e4[:, k2, 0:T]
            ig = ig4[:, k2, 0:T]
            DV(lambda e, ee=ee, bank=bank, h_=h_: e.tensor_tensor(out=ee, in0=psum[bank][:, 0:T], in1=sh3[:, 4 + h_, 0:T], op=ALU.mult),
               r=[B_ps[bank]] + B_gub[h_], w=[B_e4[k2]])
            sqv = ig4[:, k2, :].bitcast(BF16)[:, 0:T]
            AC(lambda e, ee=ee, sqv=sqv: e.activation(out=sqv, in_=ee, func=AF.Square), r=[B_e4[k2]], w=[B_ig4[k2]])
            stats_mm(sqv, B_ig4[k2], 8)
            AC(lambda e, ee=ee, h_=h_: e.activation(out=yabT[:, 4 + h_, 0:T], in_=ee, func=AF.Identity, scale=cvec[:, C_GOB + h_:C_GOB + h_ + 1]),
               r=[B_e4[k2], B_const], w=[B_yab[4 + h_]])
        for ct in range(2, 4):
            postA(ct)
            postA_act(ct)
        if sample:
            outdma(o_hs[:, :], hsout[:, :, :].rearrange("p a b -> p (a b)"), [B_hsout])
        if i == 3:
            outdma(o_hp[:, :], hlast[:, :], B_hlast)
            outdma(o_cp[:, :], xhist[:, :, :].rearrange("p a b -> p (a b)"), B_xhist)

        src8 = psum[sbank][:TP, 0:16].rearrange("p (h t two) -> p h t two", h=2, two=2)[:, :, 0:NTS, 0]
        ms8v = ms8[:TP, :].rearrange("p (h t) -> p h t", h=2)[:, :, 0:NTS]
        rstd8v = rstd8[:TP, :].rearrange("p (h t) -> p h t", h=2)[:, :, 0:NTS]
        mh8v = mhalf[:TP, :].rearrange("p (h t) -> p h t", h=2)[:, :, 0:NTS]
        DV(lambda e: e.tensor_scalar(out=ms8v, in0=src8, scalar1=1.0 / 512, scalar2=EPS, op0=ALU.mult, op1=ALU.add), r=[B_ps[sbank]], w=[B_ms8])
        S.op("pool", lambda e: e.tensor_tensor(out=rstd8v, in0=ms8v, in1=mh8v, op=ALU.pow), reads=[B_ms8, B_const], writes=[B_rstd8])
        reserved.discard(sbank)

        for half_ in range(2):
            k = wget(g0 + 4 + half_)
            for ts in range(NTS):
                for grp in range(2):
                    bank = nbank()
                    mm_group(bank, psum[bank][:TP, :], [(yabT[:, ct, ts * 128:ts * 128 + TP], ring[:, k, ct, :]) for ct in range(grp * 4, grp * 4 + 4)],
                             reads=[B_ring[k]] + B_yab)
                    hsl = hbuf[:TP, hb, ts, half_ * 512:(half_ + 1) * 512]
                    DV(lambda e, bank=bank, hsl=hsl, ts=ts, grp=grp: e.scalar_tensor_tensor(out=hsl, in0=psum[bank][:TP, :], scalar=rstd8[:TP, grp * 4 + ts:grp * 4 + ts + 1],
                                                                                       in1=hsl, op0=ALU.mult, op1=ALU.add),
                       r=[B_ps[bank], B_rstd8, B_h[hb][ts]], w=[B_h[hb][ts]])
            wrel(g0 + 4 + half_)

        norm(hb, C_GFFN, ti)
        if not sample:
            fwv = cvec[:, C_FW:C_FW + 144].rearrange("p (f j) -> p f j", j=3)
            DV(lambda e: e.tensor_tensor(out=corr[:, :, 0], in0=fhist[:, :, 0], in1=fwv[:, :, 0], op=ALU.mult), r=B_fhist + [B_const], w=[B_corr])
            DV(lambda e: e.tensor_tensor(out=corr[:, :, 1], in0=fhist[:, :, 1], in1=fwv[:, :, 1], op=ALU.mult), r=B_fhist + [B_const], w=[B_corr])
            DV(lambda e: e.tensor_tensor(out=corr[:, :, 0], in0=corr[:, :, 0], in1=corr[:, :, 1], op=ALU.add), r=[B_corr], w=[B_corr])
            DV(lambda e: e.tensor_tensor(out=corr[:, :, 1], in0=fhist[:, :, 1], in1=fwv[:, :, 0], op=ALU.mult), r=B_fhist + [B_const], w=[B_corr])
        for hg in range(6):
            kg = wget(g0 + 6 + 2 * hg)
            kl = wget(g0 + 7 + 2 * hg)
            for jj in range(4):
                j = hg * 4 + jj
                s2 = j % 2
                for part, kw in ((0, kg), (1, kl)):
                    ft = part * 24 + j
                    bank = nbank()
                    mm_group(bank, psum[bank][:, 0:T], [(ring[:, kw, kt, jj * 128:(jj + 1) * 128], nT[:, kt, 0:T]) for kt in range(8)],
                             reads=[B_ring[kw]] + B_nT)
                    cc = e4[:, 2 * s2 + part, :]
                    Bc = B_fc[s2][part]
                    w0 = cvec[:, C_FW + ft * 3:C_FW + ft * 3 + 1]
                    w1 = cvec[:, C_FW + ft * 3 + 1:C_FW + ft * 3 + 2]
                    w2 = cvec[:, C_FW + ft * 3 + 2:C_FW + ft * 3 + 3]
                    bb = cvec[:, C_FB + ft:C_FB + ft + 1]
                    pb = psum[bank]
                    if not sample:
                        AC(lambda e, cc=cc, pb=pb, w2=w2, bb=bb: e.activation(out=cc[:, 0:L], in_=pb[:, 0:L], func=AF.Identity, scale=w2, bias=bb),
                           r=[B_ps[bank], B_const], w=[Bc])
                        DV(lambda e, cc=cc, pb=pb, w1=w1: e.scalar_tensor_tensor(out=cc[:, 1:L], in0=pb[:, 0:L - 1], scalar=w1, in1=cc[:, 1:L], op0=ALU.mult, op1=ALU.add),
                           r=[B_ps[bank], B_const, Bc], w=[Bc])
                        DV(lambda e, cc=cc, pb=pb, w0=w0: e.scalar_tensor_tensor(out=cc[:, 2:L], in0=pb[:, 0:L - 2], scalar=w0, in1=cc[:, 2:L], op0=ALU.mult, op1=ALU.add),
                           r=[B_ps[bank], B_const, Bc], w=[Bc])
                        DV(lambda e, cc=cc, ft=ft: e.tensor_tensor(out=cc[:, 0:2], in0=cc[:, 0:2], in1=corr[:, ft, :], op=ALU.add),
                           r=[B_corr, Bc], w=[Bc])
                        AC(lambda e, pb=pb, ft=ft: e.activation(out=fhist[:, ft, :], in_=pb[:, L - 2:L], func=AF.Copy), r=[B_ps[bank]], w=[B_fhist[ft]])
                    else:
                        fx = fext[:, part, :, :]
                        Bx = B_fext[part]
                        ccv = cc[:, 0:T].rearrange("p (b l) -> p b l", l=L)
                        AC(lambda e, fx=fx, ft=ft: e.activation(out=fx[:, :, 0:2], in_=st_fT[:, ft, :, :], func=AF.Copy), r=[B_stf[ft], B_const], w=[Bx])
                        AC(lambda e, fx=fx, pb=pb: e.activation(out=fx[:, :, 2:6], in_=pb[:, 0:T].rearrange("p (b l) -> p b l", l=L), func=AF.Copy),
                           r=[B_ps[bank]], w=[Bx])
                        DV(lambda e, fx=fx, ccv=ccv, w0=w0, bb=bb: e.tensor_scalar(out=ccv, in0=fx[:, :, 0:4], scalar1=w0, scalar2=bb, op0=ALU.mult, op1=ALU.add),
                           r=[Bx, B_const], w=[Bc])
                        DV(lambda e, fx=fx, ccv=ccv, w1=w1: e.scalar_tensor_tensor(out=ccv, in0=fx[:, :, 1:5], scalar=w1, in1=ccv, op0=ALU.mult, op1=ALU.add),
                           r=[Bx, B_const, Bc], w=[Bc])
                        DV(lambda e, fx=fx, ccv=ccv, w2=w2: e.scalar_tensor_tensor(out=ccv, in0=fx[:, :, 2:6], scalar=w2, in1=ccv, op0=ALU.mult, op1=ALU.add),
                           r=[Bx, B_const, Bc], w=[Bc])
                        AC(lambda e, fx=fx, ft=ft: e.activation(out=st_fT[:, ft, :, :], in_=fx[:, :, 4:6], func=AF.Copy), r=[Bx], w=[B_stf[ft]])
                AC(lambda e, s2=s2: e.activation(out=ig4[:, s2, 0:T], in_=e4[:, 2 * s2, 0:T], func=AF.Gelu_apprx_tanh), r=[B_fc[s2][0]], w=[B_fG[s2]])
                S.op("pool", lambda e, s2=s2, j=j: e.tensor_tensor(out=actT[:, j, 0:T], in0=ig4[:, s2, 0:T], in1=e4[:, 2 * s2 + 1, 0:T], op=ALU.mult),
                     reads=[B_fG[s2], B_fc[s2][1]], writes=B_act[j])
            wrel(g0 + 6 + 2 * hg)
            wrel(g0 + 7 + 2 * hg)
        if i + 1 < ntiles:
            emit_xload(i + 1)
        if i == 3:
            outdma(o_fp[:, :], fhist[:, :, :].rearrange("p a b -> p (a b)"), B_fhist)
        if sample:
            outdma(o_fs[:, :], st_fT[:, :, :, :].rearrange("p a b c -> p (a b c)"), B_stf)
        for half_ in range(2):
            banks = [nbank() for _ in range(NTS)]
            for jg in range(3):
                gslot = g0 + 18 + half_ * 3 + jg
                k = wget(gslot)
                for ts in range(NTS):
                    for j8 in range(8):
                        j = jg * 8 + j8
                        S.op("pe", lambda e, ts=ts, j=j, j8=j8, jg=jg, k=k: e.matmul(psum[banks[ts]][:TP, :], actT[:, j, ts * 128:ts * 128 + TP], ring[:, k, j8, :],
                                                                                 start=(jg == 0 and j8 == 0), stop=(jg == 2 and j8 == 7), skip_group_check=True),
                             reads=[B_ring[k]] + B_act[j], writes=[B_ps[banks[ts]]], inc=(j8 == 7))
                wrel(gslot)
            for ts in range(NTS):
                hsl = hbuf[:TP, hb, ts, half_ * 512:(half_ + 1) * 512]
                DV(lambda e, ts=ts, hsl=hsl: e.tensor_tensor(out=hsl, in0=psum[banks[ts]][:TP, :], in1=hsl, op=ALU.add),
                   r=[B_ps[banks[ts]], B_h[hb][ts]], w=[B_h[hb][ts]])

        norm(hb, C_GPLE, ti)
        if i + 1 < ntiles:
            norm((i + 1) % 2, C_GMIX, tile_info(i + 1), phase="pre")
        kp = wget(g0 + 24)
        for half_ in range(2):
            kgt = wget(g0 + 25 + half_)
            for ts in range(NTS):
                s2 = ts % 2
                bg = nbank()
                mm_group(bg, psum[bg][:TP, :], [(nT[:, kt, ts * 128:ts * 128 + TP], ring[:, kgt, kt, :]) for kt in range(8)], reads=[B_ring[kgt]] + B_nT)
                bp = nbank()
                mm_group(bp, psum[bp][:TP, :], [(pTb[:, hb, kt, ts * 128:ts * 128 + TP], ring[:, kp, kt * 2 + half_, :]) for kt in range(2)],
                         reads=[B_ring[kp], B_pTb[hb]])
                AC(lambda e, s2=s2, bg=bg: e.activation(out=ig4[:TP, s2, :], in_=psum[bg][:TP, :], func=AF.Sigmoid), r=[B_ps[bg]], w=[B_gt[s2]])
                DV(lambda e, s2=s2, bp=bp: e.tensor_tensor(out=e4[:TP, 2 * s2, :], in0=psum[bp][:TP, :], in1=ig4[:TP, s2, :], op=ALU.mult),
                   r=[B_ps[bp], B_gt[s2]], w=[B_pt[s2]])
                hsl = hbuf[:TP, hb, ts, half_ * 512:(half_ + 1) * 512]
                DV(lambda e, s2=s2, hsl=hsl: e.tensor_tensor(out=hsl, in0=hsl, in1=e4[:TP, 2 * s2, :], op=ALU.add), r=[B_pt[s2], B_h[hb][ts]], w=[B_h[hb][ts]])
            wrel(g0 + 25 + half_)
        wrel(g0 + 24)

        if i + 1 < ntiles:
            norm((i + 1) % 2, C_GMIX, tile_info(i + 1), phase="post")

        for ts in range(NTS):
            AC(lambda e, ts=ts: e.activation(out=junk[:TP, :], in_=hbuf[:TP, hb, ts, :], func=AF.Square, accum_out=ss[:TP, ts:ts + 1]),
               r=[B_h[hb][ts]], w=[B_junk, B_ss])
        rstd_chain(ss[:TP, 0:NTS], ms[:TP, 0:NTS], rstd[:TP, 0:NTS], 1.0 / D, [B_ss], B_ms, B_rstd)
        for ts in range(NTS):
            hv = hbuf[:TP, hb, ts, :]
            DV(lambda e, ts=ts, hv=hv: e.scalar_tensor_tensor(out=hv, in0=hv, scalar=rstd[:TP, ts:ts + 1], in1=gfin_b[:TP, :], op0=ALU.mult, op1=ALU.mult),
               r=[B_h[hb][ts], B_rstd, B_const], w=[B_h[hb][ts]])
        if not sample:
            dst = y_p[i * 512:(i + 1) * 512, :].rearrange("(t p) f -> p t f", p=128)
            out_toks.append(S.dma("sp", ch_y[hb], lambda e, dst=dst, hb=hb: e.dma_start(out=dst, in_=hbuf[:, hb, :, :]), reads=B_h[hb]))
        else:
            out_toks.append(S.dma("sp", ch_y[hb], lambda e, hb=hb: e.dma_start(out=y_s[:, :], in_=hbuf[0:64, hb, 0, :]), reads=[B_h[hb][0]]))

    for tok in out_toks:
        S.wait("sp", tok)
    for tok in pool_toks:
        S.wait("pool", tok)

    S.replay()
    stack.close()
    return nc


def _prep_shared(inp):
    f = np.float32
    w_in = np.asarray(inp["w_in"][0], f); w_out = np.asarray(inp["w_out"][0], f)
    w_up = np.asarray(inp["w_up"][0], f); w_down = np.asarray(inp["w_down"][0], f)
    w_pg = np.asarray(inp["w_ple_gate"][0], f); w_ple = np.asarray(inp["w_ple"][0], f)
    wall = np.zeros((NSLOT, 128, 4096), f)
    wall[0:4] = w_in.reshape(8, 128, 4, 512).transpose(2, 1, 0, 3).reshape(4, 128, 4096)
    wall[4:6] = w_out.reshape(8, 128, 2, 512).transpose(2, 1, 0, 3).reshape(2, 128, 4096)
    wall[6:18] = w_up.reshape(8, 128, 2, 6, 512).transpose(3, 2, 1, 0, 4).reshape(12, 128, 4096)
    wall[18:24] = w_down.reshape(3, 8, 128, 2, 512).transpose(3, 0, 2, 1, 4).reshape(6, 128, 4096)
    wall[24, :, 0:2048] = w_ple.reshape(2, 128, 2, 512).transpose(1, 0, 2, 3).reshape(128, 2048)
    wall[25:27] = w_pg.reshape(8, 128, 2, 512).transpose(2, 1, 0, 3).reshape(2, 128, 4096)

    def col(v, n):
        return np.asarray(v, f).reshape(n, 128).T

    cv = np.zeros((128, NCV), f)
    cv[:, C_GMIX:C_GMIX + 8] = col(inp["g_mix_norm"][0], 8)
    cv[:, C_GFFN:C_GFFN + 8] = col(inp["g_ffn_norm"][0], 8)
    cv[:, C_GPLE:C_GPLE + 8] = col(inp["g_ple_norm"][0], 8)
    caw = np.asarray(inp["conv_a_w"][0], f)
    cv[:, C_CAW:C_CAW + 16] = caw.reshape(4, 4, 128).transpose(2, 1, 0).reshape(128, 16)
    cv[:, C_CAB:C_CAB + 4] = col(inp["conv_a_b"][0], 4)
    cv[:, C_BA:C_BA + 4] = col(inp["lru_ba"][0], 4)
    cv[:, C_BX:C_BX + 4] = col(inp["lru_bx"][0], 4)
    cv[:, C_AP:C_AP + 4] = col(inp["lru_a_param"][0], 4)
    cv[:, C_GOA:C_GOA + 4] = col(inp["g_out_a"][0], 4)
    cv[:, C_GOB:C_GOB + 4] = col(inp["g_out_b"][0], 4)
    fw = np.asarray(inp["ffn_conv_w"][0], f)
    cv[:, C_FW:C_FW + 144] = fw.reshape(3, 48, 128).transpose(2, 1, 0).reshape(128, 144)
    cv[:, C_FB:C_FB + 48] = col(inp["ffn_conv_b"][0], 48)

    sgu_w = np.asarray(inp["sgu_w"][0], f)
    sgu_b = np.asarray(inp["sgu_b"][0], f)
    sguT = sgu_w.transpose(2, 0, 1).reshape(128, 512)
    maskT = (np.arange(128)[:, None] <= np.arange(128)[None, :]).astype(f)
    w4 = sgu_w[:, 0:4, 0:4]
    Rs = np.broadcast_to(w4.transpose(2, 0, 1)[None, :, :, None, :], (16, 4, 4, 16, 4)).reshape(64, 256).astype(f)
    bb = np.arange(16)
    mask_s = ((bb[:, None, None, None] == bb[None, None, :, None]) &
              (np.arange(4)[None, :, None, None] <= np.arange(4)[None, None, None, :])).astype(f).reshape(64, 64)
    bsrow = sgu_b.reshape(1, 512)
    bsrow_s = np.broadcast_to(sgu_b[:, None, 0:4], (4, 16, 4)).reshape(1, 256).astype(f)

    def bd(w):
        w = np.asarray(w, f)
        o = np.zeros((128, 4, 128), f)
        for ct in range(4):
            o[0:64, ct, 0:64] = w[2 * ct]
            o[64:128, ct, 64:128] = w[2 * ct + 1]
        return o.reshape(128, 512)

    return dict(
        wall=wall, cvec=cv,
        gfin_b=np.ascontiguousarray(np.broadcast_to(np.asarray(inp["g_final"], f)[None, :], (128, D))),
        lng_b=np.ascontiguousarray(np.broadcast_to(np.asarray(inp["ln_v_g"][0], f)[None, :], (128, 512))),
        lnb_b=np.ascontiguousarray(np.broadcast_to(np.asarray(inp["ln_v_b"][0], f)[None, :], (128, 512))),
        ident=np.eye(128, dtype=f), maskT=maskT, sguT=np.ascontiguousarray(sguT),
        wabd=bd(inp["lru_wa"][0]), wxbd=bd(inp["lru_wx"][0]),
        bsrow=np.ascontiguousarray(bsrow), Rs=np.ascontiguousarray(Rs), mask_s=mask_s, bsrow_s=np.ascontiguousarray(bsrow_s),
    )


_NC_CACHE = {}


def kernel(**inp):
    f = np.float32
    shared = _prep_shared(inp)
    x_prompt = np.asarray(inp["x_prompt"], f); x_sample = np.asarray(inp["x_sample"], f)
    p_prompt = np.asarray(inp["p_prompt"], f); p_sample = np.asarray(inp["p_sample"], f)
    st_h = np.asarray(inp["state_rglru_h"], f); st_c = np.asarray(inp["state_rglru_conv"], f)
    st_f = np.asarray(inp["state_ffn_conv"], f)
    in_maps = []
    for c in range(NCORES):
        sl = slice(16 * c, 16 * c + 16)
        m = dict(shared)
        m["xp"] = np.ascontiguousarray(x_prompt[c])
        m["xs"] = np.ascontiguousarray(x_sample[sl].reshape(64, D))
        m["ppT"] = np.ascontiguousarray(p_prompt[0, c].T)
        m["psT"] = np.ascontiguousarray(p_sample[0, sl].reshape(64, 256).T)
        m["st_hT"] = np.ascontiguousarray(st_h[0, sl].reshape(16, 4, 128).transpose(2, 1, 0).reshape(128, 64))
        m["st_cT"] = np.ascontiguousarray(st_c[0, sl].reshape(16, 3, 4, 128).transpose(3, 2, 0, 1).reshape(128, 192))
        m["st_fT"] = np.ascontiguousarray(st_f[0, sl].reshape(16, 2, 48, 128).transpose(3, 2, 0, 1).reshape(128, 1536))
        in_maps.append(m)
    if "nc" not in _NC_CACHE:
        _NC_CACHE["nc"] = build_nc()
    res = run_bass_kernel_spmd(_NC_CACHE["nc"], in_maps, core_ids=list(range(NCORES)))
    R = res.results
    y_prompt = np.stack([np.asarray(R[c]["y_p"], f) for c in range(NCORES)], 0)
    y_sample = np.concatenate([np.asarray(R[c]["y_s"], f).reshape(16, 4, D) for c in range(NCORES)], 0)
    h_p = np.stack([np.asarray(R[c]["o_hp"], f).T.reshape(512) for c in range(NCORES)], 0)[None]
    h_s = np.concatenate([np.asarray(R[c]["o_hs"], f).reshape(128, 4, 16).transpose(2, 1, 0).reshape(16, 512) for c in range(NCORES)], 0)[None]
    c_p = np.stack([np.asarray(R[c]["o_cp"], f).reshape(128, 4, 3).transpose(2, 1, 0).reshape(3, 512) for c in range(NCORES)], 0)[None]
    c_s = np.concatenate([np.asarray(R[c]["o_cs"], f).reshape(128, 4, 16, 3).transpose(2, 3, 1, 0).reshape(16, 3, 512) for c in range(NCORES)], 0)[None]
    v_s = np.concatenate([np.asarray(R[c]["o_vs"], f).reshape(16, 4, 512) for c in range(NCORES)], 0)[None]
    f_p = np.stack([np.asarray(R[c]["o_fp"], f).reshape(128, 48, 2).transpose(2, 1, 0).reshape(2, 6144) for c in range(NCORES)], 0)[None]
    f_s = np.concatenate([np.asarray(R[c]["o_fs"], f).reshape(128, 48, 16, 2).transpose(2, 3, 1, 0).reshape(16, 2, 6144) for c in range(NCORES)], 0)[None]
    return (y_prompt, y_sample, np.ascontiguousarray(h_p), np.ascontiguousarray(h_s), np.ascontiguousarray(c_p),
            np.ascontiguousarray(c_s), np.ascontiguousarray(v_s), np.ascontiguousarray(f_p), np.ascontiguousarray(f_s))
```

```python
import numpy as np
import concourse.bass as bass
import concourse.mybir as mybir
from concourse.bass_utils import run_bass_kernel_spmd

F32 = mybir.dt.float32
BF16 = mybir.dt.bfloat16
AF = mybir.ActivationFunctionType
ALU = mybir.AluOpType

NCORES = 8
D = 1024
WA = 512
DFF = 3072
EPS = 1e-6
NSLOT = 27
RING = 5
SAME_ENGINE_SYNC = True

C_GMIX, C_GFFN, C_GPLE, C_CAW, C_CAB, C_BA, C_BX, C_AP, C_GOA, C_GOB, C_FW, C_FB = 0, 8, 16, 24, 40, 44, 48, 52, 56, 60, 64, 208
NCV = 256


import types


def _freeze(fn):
    if fn is None or fn.__closure__ is None:
        return fn
    cells = []
    for c in fn.__closure__:
        try:
            cells.append(types.CellType(c.cell_contents))
        except ValueError:
            cells.append(c)
    return types.FunctionType(fn.__code__, fn.__globals__, fn.__name__, fn.__defaults__, tuple(cells))


class Sem:
    def __init__(self, nc, stack, name):
        self.h = stack.enter_context(nc.semaphore(name))
        self.name = name


class Buf:
    __slots__ = ("name", "w", "r")

    def __init__(self, name):
        self.name = name
        self.w = None
        self.r = {}


class Chan:
    def __init__(self, sem):
        self.sem = sem
        self.cnt = 0


class Eng:
    def __init__(self, name, sem, same_sync):
        self.name = name
        self.sem = sem
        self.cnt = 0
        self.waited = {}
        self.ops = []
        self.same_sync = same_sync


class Sched:
    def __init__(self, nc, stack):
        self.nc = nc
        self.stack = stack
        self.nsem = 0
        self.eng = {}
        for n in ("pe", "act", "dve", "pool", "sp"):
            self.eng[n] = Eng(n, self.newsem("e_" + n), SAME_ENGINE_SYNC and n not in ("pe", "sp"))

    def newsem(self, name):
        self.nsem += 1
        return Sem(self.nc, self.stack, name)

    def chan(self, name):
        return Chan(self.newsem("c_" + name))

    def _deps(self, e, reads, writes):
        deps = {}

        def add(s, v):
            if deps.get(s, 0) < v:
                deps[s] = v

        for b in reads:
            if b.w is not None:
                add(*b.w)
        for b in writes:
            if b.w is not None:
                add(*b.w)
            for s, v in b.r.items():
                add(s, v)
        waits = []
        for s, v in deps.items():
            if s is e.sem and not e.same_sync:
                continue
            if e.waited.get(s, 0) >= v:
                continue
            e.waited[s] = v
            waits.append((s, v))
        return waits

    def _commit(self, tok, reads, writes):
        for b in writes:
            b.w = tok
            b.r = {}
        for b in reads:
            if b.r.get(tok[0], 0) < tok[1]:
                b.r[tok[0]] = tok[1]

    def op(self, en, fn, reads=(), writes=(), inc=True):
        e = self.eng[en]
        waits = self._deps(e, reads, writes)
        tok = (e.sem, e.cnt + 1)
        if inc:
            e.cnt += 1
        e.ops.append((waits, _freeze(fn), (e.sem, 1) if inc else None))
        self._commit(tok, reads, writes)
        return tok

    def dma(self, en, ch, fn, reads=(), writes=()):
        e = self.eng[en]
        waits = self._deps(e, reads, writes)
        if en == "pool" and ch.cnt > 0 and e.waited.get(ch.sem, 0) < ch.cnt:
            e.waited[ch.sem] = ch.cnt
            waits.append((ch.sem, ch.cnt))
        ch.cnt += 16
        tok = (ch.sem, ch.cnt)
        e.ops.append((waits, _freeze(fn), (ch.sem, 16)))
        self._commit(tok, reads, writes)
        return tok

    def wait(self, en, tok):
        e = self.eng[en]
        if e.waited.get(tok[0], 0) >= tok[1]:
            return
        e.waited[tok[0]] = tok[1]
        e.ops.append(([tok], None, None))

    def replay(self):
        nc = self.nc
        with nc.Block() as block:
            def run(eobj, e):
                for waits, fn, inc in e.ops:
                    for s, v in waits:
                        eobj.wait_ge(s.h, v)
                    if fn is None:
                        continue
                    ins = fn(eobj)
                    if inc is not None:
                        ins.then_inc(inc[0].h, inc[1])

            @block.tensor
            def _(x):
                run(x, self.eng["pe"])

            @block.scalar
            def _(x):
                run(x, self.eng["act"])

            @block.vector
            def _(x):
                run(x, self.eng["dve"])

            @block.gpsimd
            def _(x):
                run(x, self.eng["pool"])

            @block.sync
            def _(x):
                run(x, self.eng["sp"])


def build_nc(ntiles=5):
    from contextlib import ExitStack
    nc = bass.Bass("TRN2", target_bir_lowering=False)
    stack = ExitStack()

    def din(name, shape, dt=F32):
        return nc.dram_tensor(name, list(shape), dt, kind="ExternalInput").ap()

    def dout(name, shape, dt=F32):
        return nc.dram_tensor(name, list(shape), dt, kind="ExternalOutput").ap()

    xp = din("xp", [2048, D]); xs = din("xs", [64, D])
    ppT = din("ppT", [256, 2048]); psT = din("psT", [256, 64])
    wall = din("wall", [NSLOT, 128, 4096])
    cvec_d = din("cvec", [128, NCV]); gfin_d = din("gfin_b", [128, D])
    lng_d = din("lng_b", [128, 512]); lnb_d = din("lnb_b", [128, 512])
    ident_d = din("ident", [128, 128]); maskT_d = din("maskT", [128, 128])
    sguT_d = din("sguT", [128, 512]); wabd_d = din("wabd", [128, 512]); wxbd_d = din("wxbd", [128, 512])
    bsrow_d = din("bsrow", [1, 512]); Rs_d = din("Rs", [64, 256]); masks_d = din("mask_s", [64, 64])
    bsrow_s_d = din("bsrow_s", [1, 256])
    sth_d = din("st_hT", [128, 64]); stc_d = din("st_cT", [128, 192]); stf_d = din("st_fT", [128, 1536])

    y_p = dout("y_p", [2048, D]); y_s = dout("y_s", [64, D])
    o_hp = dout("o_hp", [128, 4]); o_hs = dout("o_hs", [128, 64])
    o_cp = dout("o_cp", [128, 12]); o_cs = dout("o_cs", [128, 192])
    o_vs = dout("o_vs", [64, 512]); o_fp = dout("o_fp", [128, 96]); o_fs = dout("o_fs", [128, 1536])
    wbf = nc.dram_tensor("wbf", [NSLOT, 128, 4096], BF16, kind="Internal").ap()

    def sb(name, shape, dt=F32):
        return stack.enter_context(nc.sbuf_tensor(name, list(shape), dt))

    hbuf = sb("hbuf", [128, 2, 4, D])
    nscr = sb("nscr", [128, 2, D])
    junk = sb("junk", [128, D], BF16)
    nT = sb("nT", [128, 8, 512], BF16)
    xa_ext = sb("xa_ext", [128, 4, 520])
    xc = sb("xc", [128, 4, 512]); xcb = sb("xcb", [128, 4, 512], BF16)
    e4 = sb("e4", [128, 4, 512]); ig4 = sb("ig4", [128, 4, 512]); rr4 = sb("rr4", [128, 4, 512])
    shared = sb("shared", [128, 6144])
    vnb = sb("vnb", [128, 4, 512], BF16)
    hs = sb("hs", [128, 2, 512])
    yabT = sb("yabT", [128, 8, 512], BF16)
    corr = sb("corr", [128, 48, 2])
    pTb = sb("pTb", [128, 2, 2, 512], BF16)
    cvec = sb("cvec_s", [128, NCV]); gfin_b = sb("gfin_s", [128, D])
    lng_b = sb("lng_s", [128, 512]); lnb_b = sb("lnb_s", [128, 512])
    ident = sb("ident_s", [128, 128])
    wsgu = sb("wsgu", [128, 4, 128], BF16); wabd = sb("wabd_s", [128, 4, 128], BF16); wxbd = sb("wxbd_s", [128, 4, 128], BF16)
    wsgu_s = sb("wsgu_ss", [128, 4, 64], BF16)
    bsh = sb("bsh", [1, 768], BF16); bsl = sb("bsl", [1, 768], BF16)
    bsf = hbuf[0:1, 1, 0, 0:768]
    ones_bf = sb("ones_bf", [1, 128], BF16); ones_f = sb("ones_f", [128, 2]); ones_b2 = sb("ones_b2", [128, 2], BF16); mhalf = sb("mhalf", [128, 8])
    st_hT = sb("st_hT_s", [128, 4, 16]); st_cT = sb("st_cT_s", [128, 4, 16, 3]); st_fT = sb("st_fT_s", [128, 48, 16, 2])
    fext = sb("fext", [128, 2, 16, 6])
    hsout = sb("hsout", [128, 4, 16]); tmp16 = sb("tmp16", [128, 16]); cs_out = sb("cs_out", [128, 4, 16, 3])
    xhist = sb("xhist", [128, 4, 3]); hlast = sb("hlast", [128, 4]); fhist = sb("fhist", [128, 48, 2])
    cA = sb("cA", [128, 4]); cA2 = sb("cA2", [128, 4]); spt = sb("spt", [128, 4])
    ss = sb("ss", [128, 4]); ms = sb("ms", [128, 4]); rstd = sb("rstd", [128, 4])
    ssn = sb("ssn", [128, 4]); msn = sb("msn", [128, 4]); rstdn = sb("rstdn", [128, 4])
    ms8 = sb("ms8", [128, 8]); rstd8 = sb("rstd8", [128, 8])
    bst = sb("bst", [128, 2, 6]); mv = sb("mv", [128, 2, 2]); rsv = sb("rsv", [128, 2])
    ring = sb("ring", [128, RING, 8, 512], BF16)
    psum = [stack.enter_context(nc.psum_tensor("ps%d" % i, [128, 512], F32)) for i in range(8)]

    actT = shared[:, :].bitcast(BF16).rearrange("p (j t) -> p j t", t=512)
    sh3 = shared[:, :].rearrange("p (a t) -> p a t", t=512)

    S = Sched(nc, stack)
    U = [Buf("U%d" % i) for i in range(24)]
    B_gga = [[U[2 * c], U[2 * c + 1]] for c in range(4)]
    B_gub = [[U[8 + 2 * c], U[9 + 2 * c]] for c in range(4)]
    B_gvb = [[U[16 + 2 * k], U[17 + 2 * k]] for k in range(2)]
    B_rr4 = [Buf("rr%d" % c) for c in range(4)]; B_e4 = [Buf("e4_%d" % c) for c in range(4)]; B_ig4 = [Buf("ig4_%d" % c) for c in range(4)]
    B_act = [[U[j]] for j in range(24)]
    B_h = [[Buf("h%d_%d" % (a, t)) for t in range(4)] for a in range(2)]
    B_nscr = [Buf("nscr0"), Buf("nscr1")]; B_junk = Buf("junk"); B_corr = Buf("corr")
    B_nT = [Buf("nT%d" % k) for k in range(8)]
    B_xa = [Buf("xa%d" % c) for c in range(4)]
    B_xc = [Buf("xc%d" % k) for k in range(4)]; B_xcb = [Buf("xcb%d" % k) for k in range(4)]
    B_vnb = [Buf("vnb%d" % t) for t in range(4)]
    B_hs = [Buf("hs%d" % c) for c in range(2)]
    B_yab = [Buf("yab%d" % c) for c in range(8)]
    B_fc = [[B_e4[2 * s_ + p_] for p_ in range(2)] for s_ in range(2)]
    B_fG = [B_ig4[0], B_ig4[1]]
    B_gt = B_fG; B_pt = [B_fc[0][0], B_fc[1][0]]
    B_pTb = [Buf("pTb%d" % k) for k in range(2)]
    B_ps = [Buf("ps%d" % k) for k in range(8)]
    B_ring = [Buf("ring%d" % k) for k in range(RING)]
    B_wbf = [Buf("wbf%d" % k) for k in range(NSLOT)]
    B_xhist = [Buf("xhist%d" % c) for c in range(4)]
    B_hlast = [Buf("hlast%d" % c) for c in range(4)]
    B_fhist = [Buf("fhist%d" % f) for f in range(48)]
    B_stf = [Buf("stf%d" % f) for f in range(48)]
    B_fext = [Buf("fext%d" % k) for k in range(2)]
    B_hsout = Buf("hsout"); B_tmp16 = Buf("tmp16"); B_csout = Buf("csout")
    B_ssn = [Buf("ssn%d" % t) for t in range(4)]; B_msn = [Buf("msn%d" % t) for t in range(4)]; B_rstdn = [Buf("rstdn%d" % t) for t in range(4)]
    B_ss = Buf("ss"); B_ms = Buf("ms"); B_rstd = Buf("rstd"); B_ms8 = Buf("ms8"); B_rstd8 = Buf("rstd8")
    B_bst = [Buf("bst%d" % k) for k in range(2)]; B_mv = [Buf("mv%d" % k) for k in range(2)]; B_rsv = [Buf("rsv%d" % k) for k in range(2)]
    B_const = Buf("const")

    ch_setup = S.chan("setup")
    setup_loads = [
        (cvec[:, :], cvec_d), (gfin_b[:, :], gfin_d), (lng_b[:, :], lng_d), (lnb_b[:, :], lnb_d),
        (ident[:, :], ident_d), (nscr[:, 0, 0:512], sguT_d), (nscr[:, 0, 512:640], maskT_d),
        (nscr[0:64, 0, 640:896], Rs_d), (nscr[0:64, 0, 896:960], masks_d),
        (bsf[:, 0:512], bsrow_d), (bsf[:, 512:768], bsrow_s_d),
        (st_hT[:, :, :].rearrange("p a b -> p (a b)"), sth_d),
        (st_cT[:, :, :, :].rearrange("p a b c -> p (a b c)"), stc_d),
        (st_fT[:, :, :, :].rearrange("p a b c -> p (a b c)"), stf_d),
    ]
    for o_, i_ in setup_loads:
        S.dma("sp", ch_setup, lambda e, o_=o_, i_=i_: e.dma_start(out=o_, in_=i_), writes=[])
    B_gw = Buf("gatew")
    ch_gw = [S.chan("gw0"), S.chan("gw1")]
    S.dma("pool", ch_gw[0], lambda e: e.dma_start(out=wabd[:, :, :].rearrange("p a b -> p (a b)"), in_=wabd_d), writes=[B_gw])
    tgw = S.dma("pool", ch_gw[1], lambda e: e.dma_start(out=wxbd[:, :, :].rearrange("p a b -> p (a b)"), in_=wxbd_d), writes=[])
    tok_setup = (ch_setup.sem, ch_setup.cnt)
    B_const.w = tok_setup
    B_nscr[0].w = tok_setup
    B_h[1][0].w = tok_setup

    def DV(fn, r=(), w=()):
        return S.op("dve", fn, reads=r, writes=w)

    def AC(fn, r=(), w=()):
        return S.op("act", fn, reads=r, writes=w)

    for h_ in range(4):
        DV(lambda e, h_=h_: e.tensor_tensor(out=wsgu[:, h_, :], in0=nscr[:, 0, h_ * 128:(h_ + 1) * 128], in1=nscr[:, 0, 512:640], op=ALU.mult),
           r=[B_nscr[0]], w=[B_const])
        DV(lambda e, h_=h_: e.tensor_tensor(out=wsgu_s[0:64, h_, :], in0=nscr[0:64, 0, 640 + h_ * 64:640 + (h_ + 1) * 64], in1=nscr[0:64, 0, 896:960], op=ALU.mult),
           r=[B_nscr[0]], w=[B_const])
    DV(lambda e: e.tensor_copy(out=bsh[0:1, :], in_=bsf), r=[B_const, B_h[1][0]], w=[B_const])
    bsg_v = nscr[0:1, 1, 0:768]
    DV(lambda e: e.tensor_copy(out=bsg_v, in_=bsh[0:1, :]), r=[B_const], w=[B_const, B_nscr[1]])
    DV(lambda e: e.tensor_tensor(out=bsl[0:1, :], in0=bsf, in1=bsg_v, op=ALU.subtract), r=[B_const, B_nscr[1], B_h[1][0]], w=[B_const])
    DV(lambda e: e.memset(ones_bf[0:1, :], 1.0), w=[B_const])
    DV(lambda e: e.memset(ones_f[:, :], 1.0), w=[B_const])
    DV(lambda e: e.memset(ones_b2[:, :], 1.0), w=[B_const])
    DV(lambda e: e.memset(mhalf[:, :], -0.5), w=[B_const])
    DV(lambda e: e.memset(xhist[:, :, :], 0.0), w=B_xhist)
    DV(lambda e: e.memset(hlast[:, :], 0.0), w=B_hlast)
    DV(lambda e: e.memset(fhist[:, :, :], 0.0), w=B_fhist)
    AC(lambda e: e.activation(out=spt[:, :], in_=cvec[:, C_AP:C_AP + 4], func=AF.Exp, scale=-1.0), r=[B_const], w=[B_const])
    AC(lambda e: e.activation(out=spt[:, :], in_=spt[:, :], func=AF.Ln, bias=1.0), r=[B_const], w=[B_const])
    DV(lambda e: e.tensor_scalar(out=cA[:, :], in0=spt[:, :], scalar1=-8.0, scalar2=None, op0=ALU.mult), r=[B_const], w=[B_const])
    DV(lambda e: e.tensor_scalar(out=cA2[:, :], in0=spt[:, :], scalar1=-16.0, scalar2=None, op0=ALU.mult), r=[B_const], w=[B_const])

    ch_wbf = [S.chan("wbf%d" % s_) for s_ in range(NSLOT)]
    ch_ring = [S.chan("ring%d" % k) for k in range(RING)]
    total_loads = NSLOT * ntiles
    st = {"next_load": 0, "released": [False] * total_loads, "prefix": 0, "cast": 0, "bank": 0}

    def emit_cast(s_):
        src = wall[s_, :, :].rearrange("p (a b) -> p a b", b=2048)
        dst = wbf[s_, :, :].rearrange("p (a b) -> p a b", b=2048)
        S.dma("pool", ch_wbf[s_], lambda e: e.dma_start(out=dst, in_=src), writes=[B_wbf[s_]])

    def pump():
        while st["next_load"] < total_loads and st["next_load"] - RING < st["prefix"]:
            g = st["next_load"]
            s_ = g % NSLOT
            k = g % RING
            S.dma("sp", ch_ring[k],
                  lambda e, s_=s_, k=k: e.dma_start(out=ring[:, k, :, :].rearrange("p a b -> p (a b)"), in_=wbf[s_, :, :]),
                  reads=[B_wbf[s_]], writes=[B_ring[k]])
            st["next_load"] += 1

    def wget(g):
        pump()
        assert g < st["next_load"], "weight slot %d not loaded (ring deadlock)" % g
        return g % RING

    def wrel(g):
        st["released"][g] = True
        while st["prefix"] < total_loads and st["released"][st["prefix"]]:
            st["prefix"] += 1
        pump()

    reserved = set()

    def nbank():
        b = st["bank"]
        while b in reserved:
            b = (b + 1) % 8
        st["bank"] = (b + 1) % 8
        return b

    def mm_group(bank, out_ap, pairs, reads, first_start=True, skip=False):
        n = len(pairs)
        for i_, (l_, r_) in enumerate(pairs):
            S.op("pe", lambda e, l_=l_, r_=r_, i_=i_: e.matmul(out_ap, l_, r_, start=(first_start and i_ == 0), stop=(i_ == n - 1),
                                                         skip_group_check=skip),
                 reads=reads, writes=[B_ps[bank]], inc=(i_ == n - 1))

    ch_x = [S.chan("x%d" % k) for k in range(2)]
    ch_p = [S.chan("p%d" % k) for k in range(2)]
    ch_y = [S.chan("y%d" % k) for k in range(2)]
    ch_out = S.chan("out")
    out_toks = []
    pool_toks = []

    def tile_info(i):
        if i < 4:
            return dict(T=512, NTS=4, TP=128, nb=1, L=512, sample=False)
        return dict(T=64, NTS=1, TP=64, nb=16, L=4, sample=True)

    def emit_xload(i):
        hb = i % 2
        ti = tile_info(i)
        if not ti["sample"]:
            src = xp[i * 512:(i + 1) * 512, :].rearrange("(t p) f -> p t f", p=128)
            S.dma("sp", ch_x[hb], lambda e: e.dma_start(out=hbuf[:, hb, :, :], in_=src), writes=B_h[hb])
            psrc = ppT[:, i * 512:(i + 1) * 512].rearrange("(k p) t -> p k t", p=128)
            S.dma("pool", ch_p[hb], lambda e: e.dma_start(out=pTb[:, hb, :, :], in_=psrc), writes=[B_pTb[hb]])
        else:
            S.dma("sp", ch_x[hb], lambda e: e.dma_start(out=hbuf[0:64, hb, 0, :], in_=xs[:, :]), writes=[B_h[hb][0]])
            psrc = psT[:, :].rearrange("(k p) t -> p k t", p=128)
            S.dma("pool", ch_p[hb], lambda e: e.dma_start(out=pTb[:, hb, :, 0:64], in_=psrc), writes=[B_pTb[hb]])

    def rstd_chain(src_ap, ms_ap, rstd_ap, scale, Bsrc, Bms, Brstd):
        DV(lambda e: e.tensor_scalar(out=ms_ap, in0=src_ap, scalar1=scale, scalar2=EPS, op0=ALU.mult, op1=ALU.add), r=Bsrc, w=[Bms])
        npart, ncol = ms_ap.shape[0], ms_ap.shape[1]
        S.op("pool", lambda e: e.tensor_tensor(out=rstd_ap, in0=ms_ap, in1=mhalf[:npart, 0:ncol], op=ALU.pow), reads=[Bms, B_const], writes=[Brstd])

    def norm(hb, gcol, ti, phase="all"):
        T, NTS, TP = ti["T"], ti["NTS"], ti["TP"]
        banks = list(range(8))
        NPRE = min(2, NTS)

        def stat(ts):
            AC(lambda e: e.activation(out=junk[:TP, :], in_=hbuf[:TP, hb, ts, :], func=AF.Square, accum_out=ssn[:TP, ts:ts + 1]),
               r=[B_h[hb][ts]], w=[B_junk, B_ssn[ts]])
            DV(lambda e: e.tensor_scalar(out=msn[:TP, ts:ts + 1], in0=ssn[:TP, ts:ts + 1], scalar1=1.0 / D, scalar2=EPS, op0=ALU.mult, op1=ALU.add),
               r=[B_ssn[ts]], w=[B_msn[ts]])
            S.op("pool", lambda e: e.tensor_tensor(out=rstdn[:TP, ts:ts + 1], in0=msn[:TP, ts:ts + 1], in1=mhalf[:TP, 0:1], op=ALU.pow),
                 reads=[B_msn[ts], B_const], writes=[B_rstdn[ts]])

        def scale(ts):
            nk = ts % 2
            DV(lambda e: e.tensor_scalar(out=nscr[:TP, nk, :], in0=hbuf[:TP, hb, ts, :], scalar1=rstdn[:TP, ts:ts + 1], scalar2=None, op0=ALU.mult),
               r=[B_h[hb][ts], B_rstdn[ts]], w=[B_nscr[nk]])

        def transp(ts):
            nk = ts % 2
            for kt in range(8):
                S.op("pe", lambda e, kt=kt: e.transpose(out=psum[banks[kt]][:, ts * 128:ts * 128 + TP], in_=nscr[:TP, nk, kt * 128:(kt + 1) * 128],
                                                        identity=ident[:TP, :TP]),
                     reads=[B_nscr[nk], B_const], writes=[B_ps[banks[kt]]], inc=(kt == 7))

        if phase == "pre":
            for ts in range(NTS):
                stat(ts)
            for ts in range(NPRE):
                scale(ts)
            return
        if phase == "post":
            for ts in range(NPRE):
                transp(ts)
            for ts in range(NPRE, NTS):
                scale(ts)
                transp(ts)
        else:
            order = []
            for ts in range(NTS):
                order.append(("stat", ts))
                if ts >= 1:
                    order.append(("scale", ts - 1))
            order.append(("scale", NTS - 1))
            for kind, ts in order:
                if kind == "stat":
                    stat(ts)
                else:
                    scale(ts)
                    transp(ts)
        for kt in range(8):
            if kt % 2 == 0:
                AC(lambda e, kt=kt: e.activation(out=nT[:, kt, 0:T], in_=psum[banks[kt]][:, 0:T], func=AF.Identity, scale=cvec[:, gcol + kt:gcol + kt + 1]),
                   r=[B_ps[banks[kt]], B_const], w=[B_nT[kt]])
            else:
                DV(lambda e, kt=kt: e.tensor_scalar(out=nT[:, kt, 0:T], in0=psum[banks[kt]][:, 0:T], scalar1=cvec[:, gcol + kt:gcol + kt + 1], scalar2=None, op0=ALU.mult),
                   r=[B_ps[banks[kt]], B_const], w=[B_nT[kt]])
        st["bank"] = 0

    def outdma(dst, src, reads):
        pool_toks.append(S.dma("pool", S.chan("o%d" % len(pool_toks)), lambda e: e.dma_start(out=dst, in_=src), reads=reads))

    emit_xload(0)
    S.wait("pool", tok_setup)
    S.wait("pool", (ch_x[0].sem, ch_x[0].cnt))
    for s_ in range(8):
        emit_cast(s_)
    for i in range(ntiles):
        ti = tile_info(i)
        T, NTS, TP, nb, L, sample = ti["T"], ti["NTS"], ti["TP"], ti["nb"], ti["L"], ti["sample"]
        hb = i % 2
        g0 = i * NSLOT
        def v3(ap, l=L):
            return ap.rearrange("p (b l) -> p b l", l=l) if sample else ap

        if i == 0:
            norm(hb, C_GMIX, ti)
            for s_ in range(8, NSLOT):
                emit_cast(s_)

        EXT = 3 + L
        def phaseA1():
            for ct in range(4):
                xav = xa_ext[:, ct, 0:nb * EXT].rearrange("p (b l) -> p b l", l=EXT)
                xcv = xc[:, ct, 0:T].rearrange("p (b l) -> p b l", l=L)
                cw = C_CAW + ct * 4
                DV(lambda e, xav=xav, xcv=xcv, cw=cw, ct=ct: e.tensor_scalar(out=xcv, in0=xav[:, :, 0:L], scalar1=cvec[:, cw:cw + 1], scalar2=cvec[:, C_CAB + ct:C_CAB + ct + 1],
                                                                        op0=ALU.mult, op1=ALU.add), r=[B_xa[ct], B_const], w=[B_xc[ct]])
                for j in range(1, 4):
                    DV(lambda e, xav=xav, xcv=xcv, cw=cw, j=j: e.scalar_tensor_tensor(out=xcv, in0=xav[:, :, j:j + L], scalar=cvec[:, cw + j:cw + j + 1], in1=xcv,
                                                                                 op0=ALU.mult, op1=ALU.add), r=[B_xa[ct], B_const, B_xc[ct]], w=[B_xc[ct]])
                DV(lambda e, ct=ct: e.tensor_copy(out=xcb[:, ct, 0:T], in_=xc[:, ct, 0:T]), r=[B_xc[ct]], w=[B_xcb[ct]])


        sbank = nbank()
        reserved.add(sbank)
        first_stat = [True]

        def stats_mm(sq_ap, Bsq, base):
            for ts in range(NTS):
                fs = first_stat[0]
                first_stat[0] = False
                S.op("pe", lambda e, ts=ts, fs=fs: e.matmul(psum[sbank][:TP, base + 2 * ts:base + 2 * ts + 2], sq_ap[:, ts * 128:ts * 128 + TP], ones_b2[:, 0:2],
                                                        start=fs, stop=True, skip_group_check=True),
                     reads=[Bsq, B_const], writes=[B_ps[sbank]], inc=(ts == NTS - 1))

        def gatesA(ct):
            b1 = nbank()
            S.wait("pe", tgw)
            mm_group(b1, psum[b1][:, 0:T], [(wabd[:, ct, :], xcb[:, ct, 0:T])], reads=[B_const, B_gw, B_xcb[ct]])
            b2 = nbank()
            mm_group(b2, psum[b2][:, 0:T], [(wxbd[:, ct, :], xcb[:, ct, 0:T])], reads=[B_const, B_xcb[ct]])
            return b1, b2

        def sigA(ct, b1, b2):
            AC(lambda e: e.activation(out=rr4[:, ct, 0:T], in_=psum[b1][:, 0:T], func=AF.Sigmoid, bias=cvec[:, C_BA + ct:C_BA + ct + 1]),
               r=[B_ps[b1], B_const], w=[B_rr4[ct]])
            AC(lambda e: e.activation(out=ig4[:, ct, 0:T], in_=psum[b2][:, 0:T], func=AF.Sigmoid, bias=cvec[:, C_BX + ct:C_BX + ct + 1]),
               r=[B_ps[b2], B_const], w=[B_ig4[ct]])

        def expA(ct):
            AC(lambda e: e.activation(out=e4[:, ct, 0:T], in_=rr4[:, ct, 0:T], func=AF.Exp, scale=cA2[:, ct:ct + 1]), r=[B_rr4[ct], B_const], w=[B_e4[ct]])
            AC(lambda e: e.activation(out=rr4[:, ct, 0:T], in_=rr4[:, ct, 0:T], func=AF.Exp, scale=cA[:, ct:ct + 1]), r=[B_rr4[ct], B_const], w=[B_rr4[ct]])

        def sqrtA(ct):
            AC(lambda e: e.activation(out=e4[:, ct, 0:T], in_=e4[:, ct, 0:T], func=AF.Sqrt, scale=-1.0, bias=1.0), r=[B_e4[ct]], w=[B_e4[ct]])

        def postA(ct):
            hk = ct % 2
            rr = rr4[:, ct, 0:T]
            ee = e4[:, ct, 0:T]
            ig = ig4[:, ct, 0:T]
            xcc = xc[:, ct, 0:T]
            if i == 0:
                DV(lambda e: e.memset(e4[:, ct, 0:1], 1.0), w=[B_e4[ct]])
            DV(lambda e: e.tensor_tensor(out=ig, in0=ig, in1=ee, op=ALU.mult), r=[B_ig4[ct], B_e4[ct]], w=[B_ig4[ct]])
            DV(lambda e: e.tensor_tensor(out=xcc, in0=xcc, in1=ig, op=ALU.mult), r=[B_ig4[ct], B_xc[ct]], w=[B_xc[ct]])
            if sample:
                rv = rr.rearrange("p (b l) -> p b l", l=L)
                uv = xcc.rearrange("p (b l) -> p b l", l=L)
                DV(lambda e: e.tensor_tensor(out=tmp16[:, :], in0=rv[:, :, 0], in1=st_hT[:, ct, :], op=ALU.mult), r=[B_rr4[ct], B_const], w=[B_tmp16])
                DV(lambda e: e.tensor_tensor(out=uv[:, :, 0], in0=uv[:, :, 0], in1=tmp16[:, :], op=ALU.add), r=[B_tmp16, B_xc[ct]], w=[B_xc[ct]])
                DV(lambda e: e.memset(rv[:, :, 0], 0.0), w=[B_rr4[ct]])
                DV(lambda e: e.tensor_tensor_scan(out=hs[:, hk, 0:T], data0=rr, data1=xcc, initial=0.0, op0=ALU.mult, op1=ALU.add),
                   r=[B_rr4[ct], B_xc[ct]], w=[B_hs[hk]])
                DV(lambda e: e.tensor_copy(out=hsout[:, ct, :], in_=hs[:, hk, 0:T].rearrange("p (b l) -> p b l", l=L)[:, :, L - 1]), r=[B_hs[hk]], w=[B_hsout])
            else:
                DV(lambda e: e.tensor_tensor_scan(out=hs[:, hk, 0:T], data0=rr, data1=xcc, initial=hlast[:, ct:ct + 1], op0=ALU.mult, op1=ALU.add),
                   r=[B_rr4[ct], B_xc[ct], B_hlast[ct]], w=[B_hs[hk]])
                DV(lambda e: e.tensor_copy(out=hlast[:, ct:ct + 1], in_=hs[:, hk, L - 1:L]), r=[B_hs[hk]], w=[B_hlast[ct]])
            DV(lambda e: e.tensor_tensor(out=ee, in0=hs[:, hk, 0:T], in1=sh3[:, ct, 0:T], op=ALU.mult), r=[B_hs[hk]] + B_gga[ct], w=[B_e4[ct]])

        def postA_act(ct):
            ee = e4[:, ct, 0:T]
            sqv = ig4[:, ct, :].bitcast(BF16)[:, 0:T]
            AC(lambda e: e.activation(out=sqv, in_=ee, func=AF.Square), r=[B_e4[ct]], w=[B_ig4[ct]])
            stats_mm(sqv, B_ig4[ct], 0)
            AC(lambda e: e.activation(out=yabT[:, ct, 0:T], in_=ee, func=AF.Identity, scale=cvec[:, C_GOA + ct:C_GOA + ct + 1]),
               r=[B_e4[ct], B_const], w=[B_yab[ct]])

        for c in range(3):
            k = wget(g0 + c)
            for ct in range(4):
                bank = nbank()
                mm_group(bank, psum[bank][:, 0:T], [(ring[:, k, kt, ct * 128:(ct + 1) * 128], nT[:, kt, 0:T]) for kt in range(8)],
                         reads=[B_ring[k]] + B_nT)
                if c == 0:
                    xav = xa_ext[:, ct, 0:nb * EXT].rearrange("p (b l) -> p b l", l=EXT)
                    if sample:
                        DV(lambda e, xav=xav, ct=ct: e.tensor_copy(out=xav[:, :, 0:3], in_=st_cT[:, ct, :, :]), r=[B_const], w=[B_xa[ct]])
                    else:
                        DV(lambda e, xav=xav, ct=ct: e.tensor_copy(out=xav[:, :, 0:3], in_=xhist[:, ct:ct + 1, :]), r=[B_xhist[ct]], w=[B_xa[ct]])
                    DV(lambda e, xav=xav, bank=bank: e.tensor_copy(out=xav[:, :, 3:3 + L], in_=psum[bank][:, 0:T].rearrange("p (b l) -> p b l", l=L)),
                       r=[B_ps[bank]], w=[B_xa[ct]])
                    if not sample:
                        DV(lambda e, xav=xav, ct=ct: e.tensor_copy(out=xhist[:, ct:ct + 1, :], in_=xav[:, :, L:L + 3]), r=[B_xa[ct]], w=[B_xhist[ct]])
                    else:
                        DV(lambda e, xav=xav, ct=ct: e.tensor_copy(out=cs_out[:, ct, :, :], in_=xav[:, :, 4:7]), r=[B_xa[ct]], w=[B_csout])
                        if ct == 3:
                            outdma(o_cs[:, :], cs_out[:, :, :, :].rearrange("p a b c -> p (a b c)"), [B_csout])
                elif c == 1:
                    AC(lambda e, bank=bank, ct=ct: e.activation(out=sh3[:, ct, 0:T], in_=psum[bank][:, 0:T], func=AF.Gelu_apprx_tanh),
                       r=[B_ps[bank]], w=B_gga[ct])
                else:
                    AC(lambda e, bank=bank, ct=ct: e.activation(out=sh3[:, 4 + ct, 0:T], in_=psum[bank][:, 0:T], func=AF.Gelu_apprx_tanh),
                       r=[B_ps[bank]], w=B_gub[ct])
            wrel(g0 + c)
            if c == 0:
                phaseA1()
        k = wget(g0 + 3)

        def vb_mm(ts):
            bank = nbank()
            kk = ts % 2
            mm_group(bank, psum[bank][:TP, :], [(nT[:, kt, ts * 128:ts * 128 + TP], ring[:, k, kt, :]) for kt in range(8)],
                     reads=[B_ring[k]] + B_nT)
            gv = sh3[:TP, 8 + kk, :]
            AC(lambda e, bank=bank, gv=gv: e.activation(out=gv, in_=psum[bank][:TP, :], func=AF.Gelu_apprx_tanh), r=[B_ps[bank]], w=B_gvb[kk])

        def vb_ln(ts):
            kk = ts % 2
            gv = sh3[:TP, 8 + kk, :]
            DV(lambda e, gv=gv, kk=kk: e.bn_stats(out=bst[:TP, kk, :], in_=gv), r=B_gvb[kk], w=[B_bst[kk]])
            DV(lambda e, kk=kk: e.bn_aggr(out=mv[:TP, kk, :], in_=bst[:TP, kk, :]), r=[B_bst[kk]], w=[B_mv[kk]])
            DV(lambda e, kk=kk: e.tensor_scalar(out=rsv[:TP, kk:kk + 1], in0=mv[:TP, kk, 1:2], scalar1=1.0, scalar2=EPS, op0=ALU.mult, op1=ALU.add),
               r=[B_mv[kk]], w=[B_rsv[kk]])
            S.op("pool", lambda e, kk=kk: e.tensor_tensor(out=rsv[:TP, kk:kk + 1], in0=rsv[:TP, kk:kk + 1], in1=mhalf[:TP, 0:1], op=ALU.pow),
                 reads=[B_rsv[kk], B_const], writes=[B_rsv[kk]])
            DV(lambda e, gv=gv, kk=kk: e.tensor_scalar(out=gv, in0=gv, scalar1=mv[:TP, kk, 0:1], scalar2=rsv[:TP, kk:kk + 1], op0=ALU.subtract, op1=ALU.mult),
               r=B_gvb[kk] + [B_mv[kk], B_rsv[kk]], w=B_gvb[kk])
            S.op("pool", lambda e, gv=gv: e.tensor_tensor(out=gv, in0=gv, in1=lng_b[:TP, :], op=ALU.mult), reads=B_gvb[kk] + [B_const], writes=B_gvb[kk])
            if not sample:
                S.op("pool", lambda e, gv=gv, ts=ts: e.tensor_tensor(out=vnb[:TP, ts, :], in0=gv, in1=lnb_b[:TP, :], op=ALU.add), reads=B_gvb[kk] + [B_const], writes=[B_vnb[ts]])
            else:
                S.op("pool", lambda e, gv=gv: e.tensor_tensor(out=gv, in0=gv, in1=lnb_b[:TP, :], op=ALU.add), reads=B_gvb[kk] + [B_const], writes=B_gvb[kk])
                AC(lambda e, gv=gv, ts=ts: e.activation(out=vnb[:TP, ts, :], in_=gv, func=AF.Copy), r=B_gvb[kk], w=[B_vnb[ts]])
                pool_toks.append(S.dma("pool", S.chan("ovs"), lambda e, gv=gv: e.dma_start(out=o_vs[:, :], in_=gv), reads=B_gvb[kk]))
        for ts in range(min(2, NTS)):
            vb_mm(ts)
            vb_ln(ts)
        for ct in range(4):
            sigA(ct, *gatesA(ct))
        for ct in range(4):
            expA(ct)
        for ts in range(2, NTS):
            vb_mm(ts)
            vb_ln(ts)
        wrel(g0 + 3)
        for ct in range(4):
            sqrtA(ct)
        for ct in range(2):
            postA(ct)
            postA_act(ct)
        sgu_banks = []
        for h_ in range(4):
            k2 = h_ % 2
            bank = nbank()
            sgu_banks.append(bank)
            if not sample:
                nmm = NTS * 3
                idx = 0
                for ts in range(NTS):
                    oc = psum[bank][:, ts * 128:(ts + 1) * 128]
                    trip = [(vnb[:, ts, h_ * 128:(h_ + 1) * 128], wsgu[:, h_, :]),
                            (ones_bf[0:1, :], bsh[0:1, h_ * 128:(h_ + 1) * 128]),
                            (ones_bf[0:1, :], bsl[0:1, h_ * 128:(h_ + 1) * 128])]
                    for l_, r_ in trip:
                        S.op("pe", lambda e, oc=oc, l_=l_, r_=r_, idx=idx: e.matmul(oc, l_, r_, start=(idx == 0), stop=(idx == nmm - 1), skip_group_check=True),
                             reads=[B_vnb[ts], B_const], writes=[B_ps[bank]], inc=(idx == nmm - 1))
                        idx += 1
            else:
                oc = psum[bank][:, 0:64]
                trip = [(vnb[0:64, 0, h_ * 128:(h_ + 1) * 128], wsgu_s[0:64, h_, :]),
                        (ones_bf[0:1, :], bsh[0:1, 512 + h_ * 64:512 + (h_ + 1) * 64]),
                        (ones_bf[0:1, :], bsl[0:1, 512 + h_ * 64:512 + (h_ + 1) * 64])]
                for idx, (l_, r_) in enumerate(trip):
                    S.op("pe", lambda e, oc=oc, l_=l_, r_=r_, idx=idx: e.matmul(oc, l_, r_, start=(idx == 0), stop=(idx == 2), skip_group_check=True),
                         reads=[B_vnb[0], B_const], writes=[B_ps[bank]], inc=(idx == 2))
        for ct in range(2, 4):
            postA(ct)
            postA_act(ct)
        if sample:
            outdma(o_hs[:, :], hsout[:, :, :].rearrange("p a b -> p (a b)"), [B_hsout])
        if i == 3:
            outdma(o_hp[:, :], hlast[:, :], B_hlast)
            outdma(o_cp[:, :], xhist[:, :, :].rearrange("p a b -> p (a b)"), B_xhist)

        for h_ in range(4):
            k2 = h_
            bank = sgu_banks[h_]
            ee = e4[:, k2, 0:T]
            ig = ig4[:, k2, 0:T]
            DV(lambda e, ee=ee, bank=bank, h_=h_: e.tensor_tensor(out=ee, in0=psum[bank][:, 0:T], in1=sh3[:, 4 + h_, 0:T], op=ALU.mult),
               r=[B_ps[bank]] + B_gub[h_], w=[B_e4[k2]])
            sqv = ig4[:, k2, :].bitcast(BF16)[:, 0:T]
            AC(lambda e, ee=ee, sqv=sqv: e.activation(out=sqv, in_=ee, func=AF.Square), r=[B_e4[k2]], w=[B_ig4[k2]])
            stats_mm(sqv, B_ig4[k2], 8)
            AC(lambda e, ee=ee, h_=h_: e.activation(out=yabT[:, 4 + h_, 0:T], in_=ee, func=AF.Identity, scale=cvec[:, C_GOB + h_:C_GOB + h_ + 1]),
               r=[B_e4[k2], B_const], w=[B_yab[4 + h_]])
        src8 = psum[sbank][:TP, 0:16].rearrange("p (h t two) -> p h t two", h=2, two=2)[:, :, 0:NTS, 0]
        ms8v = ms8[:TP, :].rearrange("p (h t) -> p h t", h=2)[:, :, 0:NTS]
        rstd8v = rstd8[:TP, :].rearrange("p (h t) -> p h t", h=2)[:, :, 0:NTS]
        mh8v = mhalf[:TP, :].rearrange("p (h t) -> p h t", h=2)[:, :, 0:NTS]
        DV(lambda e: e.tensor_scalar(out=ms8v, in0=src8, scalar1=1.0 / 512, scalar2=EPS, op0=ALU.mult, op1=ALU.add), r=[B_ps[sbank]], w=[B_ms8])
        S.op("pool", lambda e: e.tensor_tensor(out=rstd8v, in0=ms8v, in1=mh8v, op=ALU.pow), reads=[B_ms8, B_const], writes=[B_rstd8])
        reserved.discard(sbank)

        for half_ in range(2):
            k = wget(g0 + 4 + half_)
            for ts in range(NTS):
                for grp in range(2):
                    bank = nbank()
                    mm_group(bank, psum[bank][:TP, :], [(yabT[:, ct, ts * 128:ts * 128 + TP], ring[:, k, ct, :]) for ct in range(grp * 4, grp * 4 + 4)],
                             reads=[B_ring[k]] + B_yab)
                    hsl = hbuf[:TP, hb, ts, half_ * 512:(half_ + 1) * 512]
                    DV(lambda e, bank=bank, hsl=hsl, ts=ts, grp=grp: e.scalar_tensor_tensor(out=hsl, in0=psum[bank][:TP, :], scalar=rstd8[:TP, grp * 4 + ts:grp * 4 + ts + 1],
                                                                                       in1=hsl, op0=ALU.mult, op1=ALU.add),
                       r=[B_ps[bank], B_rstd8, B_h[hb][ts]], w=[B_h[hb][ts]])
            wrel(g0 + 4 + half_)

        norm(hb, C_GFFN, ti)
        if not sample:
            fwv = cvec[:, C_FW:C_FW + 144].rearrange("p (f j) -> p f j", j=3)
            DV(lambda e: e.tensor_tensor(out=corr[:, :, 0], in0=fhist[:, :, 0], in1=fwv[:, :, 0], op=ALU.mult), r=B_fhist + [B_const], w=[B_corr])
            DV(lambda e: e.tensor_tensor(out=corr[:, :, 1], in0=fhist[:, :, 1], in1=fwv[:, :, 1], op=ALU.mult), r=B_fhist + [B_const], w=[B_corr])
            DV(lambda e: e.tensor_tensor(out=corr[:, :, 0], in0=corr[:, :, 0], in1=corr[:, :, 1], op=ALU.add), r=[B_corr], w=[B_corr])
            DV(lambda e: e.tensor_tensor(out=corr[:, :, 1], in0=fhist[:, :, 1], in1=fwv[:, :, 0], op=ALU.mult), r=B_fhist + [B_const], w=[B_corr])
        for hg in range(6):
            kg = wget(g0 + 6 + 2 * hg)
            kl = wget(g0 + 7 + 2 * hg)
            for jj in range(4):
                j = hg * 4 + jj
                s2 = j % 2
                for part, kw in ((0, kg), (1, kl)):
                    ft = part * 24 + j
                    bank = nbank()
                    mm_group(bank, psum[bank][:, 0:T], [(ring[:, kw, kt, jj * 128:(jj + 1) * 128], nT[:, kt, 0:T]) for kt in range(8)],
                             reads=[B_ring[kw]] + B_nT)
                    cc = e4[:, 2 * s2 + part, :]
                    Bc = B_fc[s2][part]
                    w0 = cvec[:, C_FW + ft * 3:C_FW + ft * 3 + 1]
                    w1 = cvec[:, C_FW + ft * 3 + 1:C_FW + ft * 3 + 2]
                    w2 = cvec[:, C_FW + ft * 3 + 2:C_FW + ft * 3 + 3]
                    bb = cvec[:, C_FB + ft:C_FB + ft + 1]
                    pb = psum[bank]
                    if not sample:
                        AC(lambda e, cc=cc, pb=pb, w2=w2, bb=bb: e.activation(out=cc[:, 0:L], in_=pb[:, 0:L], func=AF.Identity, scale=w2, bias=bb),
                           r=[B_ps[bank], B_const], w=[Bc])
                        DV(lambda e, cc=cc, pb=pb, w1=w1: e.scalar_tensor_tensor(out=cc[:, 1:L], in0=pb[:, 0:L - 1], scalar=w1, in1=cc[:, 1:L], op0=ALU.mult, op1=ALU.add),
                           r=[B_ps[bank], B_const, Bc], w=[Bc])
                        DV(lambda e, cc=cc, pb=pb, w0=w0: e.scalar_tensor_tensor(out=cc[:, 2:L], in0=pb[:, 0:L - 2], scalar=w0, in1=cc[:, 2:L], op0=ALU.mult, op1=ALU.add),
                           r=[B_ps[bank], B_const, Bc], w=[Bc])
                        DV(lambda e, cc=cc, ft=ft: e.tensor_tensor(out=cc[:, 0:2], in0=cc[:, 0:2], in1=corr[:, ft, :], op=ALU.add),
                           r=[B_corr, Bc], w=[Bc])
                        AC(lambda e, pb=pb, ft=ft: e.activation(out=fhist[:, ft, :], in_=pb[:, L - 2:L], func=AF.Copy), r=[B_ps[bank]], w=[B_fhist[ft]])
                    else:
                        fx = fext[:, part, :, :]
                        Bx = B_fext[part]
                        ccv = cc[:, 0:T].rearrange("p (b l) -> p b l", l=L)
                        AC(lambda e, fx=fx, ft=ft: e.activation(out=fx[:, :, 0:2], in_=st_fT[:, ft, :, :], func=AF.Copy), r=[B_stf[ft], B_const], w=[Bx])
                        AC(lambda e, fx=fx, pb=pb: e.activation(out=fx[:, :, 2:6], in_=pb[:, 0:T].rearrange("p (b l) -> p b l", l=L), func=AF.Copy),
                           r=[B_ps[bank]], w=[Bx])
                        DV(lambda e, fx=fx, ccv=ccv, w0=w0, bb=bb: e.tensor_scalar(out=ccv, in0=fx[:, :, 0:4], scalar1=w0, scalar2=bb, op0=ALU.mult, op1=ALU.add),
                           r=[Bx, B_const], w=[Bc])
                        DV(lambda e, fx=fx, ccv=ccv, w1=w1: e.scalar_tensor_tensor(out=ccv, in0=fx[:, :, 1:5], scalar=w1, in1=ccv, op0=ALU.mult, op1=ALU.add),
                           r=[Bx, B_const, Bc], w=[Bc])
                        DV(lambda e, fx=fx, ccv=ccv, w2=w2: e.scalar_tensor_tensor(out=ccv, in0=fx[:, :, 2:6], scalar=w2, in1=ccv, op0=ALU.mult, op1=ALU.add),
                           r=[Bx, B_const, Bc], w=[Bc])
                        AC(lambda e, fx=fx, ft=ft: e.activation(out=st_fT[:, ft, :, :], in_=fx[:, :, 4:6], func=AF.Copy), r=[Bx], w=[B_stf[ft]])
                AC(lambda e, s2=s2: e.activation(out=ig4[:, s2, 0:T], in_=e4[:, 2 * s2, 0:T], func=AF.Gelu_apprx_tanh), r=[B_fc[s2][0]], w=[B_fG[s2]])
                S.op("pool", lambda e, s2=s2, j=j: e.tensor_tensor(out=actT[:, j, 0:T], in0=ig4[:, s2, 0:T], in1=e4[:, 2 * s2 + 1, 0:T], op=ALU.mult),
                     reads=[B_fG[s2], B_fc[s2][1]], writes=B_act[j])
            wrel(g0 + 6 + 2 * hg)
            wrel(g0 + 7 + 2 * hg)
        if i + 1 < ntiles:
            emit_xload(i + 1)
        if i == 3:
            outdma(o_fp[:, :], fhist[:, :, :].rearrange("p a b -> p (a b)"), B_fhist)
        if sample:
            outdma(o_fs[:, :], st_fT[:, :, :, :].rearrange("p a b c -> p (a b c)"), B_stf)
        for half_ in range(2):
            banks = [nbank() for _ in range(NTS)]
            for jg in range(3):
                gslot = g0 + 18 + half_ * 3 + jg
                k = wget(gslot)
                for ts in range(NTS):
                    for j8 in range(8):
                        j = jg * 8 + j8
                        S.op("pe", lambda e, ts=ts, j=j, j8=j8, jg=jg, k=k: e.matmul(psum[banks[ts]][:TP, :], actT[:, j, ts * 128:ts * 128 + TP], ring[:, k, j8, :],
                                                                                 start=(jg == 0 and j8 == 0), stop=(jg == 2 and j8 == 7), skip_group_check=True),
                             reads=[B_ring[k]] + B_act[j], writes=[B_ps[banks[ts]]], inc=(j8 == 7))
                wrel(gslot)
            for ts in range(NTS):
                hsl = hbuf[:TP, hb, ts, half_ * 512:(half_ + 1) * 512]
                DV(lambda e, ts=ts, hsl=hsl: e.tensor_tensor(out=hsl, in0=psum[banks[ts]][:TP, :], in1=hsl, op=ALU.add),
                   r=[B_ps[banks[ts]], B_h[hb][ts]], w=[B_h[hb][ts]])

        norm(hb, C_GPLE, ti)
        if i + 1 < ntiles:
            norm((i + 1) % 2, C_GMIX, tile_info(i + 1), phase="pre")
        kp = wget(g0 + 24)
        for half_ in range(2):
            kgt = wget(g0 + 25 + half_)
            for ts in range(NTS):
                s2 = ts % 2
                bg = nbank()
                mm_group(bg, psum[bg][:TP, :], [(nT[:, kt, ts * 128:ts * 128 + TP], ring[:, kgt, kt, :]) for kt in range(8)], reads=[B_ring[kgt]] + B_nT)
                bp = nbank()
                mm_group(bp, psum[bp][:TP, :], [(pTb[:, hb, kt, ts * 128:ts * 128 + TP], ring[:, kp, kt * 2 + half_, :]) for kt in range(2)],
                         reads=[B_ring[kp], B_pTb[hb]])
                AC(lambda e, s2=s2, bg=bg: e.activation(out=ig4[:TP, s2, :], in_=psum[bg][:TP, :], func=AF.Sigmoid), r=[B_ps[bg]], w=[B_gt[s2]])
                DV(lambda e, s2=s2, bp=bp: e.tensor_tensor(out=e4[:TP, 2 * s2, :], in0=psum[bp][:TP, :], in1=ig4[:TP, s2, :], op=ALU.mult),
                   r=[B_ps[bp], B_gt[s2]], w=[B_pt[s2]])
                hsl = hbuf[:TP, hb, ts, half_ * 512:(half_ + 1) * 512]
                DV(lambda e, s2=s2, hsl=hsl: e.tensor_tensor(out=hsl, in0=hsl, in1=e4[:TP, 2 * s2, :], op=ALU.add), r=[B_pt[s2], B_h[hb][ts]], w=[B_h[hb][ts]])
            wrel(g0 + 25 + half_)
        wrel(g0 + 24)

        if i + 1 < ntiles:
            norm((i + 1) % 2, C_GMIX, tile_info(i + 1), phase="post")

        for ts in range(NTS):
            AC(lambda e, ts=ts: e.activation(out=junk[:TP, :], in_=hbuf[:TP, hb, ts, :], func=AF.Square, accum_out=ss[:TP, ts:ts + 1]),
               r=[B_h[hb][ts]], w=[B_junk, B_ss])
        rstd_chain(ss[:TP, 0:NTS], ms[:TP, 0:NTS], rstd[:TP, 0:NTS], 1.0 / D, [B_ss], B_ms, B_rstd)
        for ts in range(NTS):
            hv = hbuf[:TP, hb, ts, :]
            DV(lambda e, ts=ts, hv=hv: e.scalar_tensor_tensor(out=hv, in0=hv, scalar=rstd[:TP, ts:ts + 1], in1=gfin_b[:TP, :], op0=ALU.mult, op1=ALU.mult),
               r=[B_h[hb][ts], B_rstd, B_const], w=[B_h[hb][ts]])
        if not sample:
            dst = y_p[i * 512:(i + 1) * 512, :].rearrange("(t p) f -> p t f", p=128)
            out_toks.append(S.dma("sp", ch_y[hb], lambda e, dst=dst, hb=hb: e.dma_start(out=dst, in_=hbuf[:, hb, :, :]), reads=B_h[hb]))
        else:
            out_toks.append(S.dma("sp", ch_y[hb], lambda e, hb=hb: e.dma_start(out=y_s[:, :], in_=hbuf[0:64, hb, 0, :]), reads=[B_h[hb][0]]))

    for tok in out_toks:
        S.wait("sp", tok)
    for tok in pool_toks:
        S.wait("pool", tok)

    S.replay()
    stack.close()
    return nc


def _prep_shared(inp):
    f = np.float32
    w_in = np.asarray(inp["w_in"][0], f); w_out = np.asarray(inp["w_out"][0], f)
    w_up = np.asarray(inp["w_up"][0], f); w_down = np.asarray(inp["w_down"][0], f)
    w_pg = np.asarray(inp["w_ple_gate"][0], f); w_ple = np.asarray(inp["w_ple"][0], f)
    wall = np.zeros((NSLOT, 128, 4096), f)
    wall[0:4] = w_in.reshape(8, 128, 4, 512).transpose(2, 1, 0, 3).reshape(4, 128, 4096)
    wall[4:6] = w_out.reshape(8, 128, 2, 512).transpose(2, 1, 0, 3).reshape(2, 128, 4096)
    wall[6:18] = w_up.reshape(8, 128, 2, 6, 512).transpose(3, 2, 1, 0, 4).reshape(12, 128, 4096)
    wall[18:24] = w_down.reshape(3, 8, 128, 2, 512).transpose(3, 0, 2, 1, 4).reshape(6, 128, 4096)
    wall[24, :, 0:2048] = w_ple.reshape(2, 128, 2, 512).transpose(1, 0, 2, 3).reshape(128, 2048)
    wall[25:27] = w_pg.reshape(8, 128, 2, 512).transpose(2, 1, 0, 3).reshape(2, 128, 4096)

    def col(v, n):
        return np.asarray(v, f).reshape(n, 128).T

    cv = np.zeros((128, NCV), f)
    cv[:, C_GMIX:C_GMIX + 8] = col(inp["g_mix_norm"][0], 8)
    cv[:, C_GFFN:C_GFFN + 8] = col(inp["g_ffn_norm"][0], 8)
    cv[:, C_GPLE:C_GPLE + 8] = col(inp["g_ple_norm"][0], 8)
    caw = np.asarray(inp["conv_a_w"][0], f)
    cv[:, C_CAW:C_CAW + 16] = caw.reshape(4, 4, 128).transpose(2, 1, 0).reshape(128, 16)
    cv[:, C_CAB:C_CAB + 4] = col(inp["conv_a_b"][0], 4)
    cv[:, C_BA:C_BA + 4] = col(inp["lru_ba"][0], 4)
    cv[:, C_BX:C_BX + 4] = col(inp["lru_bx"][0], 4)
    cv[:, C_AP:C_AP + 4] = col(inp["lru_a_param"][0], 4)
    cv[:, C_GOA:C_GOA + 4] = col(inp["g_out_a"][0], 4)
    cv[:, C_GOB:C_GOB + 4] = col(inp["g_out_b"][0], 4)
    fw = np.asarray(inp["ffn_conv_w"][0], f)
    cv[:, C_FW:C_FW + 144] = fw.reshape(3, 48, 128).transpose(2, 1, 0).reshape(128, 144)
    cv[:, C_FB:C_FB + 48] = col(inp["ffn_conv_b"][0], 48)

    sgu_w = np.asarray(inp["sgu_w"][0], f)
    sgu_b = np.asarray(inp["sgu_b"][0], f)
    sguT = sgu_w.transpose(2, 0, 1).reshape(128, 512)
    maskT = (np.arange(128)[:, None] <= np.arange(128)[None, :]).astype(f)
    w4 = sgu_w[:, 0:4, 0:4]
    Rs = np.broadcast_to(w4.transpose(2, 0, 1)[None, :, :, None, :], (16, 4, 4, 16, 4)).reshape(64, 256).astype(f)
    bb = np.arange(16)
    mask_s = ((bb[:, None, None, None] == bb[None, None, :, None]) &
              (np.arange(4)[None, :, None, None] <= np.arange(4)[None, None, None, :])).astype(f).reshape(64, 64)
    bsrow = sgu_b.reshape(1, 512)
    bsrow_s = np.broadcast_to(sgu_b[:, None, 0:4], (4, 16, 4)).reshape(1, 256).astype(f)

    def bd(w):
        w = np.asarray(w, f)
        o = np.zeros((128, 4, 128), f)
        for ct in range(4):
            o[0:64, ct, 0:64] = w[2 * ct]
            o[64:128, ct, 64:128] = w[2 * ct + 1]
        return o.reshape(128, 512)

    return dict(
        wall=wall, cvec=cv,
        gfin_b=np.ascontiguousarray(np.broadcast_to(np.asarray(inp["g_final"], f)[None, :], (128, D))),
        lng_b=np.ascontiguousarray(np.broadcast_to(np.asarray(inp["ln_v_g"][0], f)[None, :], (128, 512))),
        lnb_b=np.ascontiguousarray(np.broadcast_to(np.asarray(inp["ln_v_b"][0], f)[None, :], (128, 512))),
        ident=np.eye(128, dtype=f), maskT=maskT, sguT=np.ascontiguousarray(sguT),
        wabd=bd(inp["lru_wa"][0]), wxbd=bd(inp["lru_wx"][0]),
        bsrow=np.ascontiguousarray(bsrow), Rs=np.ascontiguousarray(Rs), mask_s=mask_s, bsrow_s=np.ascontiguousarray(bsrow_s),
    )


_NC_CACHE = {}


def kernel(**inp):
    f = np.float32
    shared = _prep_shared(inp)
    x_prompt = np.asarray(inp["x_prompt"], f); x_sample = np.asarray(inp["x_sample"], f)
    p_prompt = np.asarray(inp["p_prompt"], f); p_sample = np.asarray(inp["p_sample"], f)
    st_h = np.asarray(inp["state_rglru_h"], f); st_c = np.asarray(inp["state_rglru_conv"], f)
    st_f = np.asarray(inp["state_ffn_conv"], f)
    in_maps = []
    for c in range(NCORES):
        sl = slice(16 * c, 16 * c + 16)
        m = dict(shared)
        m["xp"] = np.ascontiguousarray(x_prompt[c])
        m["xs"] = np.ascontiguousarray(x_sample[sl].reshape(64, D))
        m["ppT"] = np.ascontiguousarray(p_prompt[0, c].T)
        m["psT"] = np.ascontiguousarray(p_sample[0, sl].reshape(64, 256).T)
        m["st_hT"] = np.ascontiguousarray(st_h[0, sl].reshape(16, 4, 128).transpose(2, 1, 0).reshape(128, 64))
        m["st_cT"] = np.ascontiguousarray(st_c[0, sl].reshape(16, 3, 4, 128).transpose(3, 2, 0, 1).reshape(128, 192))
        m["st_fT"] = np.ascontiguousarray(st_f[0, sl].reshape(16, 2, 48, 128).transpose(3, 2, 0, 1).reshape(128, 1536))
        in_maps.append(m)
    if "nc" not in _NC_CACHE:
        _NC_CACHE["nc"] = build_nc()
    res = run_bass_kernel_spmd(_NC_CACHE["nc"], in_maps, core_ids=list(range(NCORES)))
    R = res.results
    y_prompt = np.stack([np.asarray(R[c]["y_p"], f) for c in range(NCORES)], 0)
    y_sample = np.concatenate([np.asarray(R[c]["y_s"], f).reshape(16, 4, D) for c in range(NCORES)], 0)
    h_p = np.stack([np.asarray(R[c]["o_hp"], f).T.reshape(512) for c in range(NCORES)], 0)[None]
    h_s = np.concatenate([np.asarray(R[c]["o_hs"], f).reshape(128, 4, 16).transpose(2, 1, 0).reshape(16, 512) for c in range(NCORES)], 0)[None]
    c_p = np.stack([np.asarray(R[c]["o_cp"], f).reshape(128, 4, 3).transpose(2, 1, 0).reshape(3, 512) for c in range(NCORES)], 0)[None]
    c_s = np.concatenate([np.asarray(R[c]["o_cs"], f).reshape(128, 4, 16, 3).transpose(2, 3, 1, 0).reshape(16, 3, 512) for c in range(NCORES)], 0)[None]
    v_s = np.concatenate([np.asarray(R[c]["o_vs"], f).reshape(16, 4, 512) for c in range(NCORES)], 0)[None]
    f_p = np.stack([np.asarray(R[c]["o_fp"], f).reshape(128, 48, 2).transpose(2, 1, 0).reshape(2, 6144) for c in range(NCORES)], 0)[None]
    f_s = np.concatenate([np.asarray(R[c]["o_fs"], f).reshape(128, 48, 16, 2).transpose(2, 3, 1, 0).reshape(16, 2, 6144) for c in range(NCORES)], 0)[None]
    return (y_prompt, y_sample, np.ascontiguousarray(h_p), np.ascontiguousarray(h_s), np.ascontiguousarray(c_p),
            np.ascontiguousarray(c_s), np.ascontiguousarray(v_s), np.ascontiguousarray(f_p), np.ascontiguousarray(f_s))
```

```python
import numpy as np
import concourse.bass as bass
import concourse.mybir as mybir
from concourse.bass_utils import run_bass_kernel_spmd

F32 = mybir.dt.float32
BF16 = mybir.dt.bfloat16
AF = mybir.ActivationFunctionType
ALU = mybir.AluOpType

NCORES = 8
D = 1024
WA = 512
DFF = 3072
EPS = 1e-6
NSLOT = 27
RING = 5
SAME_ENGINE_SYNC = True

C_GMIX, C_GFFN, C_GPLE, C_CAW, C_CAB, C_BA, C_BX, C_AP, C_GOA, C_GOB, C_FW, C_FB = 0, 8, 16, 24, 40, 44, 48, 52, 56, 60, 64, 208
NCV = 256


import types


def _freeze(fn):
    if fn is None or fn.__closure__ is None:
        return fn
    cells = []
    for c in fn.__closure__:
        try:
            cells.append(types.CellType(c.cell_contents))
        except ValueError:
            cells.append(c)
    return types.FunctionType(fn.__code__, fn.__globals__, fn.__name__, fn.__defaults__, tuple(cells))


class Sem:
    def __init__(self, nc, stack, name):
        self.h = stack.enter_context(nc.semaphore(name))
        self.name = name


class Buf:
    __slots__ = ("name", "w", "r")

    def __init__(self, name):
        self.name = name
        self.w = None
        self.r = {}


class Chan:
    def __init__(self, sem):
        self.sem = sem
        self.cnt = 0


class Eng:
    def __init__(self, name, sem, same_sync):
        self.name = name
        self.sem = sem
        self.cnt = 0
        self.waited = {}
        self.ops = []
        self.same_sync = same_sync


class Sched:
    def __init__(self, nc, stack):
        self.nc = nc
        self.stack = stack
        self.nsem = 0
        self.eng = {}
        for n in ("pe", "act", "dve", "pool", "sp"):
            self.eng[n] = Eng(n, self.newsem("e_" + n), SAME_ENGINE_SYNC and n not in ("pe", "sp"))

    def newsem(self, name):
        self.nsem += 1
        return Sem(self.nc, self.stack, name)

    def chan(self, name):
        return Chan(self.newsem("c_" + name))

    def _deps(self, e, reads, writes):
        deps = {}

        def add(s, v):
            if deps.get(s, 0) < v:
                deps[s] = v

        for b in reads:
            if b.w is not None:
                add(*b.w)
        for b in writes:
            if b.w is not None:
                add(*b.w)
            for s, v in b.r.items():
                add(s, v)
        waits = []
        for s, v in deps.items():
            if s is e.sem and not e.same_sync:
                continue
            if e.waited.get(s, 0) >= v:
                continue
            e.waited[s] = v
            waits.append((s, v))
        return waits

    def _commit(self, tok, reads, writes):
        for b in writes:
            b.w = tok
            b.r = {}
        for b in reads:
            if b.r.get(tok[0], 0) < tok[1]:
                b.r[tok[0]] = tok[1]

    def op(self, en, fn, reads=(), writes=(), inc=True):
        e = self.eng[en]
        waits = self._deps(e, reads, writes)
        tok = (e.sem, e.cnt + 1)
        if inc:
            e.cnt += 1
        e.ops.append((waits, _freeze(fn), (e.sem, 1) if inc else None))
        self._commit(tok, reads, writes)
        return tok

    def dma(self, en, ch, fn, reads=(), writes=()):
        e = self.eng[en]
        waits = self._deps(e, reads, writes)
        if en == "pool" and ch.cnt > 0 and e.waited.get(ch.sem, 0) < ch.cnt:
            e.waited[ch.sem] = ch.cnt
            waits.append((ch.sem, ch.cnt))
        ch.cnt += 16
        tok = (ch.sem, ch.cnt)
        e.ops.append((waits, _freeze(fn), (ch.sem, 16)))
        self._commit(tok, reads, writes)
        return tok

    def wait(self, en, tok):
        e = self.eng[en]
        if e.waited.get(tok[0], 0) >= tok[1]:
            return
        e.waited[tok[0]] = tok[1]
        e.ops.append(([tok], None, None))

    def replay(self):
        nc = self.nc
        with nc.Block() as block:
            def run(eobj, e):
                for waits, fn, inc in e.ops:
                    for s, v in waits:
                        eobj.wait_ge(s.h, v)
                    if fn is None:
                        continue
                    ins = fn(eobj)
                    if inc is not None:
                        ins.then_inc(inc[0].h, inc[1])

            @block.tensor
            def _(x):
                run(x, self.eng["pe"])

            @block.scalar
            def _(x):
                run(x, self.eng["act"])

            @block.vector
            def _(x):
                run(x, self.eng["dve"])

            @block.gpsimd
            def _(x):
                run(x, self.eng["pool"])

            @block.sync
            def _(x):
                run(x, self.eng["sp"])


def build_nc(ntiles=5):
    from contextlib import ExitStack
    nc = bass.Bass("TRN2", target_bir_lowering=False)
    stack = ExitStack()

    def din(name, shape, dt=F32):
        return nc.dram_tensor(name, list(shape), dt, kind="ExternalInput").ap()

    def dout(name, shape, dt=F32):
        return nc.dram_tensor(name, list(shape), dt, kind="ExternalOutput").ap()

    xp = din("xp", [2048, D]); xs = din("xs", [64, D])
    ppT = din("ppT", [256, 2048]); psT = din("psT", [256, 64])
    wall = din("wall", [NSLOT, 128, 4096])
    cvec_d = din("cvec", [128, NCV]); gfin_d = din("gfin_b", [128, D])
    lng_d = din("lng_b", [128, 512]); lnb_d = din("lnb_b", [128, 512])
    ident_d = din("ident", [128, 128]); maskT_d = din("maskT", [128, 128])
    sguT_d = din("sguT", [128, 512]); wabd_d = din("wabd", [128, 512]); wxbd_d = din("wxbd", [128, 512])
    bsrow_d = din("bsrow", [1, 512]); Rs_d = din("Rs", [64, 256]); masks_d = din("mask_s", [64, 64])
    bsrow_s_d = din("bsrow_s", [1, 256])
    sth_d = din("st_hT", [128, 64]); stc_d = din("st_cT", [128, 192]); stf_d = din("st_fT", [128, 1536])

    y_p = dout("y_p", [2048, D]); y_s = dout("y_s", [64, D])
    o_hp = dout("o_hp", [128, 4]); o_hs = dout("o_hs", [128, 64])
    o_cp = dout("o_cp", [128, 12]); o_cs = dout("o_cs", [128, 192])
    o_vs = dout("o_vs", [64, 512]); o_fp = dout("o_fp", [128, 96]); o_fs = dout("o_fs", [128, 1536])
    wbf = nc.dram_tensor("wbf", [NSLOT, 128, 4096], BF16, kind="Internal").ap()

    def sb(name, shape, dt=F32):
        return stack.enter_context(nc.sbuf_tensor(name, list(shape), dt))

    hbuf = sb("hbuf", [128, 2, 4, D])
    nscr = sb("nscr", [128, 2, D])
    junk = sb("junk", [128, D], BF16)
    nT = sb("nT", [128, 8, 512], BF16)
    xa_ext = sb("xa_ext", [128, 4, 520])
    xc = sb("xc", [128, 4, 512]); xcb = sb("xcb", [128, 4, 512], BF16)
    e4 = sb("e4", [128, 4, 512]); ig4 = sb("ig4", [128, 4, 512]); rr4 = sb("rr4", [128, 4, 512])
    shared = sb("shared", [128, 6144])
    vnb = sb("vnb", [128, 4, 512], BF16)
    hs = sb("hs", [128, 2, 512])
    yabT = sb("yabT", [128, 8, 512], BF16)
    corr = sb("corr", [128, 48, 2])
    pTb = sb("pTb", [128, 2, 2, 512], BF16)
    cvec = sb("cvec_s", [128, NCV]); gfin_b = sb("gfin_s", [128, D])
    lng_b = sb("lng_s", [128, 512]); lnb_b = sb("lnb_s", [128, 512])
    ident = sb("ident_s", [128, 128])
    wsgu = sb("wsgu", [128, 4, 128], BF16); wabd = sb("wabd_s", [128, 4, 128], BF16); wxbd = sb("wxbd_s", [128, 4, 128], BF16)
    wsgu_s = sb("wsgu_ss", [128, 4, 64], BF16)
    bsh = sb("bsh", [1, 768], BF16); bsl = sb("bsl", [1, 768], BF16)
    bsf = hbuf[0:1, 1, 0, 0:768]
    ones_bf = sb("ones_bf", [1, 128], BF16); ones_f = sb("ones_f", [128, 2]); ones_b2 = sb("ones_b2", [128, 2], BF16); mhalf = sb("mhalf", [128, 8])
    st_hT = sb("st_hT_s", [128, 4, 16]); st_cT = sb("st_cT_s", [128, 4, 16, 3]); st_fT = sb("st_fT_s", [128, 48, 16, 2])
    fext = sb("fext", [128, 2, 16, 6])
    hsout = sb("hsout", [128, 4, 16]); tmp16 = sb("tmp16", [128, 16]); cs_out = sb("cs_out", [128, 4, 16, 3])
    xhist = sb("xhist", [128, 4, 3]); hlast = sb("hlast", [128, 4]); fhist = sb("fhist", [128, 48, 2])
    cA = sb("cA", [128, 4]); cA2 = sb("cA2", [128, 4]); spt = sb("spt", [128, 4])
    ss = sb("ss", [128, 4]); ms = sb("ms", [128, 4]); rstd = sb("rstd", [128, 4])
    ssn = sb("ssn", [128, 4]); msn = sb("msn", [128, 4]); rstdn = sb("rstdn", [128, 4])
    ms8 = sb("ms8", [128, 8]); rstd8 = sb("rstd8", [128, 8])
    bst = sb("bst", [128, 2, 6]); mv = sb("mv", [128, 2, 2]); rsv = sb("rsv", [128, 2])
    ring = sb("ring", [128, RING, 8, 512], BF16)
    psum = [stack.enter_context(nc.psum_tensor("ps%d" % i, [128, 512], F32)) for i in range(8)]

    actT = shared[:, :].bitcast(BF16).rearrange("p (j t) -> p j t", t=512)
    sh3 = shared[:, :].rearrange("p (a t) -> p a t", t=512)

    S = Sched(nc, stack)
    U = [Buf("U%d" % i) for i in range(24)]
    B_gga = [[U[2 * c], U[2 * c + 1]] for c in range(4)]
    B_gub = [[U[8 + 2 * c], U[9 + 2 * c]] for c in range(4)]
    B_gvb = [[U[16 + 2 * k], U[17 + 2 * k]] for k in range(2)]
    B_rr4 = [Buf("rr%d" % c) for c in range(4)]; B_e4 = [Buf("e4_%d" % c) for c in range(4)]; B_ig4 = [Buf("ig4_%d" % c) for c in range(4)]
    B_act = [[U[j]] for j in range(24)]
    B_h = [[Buf("h%d_%d" % (a, t)) for t in range(4)] for a in range(2)]
    B_nscr = [Buf("nscr0"), Buf("nscr1")]; B_junk = Buf("junk"); B_corr = Buf("corr")
    B_nT = [Buf("nT%d" % k) for k in range(8)]
    B_xa = [Buf("xa%d" % c) for c in range(4)]
    B_xc = [Buf("xc%d" % k) for k in range(4)]; B_xcb = [Buf("xcb%d" % k) for k in range(4)]
    B_vnb = [Buf("vnb%d" % t) for t in range(4)]
    B_hs = [Buf("hs%d" % c) for c in range(2)]
    B_yab = [Buf("yab%d" % c) for c in range(8)]
    B_fc = [[B_e4[2 * s_ + p_] for p_ in range(2)] for s_ in range(2)]
    B_fG = [B_ig4[0], B_ig4[1]]
    B_gt = B_fG; B_pt = [B_fc[0][0], B_fc[1][0]]
    B_pTb = [Buf("pTb%d" % k) for k in range(2)]
    B_ps = [Buf("ps%d" % k) for k in range(8)]
    B_ring = [Buf("ring%d" % k) for k in range(RING)]
    B_wbf = [Buf("wbf%d" % k) for k in range(NSLOT)]
    B_xhist = [Buf("xhist%d" % c) for c in range(4)]
    B_hlast = [Buf("hlast%d" % c) for c in range(4)]
    B_fhist = [Buf("fhist%d" % f) for f in range(48)]
    B_stf = [Buf("stf%d" % f) for f in range(48)]
    B_fext = [Buf("fext%d" % k) for k in range(2)]
    B_hsout = Buf("hsout"); B_tmp16 = Buf("tmp16"); B_csout = Buf("csout")
    B_ssn = [Buf("ssn%d" % t) for t in range(4)]; B_msn = [Buf("msn%d" % t) for t in range(4)]; B_rstdn = [Buf("rstdn%d" % t) for t in range(4)]
    B_ss = Buf("ss"); B_ms = Buf("ms"); B_rstd = Buf("rstd"); B_ms8 = Buf("ms8"); B_rstd8 = Buf("rstd8")
    B_bst = [Buf("bst%d" % k) for k in range(2)]; B_mv = [Buf("mv%d" % k) for k in range(2)]; B_rsv = [Buf("rsv%d" % k) for k in range(2)]
    B_const = Buf("const")

    ch_setup = S.chan("setup")
    setup_loads = [
        (cvec[:, :], cvec_d), (gfin_b[:, :], gfin_d), (lng_b[:, :], lng_d), (lnb_b[:, :], lnb_d),
        (ident[:, :], ident_d), (nscr[:, 0, 0:512], sguT_d), (nscr[:, 0, 512:640], maskT_d),
        (nscr[0:64, 0, 640:896], Rs_d), (nscr[0:64, 0, 896:960], masks_d),
        (bsf[:, 0:512], bsrow_d), (bsf[:, 512:768], bsrow_s_d),
        (st_hT[:, :, :].rearrange("p a b -> p (a b)"), sth_d),
        (st_cT[:, :, :, :].rearrange("p a b c -> p (a b c)"), stc_d),
        (st_fT[:, :, :, :].rearrange("p a b c -> p (a b c)"), stf_d),
    ]
    for o_, i_ in setup_loads:
        S.dma("sp", ch_setup, lambda e, o_=o_, i_=i_: e.dma_start(out=o_, in_=i_), writes=[])
    B_gw = Buf("gatew")
    ch_gw = [S.chan("gw0"), S.chan("gw1")]
    S.dma("pool", ch_gw[0], lambda e: e.dma_start(out=wabd[:, :, :].rearrange("p a b -> p (a b)"), in_=wabd_d), writes=[B_gw])
    tgw = S.dma("pool", ch_gw[1], lambda e: e.dma_start(out=wxbd[:, :, :].rearrange("p a b -> p (a b)"), in_=wxbd_d), writes=[])
    tok_setup = (ch_setup.sem, ch_setup.cnt)
    B_const.w = tok_setup
    B_nscr[0].w = tok_setup
    B_h[1][0].w = tok_setup

    def DV(fn, r=(), w=()):
        return S.op("dve", fn, reads=r, writes=w)

    def AC(fn, r=(), w=()):
        return S.op("act", fn, reads=r, writes=w)

    for h_ in range(4):
        DV(lambda e, h_=h_: e.tensor_tensor(out=wsgu[:, h_, :], in0=nscr[:, 0, h_ * 128:(h_ + 1) * 128], in1=nscr[:, 0, 512:640], op=ALU.mult),
           r=[B_nscr[0]], w=[B_const])
        DV(lambda e, h_=h_: e.tensor_tensor(out=wsgu_s[0:64, h_, :], in0=nscr[0:64, 0, 640 + h_ * 64:640 + (h_ + 1) * 64], in1=nscr[0:64, 0, 896:960], op=ALU.mult),
           r=[B_nscr[0]], w=[B_const])
    DV(lambda e: e.tensor_copy(out=bsh[0:1, :], in_=bsf), r=[B_const, B_h[1][0]], w=[B_const])
    bsg_v = nscr[0:1, 1, 0:768]
    DV(lambda e: e.tensor_copy(out=bsg_v, in_=bsh[0:1, :]), r=[B_const], w=[B_const, B_nscr[1]])
    DV(lambda e: e.tensor_tensor(out=bsl[0:1, :], in0=bsf, in1=bsg_v, op=ALU.subtract), r=[B_const, B_nscr[1], B_h[1][0]], w=[B_const])
    DV(lambda e: e.memset(ones_bf[0:1, :], 1.0), w=[B_const])
    DV(lambda e: e.memset(ones_f[:, :], 1.0), w=[B_const])
    DV(lambda e: e.memset(ones_b2[:, :], 1.0), w=[B_const])
    DV(lambda e: e.memset(mhalf[:, :], -0.5), w=[B_const])
    DV(lambda e: e.memset(xhist[:, :, :], 0.0), w=B_xhist)
    DV(lambda e: e.memset(hlast[:, :], 0.0), w=B_hlast)
    DV(lambda e: e.memset(fhist[:, :, :], 0.0), w=B_fhist)
    AC(lambda e: e.activation(out=spt[:, :], in_=cvec[:, C_AP:C_AP + 4], func=AF.Exp, scale=-1.0), r=[B_const], w=[B_const])
    AC(lambda e: e.activation(out=spt[:, :], in_=spt[:, :], func=AF.Ln, bias=1.0), r=[B_const], w=[B_const])
    DV(lambda e: e.tensor_scalar(out=cA[:, :], in0=spt[:, :], scalar1=-8.0, scalar2=None, op0=ALU.mult), r=[B_const], w=[B_const])
    DV(lambda e: e.tensor_scalar(out=cA2[:, :], in0=spt[:, :], scalar1=-16.0, scalar2=None, op0=ALU.mult), r=[B_const], w=[B_const])

    ch_wbf = [S.chan("wbf%d" % s_) for s_ in range(NSLOT)]
    ch_ring = [S.chan("ring%d" % k) for k in range(RING)]
    total_loads = NSLOT * ntiles
    st = {"next_load": 0, "released": [False] * total_loads, "prefix": 0, "cast": 0, "bank": 0}

    def emit_cast(s_):
        src = wall[s_, :, :].rearrange("p (a b) -> p a b", b=2048)
        dst = wbf[s_, :, :].rearrange("p (a b) -> p a b", b=2048)
        S.dma("pool", ch_wbf[s_], lambda e: e.dma_start(out=dst, in_=src), writes=[B_wbf[s_]])

    NFAST = RING
    ch_st = [S.chan("st%d" % k) for k in range(NFAST)]

    def pump():
        while st["next_load"] < total_loads and st["next_load"] - RING < st["prefix"]:
            g = st["next_load"]
            s_ = g % NSLOT
            k = g % RING
            if g < NFAST:
                src = wall[s_, :, :].rearrange("p (a b) -> p a b", b=2048)
                dst = ring[:, k, :, :].rearrange("p a b -> p (a b)").rearrange("p (a b) -> p a b", b=2048)
                S.dma("pool", ch_wbf[s_], lambda e, src=src, dst=dst: e.dma_start(out=dst, in_=src), writes=[B_ring[k]])
                S.dma("sp", ch_st[g], lambda e, s_=s_, k=k: e.dma_start(out=wbf[s_, :, :], in_=ring[:, k, :, :].rearrange("p a b -> p (a b)")),
                      reads=[B_ring[k]], writes=[B_wbf[s_]])
            else:
                S.dma("sp", ch_ring[k],
                      lambda e, s_=s_, k=k: e.dma_start(out=ring[:, k, :, :].rearrange("p a b -> p (a b)"), in_=wbf[s_, :, :]),
                      reads=[B_wbf[s_]], writes=[B_ring[k]])
            st["next_load"] += 1

    def wget(g):
        pump()
        assert g < st["next_load"], "weight slot %d not loaded (ring deadlock)" % g
        return g % RING

    def wrel(g):
        st["released"][g] = True
        while st["prefix"] < total_loads and st["released"][st["prefix"]]:
            st["prefix"] += 1
        pump()

    reserved = set()

    def nbank():
        b = st["bank"]
        while b in reserved:
            b = (b + 1) % 8
        st["bank"] = (b + 1) % 8
        return b

    def mm_group(bank, out_ap, pairs, reads, first_start=True, skip=False):
        n = len(pairs)
        for i_, (l_, r_) in enumerate(pairs):
            S.op("pe", lambda e, l_=l_, r_=r_, i_=i_: e.matmul(out_ap, l_, r_, start=(first_start and i_ == 0), stop=(i_ == n - 1),
                                                         skip_group_check=skip),
                 reads=reads, writes=[B_ps[bank]], inc=(i_ == n - 1))

    ch_x = [S.chan("x%d" % k) for k in range(2)]
    ch_p = [S.chan("p%d" % k) for k in range(2)]
    ch_y = [S.chan("y%d" % k) for k in range(2)]
    ch_out = S.chan("out")
    out_toks = []
    pool_toks = []

    def tile_info(i):
        if i < 4:
            return dict(T=512, NTS=4, TP=128, nb=1, L=512, sample=False)
        return dict(T=64, NTS=1, TP=64, nb=16, L=4, sample=True)

    def emit_xload(i):
        hb = i % 2
        ti = tile_info(i)
        if not ti["sample"]:
            src = xp[i * 512:(i + 1) * 512, :].rearrange("(t p) f -> p t f", p=128)
            S.dma("sp", ch_x[hb], lambda e: e.dma_start(out=hbuf[:, hb, :, :], in_=src), writes=B_h[hb])
            psrc = ppT[:, i * 512:(i + 1) * 512].rearrange("(k p) t -> p k t", p=128)
            S.dma("pool", ch_p[hb], lambda e: e.dma_start(out=pTb[:, hb, :, :], in_=psrc), writes=[B_pTb[hb]])
        else:
            S.dma("sp", ch_x[hb], lambda e: e.dma_start(out=hbuf[0:64, hb, 0, :], in_=xs[:, :]), writes=[B_h[hb][0]])
            psrc = psT[:, :].rearrange("(k p) t -> p k t", p=128)
            S.dma("pool", ch_p[hb], lambda e: e.dma_start(out=pTb[:, hb, :, 0:64], in_=psrc), writes=[B_pTb[hb]])

    def rstd_chain(src_ap, ms_ap, rstd_ap, scale, Bsrc, Bms, Brstd):
        DV(lambda e: e.tensor_scalar(out=ms_ap, in0=src_ap, scalar1=scale, scalar2=EPS, op0=ALU.mult, op1=ALU.add), r=Bsrc, w=[Bms])
        npart, ncol = ms_ap.shape[0], ms_ap.shape[1]
        S.op("pool", lambda e: e.tensor_tensor(out=rstd_ap, in0=ms_ap, in1=mhalf[:npart, 0:ncol], op=ALU.pow), reads=[Bms, B_const], writes=[Brstd])

    def norm(hb, gcol, ti, phase="all"):
        T, NTS, TP = ti["T"], ti["NTS"], ti["TP"]
        banks = list(range(8))
        NPRE = min(2, NTS)

        def stat(ts):
            AC(lambda e: e.activation(out=junk[:TP, :], in_=hbuf[:TP, hb, ts, :], func=AF.Square, accum_out=ssn[:TP, ts:ts + 1]),
               r=[B_h[hb][ts]], w=[B_junk, B_ssn[ts]])
            DV(lambda e: e.tensor_scalar(out=msn[:TP, ts:ts + 1], in0=ssn[:TP, ts:ts + 1], scalar1=1.0 / D, scalar2=EPS, op0=ALU.mult, op1=ALU.add),
               r=[B_ssn[ts]], w=[B_msn[ts]])
            S.op("pool", lambda e: e.tensor_tensor(out=rstdn[:TP, ts:ts + 1], in0=msn[:TP, ts:ts + 1], in1=mhalf[:TP, 0:1], op=ALU.pow),
                 reads=[B_msn[ts], B_const], writes=[B_rstdn[ts]])

        def scale(ts):
            nk = ts % 2
            DV(lambda e: e.tensor_scalar(out=nscr[:TP, nk, :], in0=hbuf[:TP, hb, ts, :], scalar1=rstdn[:TP, ts:ts + 1], scalar2=None, op0=ALU.mult),
               r=[B_h[hb][ts], B_rstdn[ts]], w=[B_nscr[nk]])

        def transp(ts):
            nk = ts % 2
            for kt in range(8):
                S.op("pe", lambda e, kt=kt: e.transpose(out=psum[banks[kt]][:, ts * 128:ts * 128 + TP], in_=nscr[:TP, nk, kt * 128:(kt + 1) * 128],
                                                        identity=ident[:TP, :TP]),
                     reads=[B_nscr[nk], B_const], writes=[B_ps[banks[kt]]], inc=(kt == 7))

        if phase == "pre":
            for ts in range(NTS):
                stat(ts)
            for ts in range(NPRE):
                scale(ts)
            return
        if phase == "post":
            for ts in range(NPRE):
                transp(ts)
            for ts in range(NPRE, NTS):
                scale(ts)
                transp(ts)
        else:
            order = []
            for ts in range(NTS):
                order.append(("stat", ts))
                if ts >= 1:
                    order.append(("scale", ts - 1))
            order.append(("scale", NTS - 1))
            for kind, ts in order:
                if kind == "stat":
                    stat(ts)
                else:
                    scale(ts)
                    transp(ts)
        for kt in range(8):
            if kt % 2 == 0:
                AC(lambda e, kt=kt: e.activation(out=nT[:, kt, 0:T], in_=psum[banks[kt]][:, 0:T], func=AF.Identity, scale=cvec[:, gcol + kt:gcol + kt + 1]),
                   r=[B_ps[banks[kt]], B_const], w=[B_nT[kt]])
            else:
                DV(lambda e, kt=kt: e.tensor_scalar(out=nT[:, kt, 0:T], in0=psum[banks[kt]][:, 0:T], scalar1=cvec[:, gcol + kt:gcol + kt + 1], scalar2=None, op0=ALU.mult),
                   r=[B_ps[banks[kt]], B_const], w=[B_nT[kt]])
        st["bank"] = 0

    def outdma(dst, src, reads):
        pool_toks.append(S.dma("pool", S.chan("o%d" % len(pool_toks)), lambda e: e.dma_start(out=dst, in_=src), reads=reads))

    emit_xload(0)
    S.wait("pool", tok_setup)
    S.wait("pool", (ch_x[0].sem, ch_x[0].cnt))
    pump()
    for s_ in range(NFAST, 8):
        emit_cast(s_)
    for i in range(ntiles):
        ti = tile_info(i)
        T, NTS, TP, nb, L, sample = ti["T"], ti["NTS"], ti["TP"], ti["nb"], ti["L"], ti["sample"]
        hb = i % 2
        g0 = i * NSLOT
        def v3(ap, l=L):
            return ap.rearrange("p (b l) -> p b l", l=l) if sample else ap

        if i == 0:
            norm(hb, C_GMIX, ti)
            for s_ in range(8, NSLOT):
                emit_cast(s_)

        EXT = 3 + L
        def phaseA1():
            for ct in range(4):
                xav = xa_ext[:, ct, 0:nb * EXT].rearrange("p (b l) -> p b l", l=EXT)
                xcv = xc[:, ct, 0:T].rearrange("p (b l) -> p b l", l=L)
                cw = C_CAW + ct * 4
                DV(lambda e, xav=xav, xcv=xcv, cw=cw, ct=ct: e.tensor_scalar(out=xcv, in0=xav[:, :, 0:L], scalar1=cvec[:, cw:cw + 1], scalar2=cvec[:, C_CAB + ct:C_CAB + ct + 1],
                                                                        op0=ALU.mult, op1=ALU.add), r=[B_xa[ct], B_const], w=[B_xc[ct]])
                for j in range(1, 4):
                    DV(lambda e, xav=xav, xcv=xcv, cw=cw, j=j: e.scalar_tensor_tensor(out=xcv, in0=xav[:, :, j:j + L], scalar=cvec[:, cw + j:cw + j + 1], in1=xcv,
                                                                                 op0=ALU.mult, op1=ALU.add), r=[B_xa[ct], B_const, B_xc[ct]], w=[B_xc[ct]])
                DV(lambda e, ct=ct: e.tensor_copy(out=xcb[:, ct, 0:T], in_=xc[:, ct, 0:T]), r=[B_xc[ct]], w=[B_xcb[ct]])


        sbank = nbank()
        reserved.add(sbank)
        first_stat = [True]

        def stats_mm(sq_ap, Bsq, base):
            for ts in range(NTS):
                fs = first_stat[0]
                first_stat[0] = False
                S.op("pe", lambda e, ts=ts, fs=fs: e.matmul(psum[sbank][:TP, base + 2 * ts:base + 2 * ts + 2], sq_ap[:, ts * 128:ts * 128 + TP], ones_b2[:, 0:2],
                                                        start=fs, stop=True, skip_group_check=True),
                     reads=[Bsq, B_const], writes=[B_ps[sbank]], inc=(ts == NTS - 1))

        def gatesA(ct):
            b1 = nbank()
            S.wait("pe", tgw)
            mm_group(b1, psum[b1][:, 0:T], [(wabd[:, ct, :], xcb[:, ct, 0:T])], reads=[B_const, B_gw, B_xcb[ct]])
            b2 = nbank()
            mm_group(b2, psum[b2][:, 0:T], [(wxbd[:, ct, :], xcb[:, ct, 0:T])], reads=[B_const, B_xcb[ct]])
            return b1, b2

        def sigA(ct, b1, b2):
            AC(lambda e: e.activation(out=rr4[:, ct, 0:T], in_=psum[b1][:, 0:T], func=AF.Sigmoid, bias=cvec[:, C_BA + ct:C_BA + ct + 1]),
               r=[B_ps[b1], B_const], w=[B_rr4[ct]])
            AC(lambda e: e.activation(out=ig4[:, ct, 0:T], in_=psum[b2][:, 0:T], func=AF.Sigmoid, bias=cvec[:, C_BX + ct:C_BX + ct + 1]),
               r=[B_ps[b2], B_const], w=[B_ig4[ct]])

        def expA(ct):
            AC(lambda e: e.activation(out=e4[:, ct, 0:T], in_=rr4[:, ct, 0:T], func=AF.Exp, scale=cA2[:, ct:ct + 1]), r=[B_rr4[ct], B_const], w=[B_e4[ct]])
            AC(lambda e: e.activation(out=rr4[:, ct, 0:T], in_=rr4[:, ct, 0:T], func=AF.Exp, scale=cA[:, ct:ct + 1]), r=[B_rr4[ct], B_const], w=[B_rr4[ct]])

        def sqrtA(ct):
            AC(lambda e: e.activation(out=e4[:, ct, 0:T], in_=e4[:, ct, 0:T], func=AF.Sqrt, scale=-1.0, bias=1.0), r=[B_e4[ct]], w=[B_e4[ct]])

        def postA(ct):
            hk = ct % 2
            rr = rr4[:, ct, 0:T]
            ee = e4[:, ct, 0:T]
            ig = ig4[:, ct, 0:T]
            xcc = xc[:, ct, 0:T]
            if i == 0:
                DV(lambda e: e.memset(e4[:, ct, 0:1], 1.0), w=[B_e4[ct]])
            DV(lambda e: e.tensor_tensor(out=ig, in0=ig, in1=ee, op=ALU.mult), r=[B_ig4[ct], B_e4[ct]], w=[B_ig4[ct]])
            DV(lambda e: e.tensor_tensor(out=xcc, in0=xcc, in1=ig, op=ALU.mult), r=[B_ig4[ct], B_xc[ct]], w=[B_xc[ct]])
            if sample:
                rv = rr.rearrange("p (b l) -> p b l", l=L)
                uv = xcc.rearrange("p (b l) -> p b l", l=L)
                DV(lambda e: e.tensor_tensor(out=tmp16[:, :], in0=rv[:, :, 0], in1=st_hT[:, ct, :], op=ALU.mult), r=[B_rr4[ct], B_const], w=[B_tmp16])
                DV(lambda e: e.tensor_tensor(out=uv[:, :, 0], in0=uv[:, :, 0], in1=tmp16[:, :], op=ALU.add), r=[B_tmp16, B_xc[ct]], w=[B_xc[ct]])
                DV(lambda e: e.memset(rv[:, :, 0], 0.0), w=[B_rr4[ct]])
                DV(lambda e: e.tensor_tensor_scan(out=hs[:, hk, 0:T], data0=rr, data1=xcc, initial=0.0, op0=ALU.mult, op1=ALU.add),
                   r=[B_rr4[ct], B_xc[ct]], w=[B_hs[hk]])
                DV(lambda e: e.tensor_copy(out=hsout[:, ct, :], in_=hs[:, hk, 0:T].rearrange("p (b l) -> p b l", l=L)[:, :, L - 1]), r=[B_hs[hk]], w=[B_hsout])
            else:
                DV(lambda e: e.tensor_tensor_scan(out=hs[:, hk, 0:T], data0=rr, data1=xcc, initial=hlast[:, ct:ct + 1], op0=ALU.mult, op1=ALU.add),
                   r=[B_rr4[ct], B_xc[ct], B_hlast[ct]], w=[B_hs[hk]])
                DV(lambda e: e.tensor_copy(out=hlast[:, ct:ct + 1], in_=hs[:, hk, L - 1:L]), r=[B_hs[hk]], w=[B_hlast[ct]])
            DV(lambda e: e.tensor_tensor(out=ee, in0=hs[:, hk, 0:T], in1=sh3[:, ct, 0:T], op=ALU.mult), r=[B_hs[hk]] + B_gga[ct], w=[B_e4[ct]])

        def postA_act(ct):
            ee = e4[:, ct, 0:T]
            sqv = ig4[:, ct, :].bitcast(BF16)[:, 0:T]
            AC(lambda e: e.activation(out=sqv, in_=ee, func=AF.Square), r=[B_e4[ct]], w=[B_ig4[ct]])
            stats_mm(sqv, B_ig4[ct], 0)
            AC(lambda e: e.activation(out=yabT[:, ct, 0:T], in_=ee, func=AF.Identity, scale=cvec[:, C_GOA + ct:C_GOA + ct + 1]),
               r=[B_e4[ct], B_const], w=[B_yab[ct]])

        for c in range(3):
            k = wget(g0 + c)
            for ct in range(4):
                bank = nbank()
                mm_group(bank, psum[bank][:, 0:T], [(ring[:, k, kt, ct * 128:(ct + 1) * 128], nT[:, kt, 0:T]) for kt in range(8)],
                         reads=[B_ring[k]] + B_nT)
                if c == 0:
                    xav = xa_ext[:, ct, 0:nb * EXT].rearrange("p (b l) -> p b l", l=EXT)
                    if sample:
                        DV(lambda e, xav=xav, ct=ct: e.tensor_copy(out=xav[:, :, 0:3], in_=st_cT[:, ct, :, :]), r=[B_const], w=[B_xa[ct]])
                    else:
                        DV(lambda e, xav=xav, ct=ct: e.tensor_copy(out=xav[:, :, 0:3], in_=xhist[:, ct:ct + 1, :]), r=[B_xhist[ct]], w=[B_xa[ct]])
                    DV(lambda e, xav=xav, bank=bank: e.tensor_copy(out=xav[:, :, 3:3 + L], in_=psum[bank][:, 0:T].rearrange("p (b l) -> p b l", l=L)),
                       r=[B_ps[bank]], w=[B_xa[ct]])
                    if not sample:
                        DV(lambda e, xav=xav, ct=ct: e.tensor_copy(out=xhist[:, ct:ct + 1, :], in_=xav[:, :, L:L + 3]), r=[B_xa[ct]], w=[B_xhist[ct]])
                    else:
                        DV(lambda e, xav=xav, ct=ct: e.tensor_copy(out=cs_out[:, ct, :, :], in_=xav[:, :, 4:7]), r=[B_xa[ct]], w=[B_csout])
                        if ct == 3:
                            outdma(o_cs[:, :], cs_out[:, :, :, :].rearrange("p a b c -> p (a b c)"), [B_csout])
                elif c == 1:
                    AC(lambda e, bank=bank, ct=ct: e.activation(out=sh3[:, ct, 0:T], in_=psum[bank][:, 0:T], func=AF.Gelu_apprx_tanh),
                       r=[B_ps[bank]], w=B_gga[ct])
                else:
                    AC(lambda e, bank=bank, ct=ct: e.activation(out=sh3[:, 4 + ct, 0:T], in_=psum[bank][:, 0:T], func=AF.Gelu_apprx_tanh),
                       r=[B_ps[bank]], w=B_gub[ct])
            wrel(g0 + c)
            if c == 0:
                phaseA1()
        k = wget(g0 + 3)

        def vb_mm(ts):
            bank = nbank()
            kk = ts % 2
            mm_group(bank, psum[bank][:TP, :], [(nT[:, kt, ts * 128:ts * 128 + TP], ring[:, k, kt, :]) for kt in range(8)],
                     reads=[B_ring[k]] + B_nT)
            gv = sh3[:TP, 8 + kk, :]
            AC(lambda e, bank=bank, gv=gv: e.activation(out=gv, in_=psum[bank][:TP, :], func=AF.Gelu_apprx_tanh), r=[B_ps[bank]], w=B_gvb[kk])

        def vb_ln(ts):
            kk = ts % 2
            gv = sh3[:TP, 8 + kk, :]
            DV(lambda e, gv=gv, kk=kk: e.bn_stats(out=bst[:TP, kk, :], in_=gv), r=B_gvb[kk], w=[B_bst[kk]])
            DV(lambda e, kk=kk: e.bn_aggr(out=mv[:TP, kk, :], in_=bst[:TP, kk, :]), r=[B_bst[kk]], w=[B_mv[kk]])
            DV(lambda e, kk=kk: e.tensor_scalar(out=rsv[:TP, kk:kk + 1], in0=mv[:TP, kk, 1:2], scalar1=1.0, scalar2=EPS, op0=ALU.mult, op1=ALU.add),
               r=[B_mv[kk]], w=[B_rsv[kk]])
            S.op("pool", lambda e, kk=kk: e.tensor_tensor(out=rsv[:TP, kk:kk + 1], in0=rsv[:TP, kk:kk + 1], in1=mhalf[:TP, 0:1], op=ALU.pow),
                 reads=[B_rsv[kk], B_const], writes=[B_rsv[kk]])
            DV(lambda e, gv=gv, kk=kk: e.tensor_scalar(out=gv, in0=gv, scalar1=mv[:TP, kk, 0:1], scalar2=rsv[:TP, kk:kk + 1], op0=ALU.subtract, op1=ALU.mult),
               r=B_gvb[kk] + [B_mv[kk], B_rsv[kk]], w=B_gvb[kk])
            S.op("pool", lambda e, gv=gv: e.tensor_tensor(out=gv, in0=gv, in1=lng_b[:TP, :], op=ALU.mult), reads=B_gvb[kk] + [B_const], writes=B_gvb[kk])
            if not sample:
                S.op("pool", lambda e, gv=gv, ts=ts: e.tensor_tensor(out=vnb[:TP, ts, :], in0=gv, in1=lnb_b[:TP, :], op=ALU.add), reads=B_gvb[kk] + [B_const], writes=[B_vnb[ts]])
            else:
                S.op("pool", lambda e, gv=gv: e.tensor_tensor(out=gv, in0=gv, in1=lnb_b[:TP, :], op=ALU.add), reads=B_gvb[kk] + [B_const], writes=B_gvb[kk])
                AC(lambda e, gv=gv, ts=ts: e.activation(out=vnb[:TP, ts, :], in_=gv, func=AF.Copy), r=B_gvb[kk], w=[B_vnb[ts]])
                pool_toks.append(S.dma("pool", S.chan("ovs"), lambda e, gv=gv: e.dma_start(out=o_vs[:, :], in_=gv), reads=B_gvb[kk]))
        for ts in range(min(2, NTS)):
            vb_mm(ts)
            vb_ln(ts)
        for ct in range(4):
            sigA(ct, *gatesA(ct))
        for ts in range(2, NTS):
            vb_mm(ts)
            vb_ln(ts)
        wrel(g0 + 3)
        for ct in range(4):
            expA(ct)
        for ct in range(4):
            sqrtA(ct)
        for ct in range(2):
            postA(ct)
            postA_act(ct)
        sgu_banks = []
        for h_ in range(4):
            k2 = h_ % 2
            bank = nbank()
            sgu_banks.append(bank)
            if not sample:
                nmm = NTS * 3
                idx = 0
                for ts in range(NTS):
                    oc = psum[bank][:, ts * 128:(ts + 1) * 128]
                    trip = [(vnb[:, ts, h_ * 128:(h_ + 1) * 128], wsgu[:, h_, :]),
                            (ones_bf[0:1, :], bsh[0:1, h_ * 128:(h_ + 1) * 128]),
                            (ones_bf[0:1, :], bsl[0:1, h_ * 128:(h_ + 1) * 128])]
                    for l_, r_ in trip:
                        S.op("pe", lambda e, oc=oc, l_=l_, r_=r_, idx=idx: e.matmul(oc, l_, r_, start=(idx == 0), stop=(idx == nmm - 1), skip_group_check=True),
                             reads=[B_vnb[ts], B_const], writes=[B_ps[bank]], inc=(idx == nmm - 1))
                        idx += 1
            else:
                oc = psum[bank][:, 0:64]
                trip = [(vnb[0:64, 0, h_ * 128:(h_ + 1) * 128], wsgu_s[0:64, h_, :]),
                        (ones_bf[0:1, :], bsh[0:1, 512 + h_ * 64:512 + (h_ + 1) * 64]),
                        (ones_bf[0:1, :], bsl[0:1, 512 + h_ * 64:512 + (h_ + 1) * 64])]
                for idx, (l_, r_) in enumerate(trip):
                    S.op("pe", lambda e, oc=oc, l_=l_, r_=r_, idx=idx: e.matmul(oc, l_, r_, start=(idx == 0), stop=(idx == 2), skip_group_check=True),
                         reads=[B_vnb[0], B_const], writes=[B_ps[bank]], inc=(idx == 2))
        for ct in range(2, 4):
            postA(ct)
            postA_act(ct)
        if sample:
            outdma(o_hs[:, :], hsout[:, :, :].rearrange("p a b -> p (a b)"), [B_hsout])
        if i == 3:
            outdma(o_hp[:, :], hlast[:, :], B_hlast)
            outdma(o_cp[:, :], xhist[:, :, :].rearrange("p a b -> p (a b)"), B_xhist)

        for h_ in range(4):
            k2 = h_
            bank = sgu_banks[h_]
            ee = e4[:, k2, 0:T]
            ig = ig4[:, k2, 0:T]
            DV(lambda e, ee=ee, bank=bank, h_=h_: e.tensor_tensor(out=ee, in0=psum[bank][:, 0:T], in1=sh3[:, 4 + h_, 0:T], op=ALU.mult),
               r=[B_ps[bank]] + B_gub[h_], w=[B_e4[k2]])
            sqv = ig4[:, k2, :].bitcast(BF16)[:, 0:T]
            AC(lambda e, ee=ee, sqv=sqv: e.activation(out=sqv, in_=ee, func=AF.Square), r=[B_e4[k2]], w=[B_ig4[k2]])
            stats_mm(sqv, B_ig4[k2], 8)
            AC(lambda e, ee=ee, h_=h_: e.activation(out=yabT[:, 4 + h_, 0:T], in_=ee, func=AF.Identity, scale=cvec[:, C_GOB + h_:C_GOB + h_ + 1]),
               r=[B_e4[k2], B_const], w=[B_yab[4 + h_]])
        src8 = psum[sbank][:TP, 0:16].rearrange("p (h t two) -> p h t two", h=2, two=2)[:, :, 0:NTS, 0]
        ms8v = ms8[:TP, :].rearrange("p (h t) -> p h t", h=2)[:, :, 0:NTS]
        rstd8v = rstd8[:TP, :].rearrange("p (h t) -> p h t", h=2)[:, :, 0:NTS]
        mh8v = mhalf[:TP, :].rearrange("p (h t) -> p h t", h=2)[:, :, 0:NTS]
        DV(lambda e: e.tensor_scalar(out=ms8v, in0=src8, scalar1=1.0 / 512, scalar2=EPS, op0=ALU.mult, op1=ALU.add), r=[B_ps[sbank]], w=[B_ms8])
        S.op("pool", lambda e: e.tensor_tensor(out=rstd8v, in0=ms8v, in1=mh8v, op=ALU.pow), reads=[B_ms8, B_const], writes=[B_rstd8])
        reserved.discard(sbank)

        for half_ in range(2):
            k = wget(g0 + 4 + half_)
            for ts in range(NTS):
                for grp in range(2):
                    bank = nbank()
                    mm_group(bank, psum[bank][:TP, :], [(yabT[:, ct, ts * 128:ts * 128 + TP], ring[:, k, ct, :]) for ct in range(grp * 4, grp * 4 + 4)],
                             reads=[B_ring[k]] + B_yab)
                    hsl = hbuf[:TP, hb, ts, half_ * 512:(half_ + 1) * 512]
                    DV(lambda e, bank=bank, hsl=hsl, ts=ts, grp=grp: e.scalar_tensor_tensor(out=hsl, in0=psum[bank][:TP, :], scalar=rstd8[:TP, grp * 4 + ts:grp * 4 + ts + 1],
                                                                                       in1=hsl, op0=ALU.mult, op1=ALU.add),
                       r=[B_ps[bank], B_rstd8, B_h[hb][ts]], w=[B_h[hb][ts]])
            wrel(g0 + 4 + half_)

        norm(hb, C_GFFN, ti)
        if not sample:
            fwv = cvec[:, C_FW:C_FW + 144].rearrange("p (f j) -> p f j", j=3)
            DV(lambda e: e.tensor_tensor(out=corr[:, :, 0], in0=fhist[:, :, 0], in1=fwv[:, :, 0], op=ALU.mult), r=B_fhist + [B_const], w=[B_corr])
            DV(lambda e: e.tensor_tensor(out=corr[:, :, 1], in0=fhist[:, :, 1], in1=fwv[:, :, 1], op=ALU.mult), r=B_fhist + [B_const], w=[B_corr])
            DV(lambda e: e.tensor_tensor(out=corr[:, :, 0], in0=corr[:, :, 0], in1=corr[:, :, 1], op=ALU.add), r=[B_corr], w=[B_corr])
            DV(lambda e: e.tensor_tensor(out=corr[:, :, 1], in0=fhist[:, :, 1], in1=fwv[:, :, 0], op=ALU.mult), r=B_fhist + [B_const], w=[B_corr])
        for hg in range(6):
            kg = wget(g0 + 6 + 2 * hg)
            kl = wget(g0 + 7 + 2 * hg)
            for jj in range(4):
                j = hg * 4 + jj
                s2 = j % 2
                for part, kw in ((0, kg), (1, kl)):
                    ft = part * 24 + j
                    bank = nbank()
                    mm_group(bank, psum[bank][:, 0:T], [(ring[:, kw, kt, jj * 128:(jj + 1) * 128], nT[:, kt, 0:T]) for kt in range(8)],
                             reads=[B_ring[kw]] + B_nT)
                    cc = e4[:, 2 * s2 + part, :]
                    Bc = B_fc[s2][part]
                    w0 = cvec[:, C_FW + ft * 3:C_FW + ft * 3 + 1]
                    w1 = cvec[:, C_FW + ft * 3 + 1:C_FW + ft * 3 + 2]
                    w2 = cvec[:, C_FW + ft * 3 + 2:C_FW + ft * 3 + 3]
                    bb = cvec[:, C_FB + ft:C_FB + ft + 1]
                    pb = psum[bank]
                    if not sample:
                        AC(lambda e, cc=cc, pb=pb, w2=w2, bb=bb: e.activation(out=cc[:, 0:L], in_=pb[:, 0:L], func=AF.Identity, scale=w2, bias=bb),
                           r=[B_ps[bank], B_const], w=[Bc])
                        DV(lambda e, cc=cc, pb=pb, w1=w1: e.scalar_tensor_tensor(out=cc[:, 1:L], in0=pb[:, 0:L - 1], scalar=w1, in1=cc[:, 1:L], op0=ALU.mult, op1=ALU.add),
                           r=[B_ps[bank], B_const, Bc], w=[Bc])
                        DV(lambda e, cc=cc, pb=pb, w0=w0: e.scalar_tensor_tensor(out=cc[:, 2:L], in0=pb[:, 0:L - 2], scalar=w0, in1=cc[:, 2:L], op0=ALU.mult, op1=ALU.add),
                           r=[B_ps[bank], B_const, Bc], w=[Bc])
                        DV(lambda e, cc=cc, ft=ft: e.tensor_tensor(out=cc[:, 0:2], in0=cc[:, 0:2], in1=corr[:, ft, :], op=ALU.add),
                           r=[B_corr, Bc], w=[Bc])
                        AC(lambda e, pb=pb, ft=ft: e.activation(out=fhist[:, ft, :], in_=pb[:, L - 2:L], func=AF.Copy), r=[B_ps[bank]], w=[B_fhist[ft]])
                    else:
                        fx = fext[:, part, :, :]
                        Bx = B_fext[part]
                        ccv = cc[:, 0:T].rearrange("p (b l) -> p b l", l=L)
                        AC(lambda e, fx=fx, ft=ft: e.activation(out=fx[:, :, 0:2], in_=st_fT[:, ft, :, :], func=AF.Copy), r=[B_stf[ft], B_const], w=[Bx])
                        AC(lambda e, fx=fx, pb=pb: e.activation(out=fx[:, :, 2:6], in_=pb[:, 0:T].rearrange("p (b l) -> p b l", l=L), func=AF.Copy),
                           r=[B_ps[bank]], w=[Bx])
                        DV(lambda e, fx=fx, ccv=ccv, w0=w0, bb=bb: e.tensor_scalar(out=ccv, in0=fx[:, :, 0:4], scalar1=w0, scalar2=bb, op0=ALU.mult, op1=ALU.add),
                           r=[Bx, B_const], w=[Bc])
                        DV(lambda e, fx=fx, ccv=ccv, w1=w1: e.scalar_tensor_tensor(out=ccv, in0=fx[:, :, 1:5], scalar=w1, in1=ccv, op0=ALU.mult, op1=ALU.add),
                           r=[Bx, B_const, Bc], w=[Bc])
                        DV(lambda e, fx=fx, ccv=ccv, w2=w2: e.scalar_tensor_tensor(out=ccv, in0=fx[:, :, 2:6], scalar=w2, in1=ccv, op0=ALU.mult, op1=ALU.add),
                           r=[Bx, B_const, Bc], w=[Bc])
                        AC(lambda e, fx=fx, ft=ft: e.activation(out=st_fT[:, ft, :, :], in_=fx[:, :, 4:6], func=AF.Copy), r=[Bx], w=[B_stf[ft]])
                AC(lambda e, s2=s2: e.activation(out=ig4[:, s2, 0:T], in_=e4[:, 2 * s2, 0:T], func=AF.Gelu_apprx_tanh), r=[B_fc[s2][0]], w=[B_fG[s2]])
                S.op("pool", lambda e, s2=s2, j=j: e.tensor_tensor(out=actT[:, j, 0:T], in0=ig4[:, s2, 0:T], in1=e4[:, 2 * s2 + 1, 0:T], op=ALU.mult),
                     reads=[B_fG[s2], B_fc[s2][1]], writes=B_act[j])
            wrel(g0 + 6 + 2 * hg)
            wrel(g0 + 7 + 2 * hg)
        if i + 1 < ntiles:
            emit_xload(i + 1)
        if i == 3:
            outdma(o_fp[:, :], fhist[:, :, :].rearrange("p a b -> p (a b)"), B_fhist)
        if sample:
            outdma(o_fs[:, :], st_fT[:, :, :, :].rearrange("p a b c -> p (a b c)"), B_stf)
        for half_ in range(2):
            banks = [nbank() for _ in range(NTS)]
            for jg in range(3):
                gslot = g0 + 18 + half_ * 3 + jg
                k = wget(gslot)
                for ts in range(NTS):
                    for j8 in range(8):
                        j = jg * 8 + j8
                        S.op("pe", lambda e, ts=ts, j=j, j8=j8, jg=jg, k=k: e.matmul(psum[banks[ts]][:TP, :], actT[:, j, ts * 128:ts * 128 + TP], ring[:, k, j8, :],
                                                                                 start=(jg == 0 and j8 == 0), stop=(jg == 2 and j8 == 7), skip_group_check=True),
                             reads=[B_ring[k]] + B_act[j], writes=[B_ps[banks[ts]]], inc=(j8 == 7))
                wrel(gslot)
            for ts in range(NTS):
                hsl = hbuf[:TP, hb, ts, half_ * 512:(half_ + 1) * 512]
                DV(lambda e, ts=ts, hsl=hsl: e.tensor_tensor(out=hsl, in0=psum[banks[ts]][:TP, :], in1=hsl, op=ALU.add),
                   r=[B_ps[banks[ts]], B_h[hb][ts]], w=[B_h[hb][ts]])

        norm(hb, C_GPLE, ti)
        if i + 1 < ntiles:
            norm((i + 1) % 2, C_GMIX, tile_info(i + 1), phase="pre")
        kp = wget(g0 + 24)
        for half_ in range(2):
            kgt = wget(g0 + 25 + half_)
            for ts in range(NTS):
                s2 = ts % 2
                bg = nbank()
                mm_group(bg, psum[bg][:TP, :], [(nT[:, kt, ts * 128:ts * 128 + TP], ring[:, kgt, kt, :]) for kt in range(8)], reads=[B_ring[kgt]] + B_nT)
                bp = nbank()
                mm_group(bp, psum[bp][:TP, :], [(pTb[:, hb, kt, ts * 128:ts * 128 + TP], ring[:, kp, kt * 2 + half_, :]) for kt in range(2)],
                         reads=[B_ring[kp], B_pTb[hb]])
                AC(lambda e, s2=s2, bg=bg: e.activation(out=ig4[:TP, s2, :], in_=psum[bg][:TP, :], func=AF.Sigmoid), r=[B_ps[bg]], w=[B_gt[s2]])
                DV(lambda e, s2=s2, bp=bp: e.tensor_tensor(out=e4[:TP, 2 * s2, :], in0=psum[bp][:TP, :], in1=ig4[:TP, s2, :], op=ALU.mult),
                   r=[B_ps[bp], B_gt[s2]], w=[B_pt[s2]])
                hsl = hbuf[:TP, hb, ts, half_ * 512:(half_ + 1) * 512]
                DV(lambda e, s2=s2, hsl=hsl: e.tensor_tensor(out=hsl, in0=hsl, in1=e4[:TP, 2 * s2, :], op=ALU.add), r=[B_pt[s2], B_h[hb][ts]], w=[B_h[hb][ts]])
            wrel(g0 + 25 + half_)
        wrel(g0 + 24)

        if i + 1 < ntiles:
            norm((i + 1) % 2, C_GMIX, tile_info(i + 1), phase="post")

        for ts in range(NTS):
            AC(lambda e, ts=ts: e.activation(out=junk[:TP, :], in_=hbuf[:TP, hb, ts, :], func=AF.Square, accum_out=ss[:TP, ts:ts + 1]),
               r=[B_h[hb][ts]], w=[B_junk, B_ss])
        rstd_chain(ss[:TP, 0:NTS], ms[:TP, 0:NTS], rstd[:TP, 0:NTS], 1.0 / D, [B_ss], B_ms, B_rstd)
        for ts in range(NTS):
            hv = hbuf[:TP, hb, ts, :]
            DV(lambda e, ts=ts, hv=hv: e.scalar_tensor_tensor(out=hv, in0=hv, scalar=rstd[:TP, ts:ts + 1], in1=gfin_b[:TP, :], op0=ALU.mult, op1=ALU.mult),
               r=[B_h[hb][ts], B_rstd, B_const], w=[B_h[hb][ts]])
        if not sample:
            dst = y_p[i * 512:(i + 1) * 512, :].rearrange("(t p) f -> p t f", p=128)
            out_toks.append(S.dma("sp", ch_y[hb], lambda e, dst=dst, hb=hb: e.dma_start(out=dst, in_=hbuf[:, hb, :, :]), reads=B_h[hb]))
        else:
            out_toks.append(S.dma("sp", ch_y[hb], lambda e, hb=hb: e.dma_start(out=y_s[:, :], in_=hbuf[0:64, hb, 0, :]), reads=[B_h[hb][0]]))

    for tok in out_toks:
        S.wait("sp", tok)
    for tok in pool_toks:
        S.wait("pool", tok)

    S.replay()
    stack.close()
    return nc


def _prep_shared(inp):
    f = np.float32
    w_in = np.asarray(inp["w_in"][0], f); w_out = np.asarray(inp["w_out"][0], f)
    w_up = np.asarray(inp["w_up"][0], f); w_down = np.asarray(inp["w_down"][0], f)
    w_pg = np.asarray(inp["w_ple_gate"][0], f); w_ple = np.asarray(inp["w_ple"][0], f)
    wall = np.zeros((NSLOT, 128, 4096), f)
    wall[0:4] = w_in.reshape(8, 128, 4, 512).transpose(2, 1, 0, 3).reshape(4, 128, 4096)
    wall[4:6] = w_out.reshape(8, 128, 2, 512).transpose(2, 1, 0, 3).reshape(2, 128, 4096)
    wall[6:18] = w_up.reshape(8, 128, 2, 6, 512).transpose(3, 2, 1, 0, 4).reshape(12, 128, 4096)
    wall[18:24] = w_down.reshape(3, 8, 128, 2, 512).transpose(3, 0, 2, 1, 4).reshape(6, 128, 4096)
    wall[24, :, 0:2048] = w_ple.reshape(2, 128, 2, 512).transpose(1, 0, 2, 3).reshape(128, 2048)
    wall[25:27] = w_pg.reshape(8, 128, 2, 512).transpose(2, 1, 0, 3).reshape(2, 128, 4096)

    def col(v, n):
        return np.asarray(v, f).reshape(n, 128).T

    cv = np.zeros((128, NCV), f)
    cv[:, C_GMIX:C_GMIX + 8] = col(inp["g_mix_norm"][0], 8)
    cv[:, C_GFFN:C_GFFN + 8] = col(inp["g_ffn_norm"][0], 8)
    cv[:, C_GPLE:C_GPLE + 8] = col(inp["g_ple_norm"][0], 8)
    caw = np.asarray(inp["conv_a_w"][0], f)
    cv[:, C_CAW:C_CAW + 16] = caw.reshape(4, 4, 128).transpose(2, 1, 0).reshape(128, 16)
    cv[:, C_CAB:C_CAB + 4] = col(inp["conv_a_b"][0], 4)
    cv[:, C_BA:C_BA + 4] = col(inp["lru_ba"][0], 4)
    cv[:, C_BX:C_BX + 4] = col(inp["lru_bx"][0], 4)
    cv[:, C_AP:C_AP + 4] = col(inp["lru_a_param"][0], 4)
    cv[:, C_GOA:C_GOA + 4] = col(inp["g_out_a"][0], 4)
    cv[:, C_GOB:C_GOB + 4] = col(inp["g_out_b"][0], 4)
    fw = np.asarray(inp["ffn_conv_w"][0], f)
    cv[:, C_FW:C_FW + 144] = fw.reshape(3, 48, 128).transpose(2, 1, 0).reshape(128, 144)
    cv[:, C_FB:C_FB + 48] = col(inp["ffn_conv_b"][0], 48)

    sgu_w = np.asarray(inp["sgu_w"][0], f)
    sgu_b = np.asarray(inp["sgu_b"][0], f)
    sguT = sgu_w.transpose(2, 0, 1).reshape(128, 512)
    maskT = (np.arange(128)[:, None] <= np.arange(128)[None, :]).astype(f)
    w4 = sgu_w[:, 0:4, 0:4]
    Rs = np.broadcast_to(w4.transpose(2, 0, 1)[None, :, :, None, :], (16, 4, 4, 16, 4)).reshape(64, 256).astype(f)
    bb = np.arange(16)
    mask_s = ((bb[:, None, None, None] == bb[None, None, :, None]) &
              (np.arange(4)[None, :, None, None] <= np.arange(4)[None, None, None, :])).astype(f).reshape(64, 64)
    bsrow = sgu_b.reshape(1, 512)
    bsrow_s = np.broadcast_to(sgu_b[:, None, 0:4], (4, 16, 4)).reshape(1, 256).astype(f)

    def bd(w):
        w = np.asarray(w, f)
        o = np.zeros((128, 4, 128), f)
        for ct in range(4):
            o[0:64, ct, 0:64] = w[2 * ct]
            o[64:128, ct, 64:128] = w[2 * ct + 1]
        return o.reshape(128, 512)

    return dict(
        wall=wall, cvec=cv,
        gfin_b=np.ascontiguousarray(np.broadcast_to(np.asarray(inp["g_final"], f)[None, :], (128, D))),
        lng_b=np.ascontiguousarray(np.broadcast_to(np.asarray(inp["ln_v_g"][0], f)[None, :], (128, 512))),
        lnb_b=np.ascontiguousarray(np.broadcast_to(np.asarray(inp["ln_v_b"][0], f)[None, :], (128, 512))),
        ident=np.eye(128, dtype=f), maskT=maskT, sguT=np.ascontiguousarray(sguT),
        wabd=bd(inp["lru_wa"][0]), wxbd=bd(inp["lru_wx"][0]),
        bsrow=np.ascontiguousarray(bsrow), Rs=np.ascontiguousarray(Rs), mask_s=mask_s, bsrow_s=np.ascontiguousarray(bsrow_s),
    )


_NC_CACHE = {}


def kernel(**inp):
    f = np.float32
    shared = _prep_shared(inp)
    x_prompt = np.asarray(inp["x_prompt"], f); x_sample = np.asarray(inp["x_sample"], f)
    p_prompt = np.asarray(inp["p_prompt"], f); p_sample = np.asarray(inp["p_sample"], f)
    st_h = np.asarray(inp["state_rglru_h"], f); st_c = np.asarray(inp["state_rglru_conv"], f)
    st_f = np.asarray(inp["state_ffn_conv"], f)
    in_maps = []
    for c in range(NCORES):
        sl = slice(16 * c, 16 * c + 16)
        m = dict(shared)
        m["xp"] = np.ascontiguousarray(x_prompt[c])
        m["xs"] = np.ascontiguousarray(x_sample[sl].reshape(64, D))
        m["ppT"] = np.ascontiguousarray(p_prompt[0, c].T)
        m["psT"] = np.ascontiguousarray(p_sample[0, sl].reshape(64, 256).T)
        m["st_hT"] = np.ascontiguousarray(st_h[0, sl].reshape(16, 4, 128).transpose(2, 1, 0).reshape(128, 64))
        m["st_cT"] = np.ascontiguousarray(st_c[0, sl].reshape(16, 3, 4, 128).transpose(3, 2, 0, 1).reshape(128, 192))
        m["st_fT"] = np.ascontiguousarray(st_f[0, sl].reshape(16, 2, 48, 128).transpose(3, 2, 0, 1).reshape(128, 1536))
        in_maps.append(m)
    if "nc" not in _NC_CACHE:
        _NC_CACHE["nc"] = build_nc()
    res = run_bass_kernel_spmd(_NC_CACHE["nc"], in_maps, core_ids=list(range(NCORES)))
    R = res.results
    y_prompt = np.stack([np.asarray(R[c]["y_p"], f) for c in range(NCORES)], 0)
    y_sample = np.concatenate([np.asarray(R[c]["y_s"], f).reshape(16, 4, D) for c in range(NCORES)], 0)
    h_p = np.stack([np.asarray(R[c]["o_hp"], f).T.reshape(512) for c in range(NCORES)], 0)[None]
    h_s = np.concatenate([np.asarray(R[c]["o_hs"], f).reshape(128, 4, 16).transpose(2, 1, 0).reshape(16, 512) for c in range(NCORES)], 0)[None]
    c_p = np.stack([np.asarray(R[c]["o_cp"], f).reshape(128, 4, 3).transpose(2, 1, 0).reshape(3, 512) for c in range(NCORES)], 0)[None]
    c_s = np.concatenate([np.asarray(R[c]["o_cs"], f).reshape(128, 4, 16, 3).transpose(2, 3, 1, 0).reshape(16, 3, 512) for c in range(NCORES)], 0)[None]
    v_s = np.concatenate([np.asarray(R[c]["o_vs"], f).reshape(16, 4, 512) for c in range(NCORES)], 0)[None]
    f_p = np.stack([np.asarray(R[c]["o_fp"], f).reshape(128, 48, 2).transpose(2, 1, 0).reshape(2, 6144) for c in range(NCORES)], 0)[None]
    f_s = np.concatenate([np.asarray(R[c]["o_fs"], f).reshape(128, 48, 16, 2).transpose(2, 3, 1, 0).reshape(16, 2, 6144) for c in range(NCORES)], 0)[None]
    return (y_prompt, y_sample, np.ascontiguousarray(h_p), np.ascontiguousarray(h_s), np.ascontiguousarray(c_p),
            np.ascontiguousarray(c_s), np.ascontiguousarray(v_s), np.ascontiguousarray(f_p), np.ascontiguousarray(f_s))
```

```python
import numpy as np
import concourse.bass as bass
import concourse.mybir as mybir
from concourse.bass_utils import run_bass_kernel_spmd

F32 = mybir.dt.float32
BF16 = mybir.dt.bfloat16
AF = mybir.ActivationFunctionType
ALU = mybir.AluOpType

NCORES = 8
D = 1024
WA = 512
DFF = 3072
EPS = 1e-6
NSLOT = 27
RING = 5
SAME_ENGINE_SYNC = True

C_GMIX, C_GFFN, C_GPLE, C_CAW, C_CAB, C_BA, C_BX, C_AP, C_GOA, C_GOB, C_FW, C_FB = 0, 8, 16, 24, 40, 44, 48, 52, 56, 60, 64, 208
NCV = 256


import types


def _freeze(fn):
    if fn is None or fn.__closure__ is None:
        return fn
    cells = []
    for c in fn.__closure__:
        try:
            cells.append(types.CellType(c.cell_contents))
        except ValueError:
            cells.append(c)
    return types.FunctionType(fn.__code__, fn.__globals__, fn.__name__, fn.__defaults__, tuple(cells))


class Sem:
    def __init__(self, nc, stack, name):
        self.h = stack.enter_context(nc.semaphore(name))
        self.name = name


class Buf:
    __slots__ = ("name", "w", "r")

    def __init__(self, name):
        self.name = name
        self.w = None
        self.r = {}


class Chan:
    def __init__(self, sem):
        self.sem = sem
        self.cnt = 0


class Eng:
    def __init__(self, name, sem, same_sync):
        self.name = name
        self.sem = sem
        self.cnt = 0
        self.waited = {}
        self.ops = []
        self.same_sync = same_sync


class Sched:
    def __init__(self, nc, stack):
        self.nc = nc
        self.stack = stack
        self.nsem = 0
        self.eng = {}
        for n in ("pe", "act", "dve", "pool", "sp"):
            self.eng[n] = Eng(n, self.newsem("e_" + n), SAME_ENGINE_SYNC and n not in ("pe", "sp"))

    def newsem(self, name):
        self.nsem += 1
        return Sem(self.nc, self.stack, name)

    def chan(self, name):
        return Chan(self.newsem("c_" + name))

    def _deps(self, e, reads, writes):
        deps = {}

        def add(s, v):
            if deps.get(s, 0) < v:
                deps[s] = v

        for b in reads:
            if b.w is not None:
                add(*b.w)
        for b in writes:
            if b.w is not None:
                add(*b.w)
            for s, v in b.r.items():
                add(s, v)
        waits = []
        for s, v in deps.items():
            if s is e.sem and not e.same_sync:
                continue
            if e.waited.get(s, 0) >= v:
                continue
            e.waited[s] = v
            waits.append((s, v))
        return waits

    def _commit(self, tok, reads, writes):
        for b in writes:
            b.w = tok
            b.r = {}
        for b in reads:
            if b.r.get(tok[0], 0) < tok[1]:
                b.r[tok[0]] = tok[1]

    def op(self, en, fn, reads=(), writes=(), inc=True):
        e = self.eng[en]
        waits = self._deps(e, reads, writes)
        tok = (e.sem, e.cnt + 1)
        if inc:
            e.cnt += 1
        e.ops.append((waits, _freeze(fn), (e.sem, 1) if inc else None))
        self._commit(tok, reads, writes)
        return tok

    def dma(self, en, ch, fn, reads=(), writes=()):
        e = self.eng[en]
        waits = self._deps(e, reads, writes)
        if en == "pool" and ch.cnt > 0 and e.waited.get(ch.sem, 0) < ch.cnt:
            e.waited[ch.sem] = ch.cnt
            waits.append((ch.sem, ch.cnt))
        ch.cnt += 16
        tok = (ch.sem, ch.cnt)
        e.ops.append((waits, _freeze(fn), (ch.sem, 16)))
        self._commit(tok, reads, writes)
        return tok

    def wait(self, en, tok):
        e = self.eng[en]
        if e.waited.get(tok[0], 0) >= tok[1]:
            return
        e.waited[tok[0]] = tok[1]
        e.ops.append(([tok], None, None))

    def replay(self):
        nc = self.nc
        with nc.Block() as block:
            def run(eobj, e):
                for waits, fn, inc in e.ops:
                    for s, v in waits:
                        eobj.wait_ge(s.h, v)
                    if fn is None:
                        continue
                    ins = fn(eobj)
                    if inc is not None:
                        ins.then_inc(inc[0].h, inc[1])

            @block.tensor
            def _(x):
                run(x, self.eng["pe"])

            @block.scalar
            def _(x):
                run(x, self.eng["act"])

            @block.vector
            def _(x):
                run(x, self.eng["dve"])

            @block.gpsimd
            def _(x):
                run(x, self.eng["pool"])

            @block.sync
            def _(x):
                run(x, self.eng["sp"])


def build_nc(ntiles=5):
    from contextlib import ExitStack
    nc = bass.Bass("TRN2", target_bir_lowering=False)
    stack = ExitStack()

    def din(name, shape, dt=F32):
        return nc.dram_tensor(name, list(shape), dt, kind="ExternalInput").ap()

    def dout(name, shape, dt=F32):
        return nc.dram_tensor(name, list(shape), dt, kind="ExternalOutput").ap()

    xp = din("xp", [2048, D]); xs = din("xs", [64, D])
    ppT = din("ppT", [256, 2048]); psT = din("psT", [256, 64])
    wall = din("wall", [NSLOT, 128, 4096])
    cvec_d = din("cvec", [128, NCV]); gfin_d = din("gfin_b", [128, D])
    lng_d = din("lng_b", [128, 512]); lnb_d = din("lnb_b", [128, 512])
    ident_d = din("ident", [128, 128]); maskT_d = din("maskT", [128, 128])
    sguT_d = din("sguT", [128, 512]); wabd_d = din("wabd", [128, 512]); wxbd_d = din("wxbd", [128, 512])
    bsrow_d = din("bsrow", [1, 512]); Rs_d = din("Rs", [64, 256]); masks_d = din("mask_s", [64, 64])
    bsrow_s_d = din("bsrow_s", [1, 256])
    sth_d = din("st_hT", [128, 64]); stc_d = din("st_cT", [128, 192]); stf_d = din("st_fT", [128, 1536])

    y_p = dout("y_p", [2048, D]); y_s = dout("y_s", [64, D])
    o_hp = dout("o_hp", [128, 4]); o_hs = dout("o_hs", [128, 64])
    o_cp = dout("o_cp", [128, 12]); o_cs = dout("o_cs", [128, 192])
    o_vs = dout("o_vs", [64, 512]); o_fp = dout("o_fp", [128, 96]); o_fs = dout("o_fs", [128, 1536])
    wbf = nc.dram_tensor("wbf", [NSLOT, 128, 4096], BF16, kind="Internal").ap()

    def sb(name, shape, dt=F32):
        return stack.enter_context(nc.sbuf_tensor(name, list(shape), dt))

    hbuf = sb("hbuf", [128, 2, 4, D])
    nscr = sb("nscr", [128, 2, D])
    junk = sb("junk", [128, D], BF16)
    nT = sb("nT", [128, 8, 512], BF16)
    xa_ext = sb("xa_ext", [128, 4, 520])
    xc = sb("xc", [128, 4, 512]); xcb = sb("xcb", [128, 4, 512], BF16)
    e4 = sb("e4", [128, 4, 512]); ig4 = sb("ig4", [128, 4, 512]); rr4 = sb("rr4", [128, 4, 512])
    shared = sb("shared", [128, 6144])
    vnb = sb("vnb", [128, 4, 512], BF16)
    hs = sb("hs", [128, 2, 512])
    yabT = sb("yabT", [128, 8, 512], BF16)
    corr = sb("corr", [128, 48, 2])
    pTb = sb("pTb", [128, 2, 2, 512], BF16)
    cvec = sb("cvec_s", [128, NCV]); gfin_b = sb("gfin_s", [128, D])
    lng_b = sb("lng_s", [128, 512]); lnb_b = sb("lnb_s", [128, 512])
    ident = sb("ident_s", [128, 128])
    wsgu = sb("wsgu", [128, 4, 128], BF16); wabd = sb("wabd_s", [128, 4, 128], BF16); wxbd = sb("wxbd_s", [128, 4, 128], BF16)
    wsgu_s = sb("wsgu_ss", [128, 4, 64], BF16)
    bsh = sb("bsh", [1, 768], BF16); bsl = sb("bsl", [1, 768], BF16)
    bsf = hbuf[0:1, 1, 0, 0:768]
    ones_bf = sb("ones_bf", [1, 128], BF16); ones_f = sb("ones_f", [128, 2]); ones_b2 = sb("ones_b2", [128, 2], BF16); mhalf = sb("mhalf", [128, 8])
    st_hT = sb("st_hT_s", [128, 4, 16]); st_cT = sb("st_cT_s", [128, 4, 16, 3]); st_fT = sb("st_fT_s", [128, 48, 16, 2])
    fext = sb("fext", [128, 2, 16, 6])
    hsout = sb("hsout", [128, 4, 16]); tmp16 = sb("tmp16", [128, 16]); cs_out = sb("cs_out", [128, 4, 16, 3])
    xhist = sb("xhist", [128, 4, 3]); hlast = sb("hlast", [128, 4]); fhist = sb("fhist", [128, 48, 2])
    cA = sb("cA", [128, 4]); cA2 = sb("cA2", [128, 4]); spt = sb("spt", [128, 4])
    ss = sb("ss", [128, 4]); ms = sb("ms", [128, 4]); rstd = sb("rstd", [128, 4])
    ssn = sb("ssn", [128, 4]); msn = sb("msn", [128, 4]); rstdn = sb("rstdn", [128, 4])
    ms8 = sb("ms8", [128, 8]); rstd8 = sb("rstd8", [128, 8])
    bst = sb("bst", [128, 2, 6]); mv = sb("mv", [128, 2, 2]); rsv = sb("rsv", [128, 2])
    ring = sb("ring", [128, RING, 8, 512], BF16)
    psum = [stack.enter_context(nc.psum_tensor("ps%d" % i, [128, 512], F32)) for i in range(8)]

    actT = shared[:, :].bitcast(BF16).rearrange("p (j t) -> p j t", t=512)
    sh3 = shared[:, :].rearrange("p (a t) -> p a t", t=512)

    S = Sched(nc, stack)
    U = [Buf("U%d" % i) for i in range(24)]
    B_gga = [[U[2 * c], U[2 * c + 1]] for c in range(4)]
    B_gub = [[U[8 + 2 * c], U[9 + 2 * c]] for c in range(4)]
    B_gvb = [[U[16 + 2 * k], U[17 + 2 * k]] for k in range(2)]
    B_rr4 = [Buf("rr%d" % c) for c in range(4)]; B_e4 = [Buf("e4_%d" % c) for c in range(4)]; B_ig4 = [Buf("ig4_%d" % c) for c in range(4)]
    B_act = [[U[j]] for j in range(24)]
    B_h = [[Buf("h%d_%d" % (a, t)) for t in range(4)] for a in range(2)]
    B_nscr = [Buf("nscr0"), Buf("nscr1")]; B_junk = Buf("junk"); B_corr = Buf("corr")
    B_nT = [Buf("nT%d" % k) for k in range(8)]
    B_xa = [Buf("xa%d" % c) for c in range(4)]
    B_xc = [Buf("xc%d" % k) for k in range(4)]; B_xcb = [Buf("xcb%d" % k) for k in range(4)]
    B_vnb = [Buf("vnb%d" % t) for t in range(4)]
    B_hs = [Buf("hs%d" % c) for c in range(2)]
    B_yab = [Buf("yab%d" % c) for c in range(8)]
    B_fc = [[B_e4[2 * s_ + p_] for p_ in range(2)] for s_ in range(2)]
    B_fG = [B_ig4[0], B_ig4[1]]
    B_gt = B_fG; B_pt = [B_fc[0][0], B_fc[1][0]]
    B_pTb = [Buf("pTb%d" % k) for k in range(2)]
    B_ps = [Buf("ps%d" % k) for k in range(8)]
    B_ring = [Buf("ring%d" % k) for k in range(RING)]
    B_wbf = [Buf("wbf%d" % k) for k in range(NSLOT)]
    B_xhist = [Buf("xhist%d" % c) for c in range(4)]
    B_hlast = [Buf("hlast%d" % c) for c in range(4)]
    B_fhist = [Buf("fhist%d" % f) for f in range(48)]
    B_stf = [Buf("stf%d" % f) for f in range(48)]
    B_fext = [Buf("fext%d" % k) for k in range(2)]
    B_hsout = Buf("hsout"); B_tmp16 = Buf("tmp16"); B_csout = Buf("csout")
    B_ssn = [Buf("ssn%d" % t) for t in range(4)]; B_msn = [Buf("msn%d" % t) for t in range(4)]; B_rstdn = [Buf("rstdn%d" % t) for t in range(4)]
    B_ss = Buf("ss"); B_ms = Buf("ms"); B_rstd = Buf("rstd"); B_ms8 = Buf("ms8"); B_rstd8 = Buf("rstd8")
    B_bst = [Buf("bst%d" % k) for k in range(2)]; B_mv = [Buf("mv%d" % k) for k in range(2)]; B_rsv = [Buf("rsv%d" % k) for k in range(2)]
    B_const = Buf("const")

    ch_setup = S.chan("setup")
    setup_loads = [
        (cvec[:, :], cvec_d), (gfin_b[:, :], gfin_d), (lng_b[:, :], lng_d), (lnb_b[:, :], lnb_d),
        (ident[:, :], ident_d), (nscr[:, 0, 0:512], sguT_d), (nscr[:, 0, 512:640], maskT_d),
        (nscr[0:64, 0, 640:896], Rs_d), (nscr[0:64, 0, 896:960], masks_d),
        (bsf[:, 0:512], bsrow_d), (bsf[:, 512:768], bsrow_s_d),
        (st_hT[:, :, :].rearrange("p a b -> p (a b)"), sth_d),
        (st_cT[:, :, :, :].rearrange("p a b c -> p (a b c)"), stc_d),
        (st_fT[:, :, :, :].rearrange("p a b c -> p (a b c)"), stf_d),
    ]
    for o_, i_ in setup_loads:
        S.dma("sp", ch_setup, lambda e, o_=o_, i_=i_: e.dma_start(out=o_, in_=i_), writes=[])
    B_gw = Buf("gatew")
    ch_gw = [S.chan("gw0"), S.chan("gw1")]
    S.dma("pool", ch_gw[0], lambda e: e.dma_start(out=wabd[:, :, :].rearrange("p a b -> p (a b)"), in_=wabd_d), writes=[B_gw])
    tgw = S.dma("pool", ch_gw[1], lambda e: e.dma_start(out=wxbd[:, :, :].rearrange("p a b -> p (a b)"), in_=wxbd_d), writes=[])
    tok_setup = (ch_setup.sem, ch_setup.cnt)
    B_const.w = tok_setup
    B_nscr[0].w = tok_setup
    B_h[1][0].w = tok_setup

    def DV(fn, r=(), w=()):
        return S.op("dve", fn, reads=r, writes=w)

    def AC(fn, r=(), w=()):
        return S.op("act", fn, reads=r, writes=w)

    for h_ in range(4):
        DV(lambda e, h_=h_: e.tensor_tensor(out=wsgu[:, h_, :], in0=nscr[:, 0, h_ * 128:(h_ + 1) * 128], in1=nscr[:, 0, 512:640], op=ALU.mult),
           r=[B_nscr[0]], w=[B_const])
        DV(lambda e, h_=h_: e.tensor_tensor(out=wsgu_s[0:64, h_, :], in0=nscr[0:64, 0, 640 + h_ * 64:640 + (h_ + 1) * 64], in1=nscr[0:64, 0, 896:960], op=ALU.mult),
           r=[B_nscr[0]], w=[B_const])
    DV(lambda e: e.tensor_copy(out=bsh[0:1, :], in_=bsf), r=[B_const, B_h[1][0]], w=[B_const])
    bsg_v = nscr[0:1, 1, 0:768]
    DV(lambda e: e.tensor_copy(out=bsg_v, in_=bsh[0:1, :]), r=[B_const], w=[B_const, B_nscr[1]])
    DV(lambda e: e.tensor_tensor(out=bsl[0:1, :], in0=bsf, in1=bsg_v, op=ALU.subtract), r=[B_const, B_nscr[1], B_h[1][0]], w=[B_const])
    DV(lambda e: e.memset(ones_bf[0:1, :], 1.0), w=[B_const])
    DV(lambda e: e.memset(ones_f[:, :], 1.0), w=[B_const])
    DV(lambda e: e.memset(ones_b2[:, :], 1.0), w=[B_const])
    DV(lambda e: e.memset(mhalf[:, :], -0.5), w=[B_const])
    DV(lambda e: e.memset(xhist[:, :, :], 0.0), w=B_xhist)
    DV(lambda e: e.memset(hlast[:, :], 0.0), w=B_hlast)
    DV(lambda e: e.memset(fhist[:, :, :], 0.0), w=B_fhist)
    AC(lambda e: e.activation(out=spt[:, :], in_=cvec[:, C_AP:C_AP + 4], func=AF.Exp, scale=-1.0), r=[B_const], w=[B_const])
    AC(lambda e: e.activation(out=spt[:, :], in_=spt[:, :], func=AF.Ln, bias=1.0), r=[B_const], w=[B_const])
    DV(lambda e: e.tensor_scalar(out=cA[:, :], in0=spt[:, :], scalar1=-8.0, scalar2=None, op0=ALU.mult), r=[B_const], w=[B_const])
    DV(lambda e: e.tensor_scalar(out=cA2[:, :], in0=spt[:, :], scalar1=-16.0, scalar2=None, op0=ALU.mult), r=[B_const], w=[B_const])

    ch_wbf = [S.chan("wbf%d" % s_) for s_ in range(NSLOT)]
    ch_ring = [S.chan("ring%d" % k) for k in range(RING)]
    total_loads = NSLOT * ntiles
    st = {"next_load": 0, "released": [False] * total_loads, "prefix": 0, "cast": 0, "bank": 0}

    def emit_cast(s_):
        src = wall[s_, :, :].rearrange("p (a b) -> p a b", b=2048)
        dst = wbf[s_, :, :].rearrange("p (a b) -> p a b", b=2048)
        S.dma("pool", ch_wbf[s_], lambda e: e.dma_start(out=dst, in_=src), writes=[B_wbf[s_]])

    NFAST = NSLOT
    ch_st = [S.chan("st%d" % k) for k in range(NSLOT)]

    def pump():
        while st["next_load"] < total_loads and st["next_load"] - RING < st["prefix"]:
            g = st["next_load"]
            s_ = g % NSLOT
            k = g % RING
            if g < NFAST:
                src = wall[s_, :, :].rearrange("p (a b) -> p a b", b=2048)
                dst = ring[:, k, :, :].rearrange("p a b -> p (a b)").rearrange("p (a b) -> p a b", b=2048)
                S.dma("pool", ch_wbf[s_], lambda e, src=src, dst=dst: e.dma_start(out=dst, in_=src), writes=[B_ring[k]])
                S.dma("sp", ch_st[g], lambda e, s_=s_, k=k: e.dma_start(out=wbf[s_, :, :], in_=ring[:, k, :, :].rearrange("p a b -> p (a b)")),
                      reads=[B_ring[k]], writes=[B_wbf[s_]])
            else:
                S.dma("sp", ch_ring[k],
                      lambda e, s_=s_, k=k: e.dma_start(out=ring[:, k, :, :].rearrange("p a b -> p (a b)"), in_=wbf[s_, :, :]),
                      reads=[B_wbf[s_]], writes=[B_ring[k]])
            st["next_load"] += 1

    def wget(g):
        pump()
        assert g < st["next_load"], "weight slot %d not loaded (ring deadlock)" % g
        return g % RING

    def wrel(g):
        st["released"][g] = True
        while st["prefix"] < total_loads and st["released"][st["prefix"]]:
            st["prefix"] += 1
        pump()

    reserved = set()

    def nbank():
        b = st["bank"]
        while b in reserved:
            b = (b + 1) % 8
        st["bank"] = (b + 1) % 8
        return b

    def mm_group(bank, out_ap, pairs, reads, first_start=True, skip=False):
        n = len(pairs)
        for i_, (l_, r_) in enumerate(pairs):
            S.op("pe", lambda e, l_=l_, r_=r_, i_=i_: e.matmul(out_ap, l_, r_, start=(first_start and i_ == 0), stop=(i_ == n - 1),
                                                         skip_group_check=skip),
                 reads=reads, writes=[B_ps[bank]], inc=(i_ == n - 1))

    ch_x = [S.chan("x%d" % k) for k in range(2)]
    ch_p = [S.chan("p%d" % k) for k in range(2)]
    ch_y = [S.chan("y%d" % k) for k in range(2)]
    ch_out = S.chan("out")
    out_toks = []
    pool_toks = []

    def tile_info(i):
        if i < 4:
            return dict(T=512, NTS=4, TP=128, nb=1, L=512, sample=False)
        return dict(T=64, NTS=1, TP=64, nb=16, L=4, sample=True)

    def emit_xload(i):
        hb = i % 2
        ti = tile_info(i)
        if not ti["sample"]:
            src = xp[i * 512:(i + 1) * 512, :].rearrange("(t p) f -> p t f", p=128)
            S.dma("sp", ch_x[hb], lambda e: e.dma_start(out=hbuf[:, hb, :, :], in_=src), writes=B_h[hb])
            psrc = ppT[:, i * 512:(i + 1) * 512].rearrange("(k p) t -> p k t", p=128)
            S.dma("pool", ch_p[hb], lambda e: e.dma_start(out=pTb[:, hb, :, :], in_=psrc), writes=[B_pTb[hb]])
        else:
            S.dma("sp", ch_x[hb], lambda e: e.dma_start(out=hbuf[0:64, hb, 0, :], in_=xs[:, :]), writes=[B_h[hb][0]])
            psrc = psT[:, :].rearrange("(k p) t -> p k t", p=128)
            S.dma("pool", ch_p[hb], lambda e: e.dma_start(out=pTb[:, hb, :, 0:64], in_=psrc), writes=[B_pTb[hb]])

    def rstd_chain(src_ap, ms_ap, rstd_ap, scale, Bsrc, Bms, Brstd):
        DV(lambda e: e.tensor_scalar(out=ms_ap, in0=src_ap, scalar1=scale, scalar2=EPS, op0=ALU.mult, op1=ALU.add), r=Bsrc, w=[Bms])
        npart, ncol = ms_ap.shape[0], ms_ap.shape[1]
        S.op("pool", lambda e: e.tensor_tensor(out=rstd_ap, in0=ms_ap, in1=mhalf[:npart, 0:ncol], op=ALU.pow), reads=[Bms, B_const], writes=[Brstd])

    def norm(hb, gcol, ti, phase="all"):
        T, NTS, TP = ti["T"], ti["NTS"], ti["TP"]
        banks = list(range(8))
        NPRE = min(2, NTS)

        def stat(ts):
            AC(lambda e: e.activation(out=junk[:TP, :], in_=hbuf[:TP, hb, ts, :], func=AF.Square, accum_out=ssn[:TP, ts:ts + 1]),
               r=[B_h[hb][ts]], w=[B_junk, B_ssn[ts]])
            DV(lambda e: e.tensor_scalar(out=msn[:TP, ts:ts + 1], in0=ssn[:TP, ts:ts + 1], scalar1=1.0 / D, scalar2=EPS, op0=ALU.mult, op1=ALU.add),
               r=[B_ssn[ts]], w=[B_msn[ts]])
            S.op("pool", lambda e: e.tensor_tensor(out=rstdn[:TP, ts:ts + 1], in0=msn[:TP, ts:ts + 1], in1=mhalf[:TP, 0:1], op=ALU.pow),
                 reads=[B_msn[ts], B_const], writes=[B_rstdn[ts]])

        def scale(ts):
            nk = ts % 2
            DV(lambda e: e.tensor_scalar(out=nscr[:TP, nk, :], in0=hbuf[:TP, hb, ts, :], scalar1=rstdn[:TP, ts:ts + 1], scalar2=None, op0=ALU.mult),
               r=[B_h[hb][ts], B_rstdn[ts]], w=[B_nscr[nk]])

        def transp(ts):
            nk = ts % 2
            for kt in range(8):
                S.op("pe", lambda e, kt=kt: e.transpose(out=psum[banks[kt]][:, ts * 128:ts * 128 + TP], in_=nscr[:TP, nk, kt * 128:(kt + 1) * 128],
                                                        identity=ident[:TP, :TP]),
                     reads=[B_nscr[nk], B_const], writes=[B_ps[banks[kt]]], inc=(kt == 7))

        if phase == "pre":
            for ts in range(NTS):
                stat(ts)
            for ts in range(NPRE):
                scale(ts)
            return
        if phase == "post":
            for ts in range(NPRE):
                transp(ts)
            for ts in range(NPRE, NTS):
                scale(ts)
                transp(ts)
        else:
            order = []
            for ts in range(NTS):
                order.append(("stat", ts))
                if ts >= 1:
                    order.append(("scale", ts - 1))
            order.append(("scale", NTS - 1))
            for kind, ts in order:
                if kind == "stat":
                    stat(ts)
                else:
                    scale(ts)
                    transp(ts)
        for kt in range(8):
            if kt % 2 == 0:
                AC(lambda e, kt=kt: e.activation(out=nT[:, kt, 0:T], in_=psum[banks[kt]][:, 0:T], func=AF.Identity, scale=cvec[:, gcol + kt:gcol + kt + 1]),
                   r=[B_ps[banks[kt]], B_const], w=[B_nT[kt]])
            else:
                DV(lambda e, kt=kt: e.tensor_scalar(out=nT[:, kt, 0:T], in0=psum[banks[kt]][:, 0:T], scalar1=cvec[:, gcol + kt:gcol + kt + 1], scalar2=None, op0=ALU.mult),
                   r=[B_ps[banks[kt]], B_const], w=[B_nT[kt]])
        st["bank"] = 0

    def outdma(dst, src, reads):
        pool_toks.append(S.dma("pool", S.chan("o%d" % len(pool_toks)), lambda e: e.dma_start(out=dst, in_=src), reads=reads))

    emit_xload(0)
    S.wait("pool", tok_setup)
    S.wait("pool", (ch_x[0].sem, ch_x[0].cnt))
    pump()
    for s_ in range(NFAST, 8):
        emit_cast(s_)
    for i in range(ntiles):
        ti = tile_info(i)
        T, NTS, TP, nb, L, sample = ti["T"], ti["NTS"], ti["TP"], ti["nb"], ti["L"], ti["sample"]
        hb = i % 2
        g0 = i * NSLOT
        def v3(ap, l=L):
            return ap.rearrange("p (b l) -> p b l", l=l) if sample else ap

        if i == 0:
            norm(hb, C_GMIX, ti)
            for s_ in range(max(8, NFAST), NSLOT):
                emit_cast(s_)

        EXT = 3 + L
        def phaseA1():
            for ct in range(4):
                xav = xa_ext[:, ct, 0:nb * EXT].rearrange("p (b l) -> p b l", l=EXT)
                xcv = xc[:, ct, 0:T].rearrange("p (b l) -> p b l", l=L)
                cw = C_CAW + ct * 4
                DV(lambda e, xav=xav, xcv=xcv, cw=cw, ct=ct: e.tensor_scalar(out=xcv, in0=xav[:, :, 0:L], scalar1=cvec[:, cw:cw + 1], scalar2=cvec[:, C_CAB + ct:C_CAB + ct + 1],
                                                                        op0=ALU.mult, op1=ALU.add), r=[B_xa[ct], B_const], w=[B_xc[ct]])
                for j in range(1, 4):
                    DV(lambda e, xav=xav, xcv=xcv, cw=cw, j=j: e.scalar_tensor_tensor(out=xcv, in0=xav[:, :, j:j + L], scalar=cvec[:, cw + j:cw + j + 1], in1=xcv,
                                                                                 op0=ALU.mult, op1=ALU.add), r=[B_xa[ct], B_const, B_xc[ct]], w=[B_xc[ct]])
                DV(lambda e, ct=ct: e.tensor_copy(out=xcb[:, ct, 0:T], in_=xc[:, ct, 0:T]), r=[B_xc[ct]], w=[B_xcb[ct]])


        sbank = nbank()
        reserved.add(sbank)
        first_stat = [True]

        def stats_mm(sq_ap, Bsq, base):
            for ts in range(NTS):
                fs = first_stat[0]
                first_stat[0] = False
                S.op("pe", lambda e, ts=ts, fs=fs: e.matmul(psum[sbank][:TP, base + 2 * ts:base + 2 * ts + 2], sq_ap[:, ts * 128:ts * 128 + TP], ones_b2[:, 0:2],
                                                        start=fs, stop=True, skip_group_check=True),
                     reads=[Bsq, B_const], writes=[B_ps[sbank]], inc=(ts == NTS - 1))

        def gatesA(ct):
            b1 = nbank()
            S.wait("pe", tgw)
            mm_group(b1, psum[b1][:, 0:T], [(wabd[:, ct, :], xcb[:, ct, 0:T])], reads=[B_const, B_gw, B_xcb[ct]])
            b2 = nbank()
            mm_group(b2, psum[b2][:, 0:T], [(wxbd[:, ct, :], xcb[:, ct, 0:T])], reads=[B_const, B_xcb[ct]])
            return b1, b2

        def sigA(ct, b1, b2):
            AC(lambda e: e.activation(out=rr4[:, ct, 0:T], in_=psum[b1][:, 0:T], func=AF.Sigmoid, bias=cvec[:, C_BA + ct:C_BA + ct + 1]),
               r=[B_ps[b1], B_const], w=[B_rr4[ct]])
            AC(lambda e: e.activation(out=ig4[:, ct, 0:T], in_=psum[b2][:, 0:T], func=AF.Sigmoid, bias=cvec[:, C_BX + ct:C_BX + ct + 1]),
               r=[B_ps[b2], B_const], w=[B_ig4[ct]])

        def expA(ct):
            AC(lambda e: e.activation(out=e4[:, ct, 0:T], in_=rr4[:, ct, 0:T], func=AF.Exp, scale=cA2[:, ct:ct + 1]), r=[B_rr4[ct], B_const], w=[B_e4[ct]])
            AC(lambda e: e.activation(out=rr4[:, ct, 0:T], in_=rr4[:, ct, 0:T], func=AF.Exp, scale=cA[:, ct:ct + 1]), r=[B_rr4[ct], B_const], w=[B_rr4[ct]])

        def sqrtA(ct):
            AC(lambda e: e.activation(out=e4[:, ct, 0:T], in_=e4[:, ct, 0:T], func=AF.Sqrt, scale=-1.0, bias=1.0), r=[B_e4[ct]], w=[B_e4[ct]])

        def postA(ct):
            hk = ct % 2
            rr = rr4[:, ct, 0:T]
            ee = e4[:, ct, 0:T]
            ig = ig4[:, ct, 0:T]
            xcc = xc[:, ct, 0:T]
            if i == 0:
                DV(lambda e: e.memset(e4[:, ct, 0:1], 1.0), w=[B_e4[ct]])
            DV(lambda e: e.tensor_tensor(out=ig, in0=ig, in1=ee, op=ALU.mult), r=[B_ig4[ct], B_e4[ct]], w=[B_ig4[ct]])
            DV(lambda e: e.tensor_tensor(out=xcc, in0=xcc, in1=ig, op=ALU.mult), r=[B_ig4[ct], B_xc[ct]], w=[B_xc[ct]])
            if sample:
                rv = rr.rearrange("p (b l) -> p b l", l=L)
                uv = xcc.rearrange("p (b l) -> p b l", l=L)
                DV(lambda e: e.tensor_tensor(out=tmp16[:, :], in0=rv[:, :, 0], in1=st_hT[:, ct, :], op=ALU.mult), r=[B_rr4[ct], B_const], w=[B_tmp16])
                DV(lambda e: e.tensor_tensor(out=uv[:, :, 0], in0=uv[:, :, 0], in1=tmp16[:, :], op=ALU.add), r=[B_tmp16, B_xc[ct]], w=[B_xc[ct]])
                DV(lambda e: e.memset(rv[:, :, 0], 0.0), w=[B_rr4[ct]])
                DV(lambda e: e.tensor_tensor_scan(out=hs[:, hk, 0:T], data0=rr, data1=xcc, initial=0.0, op0=ALU.mult, op1=ALU.add),
                   r=[B_rr4[ct], B_xc[ct]], w=[B_hs[hk]])
                DV(lambda e: e.tensor_copy(out=hsout[:, ct, :], in_=hs[:, hk, 0:T].rearrange("p (b l) -> p b l", l=L)[:, :, L - 1]), r=[B_hs[hk]], w=[B_hsout])
            else:
                DV(lambda e: e.tensor_tensor_scan(out=hs[:, hk, 0:T], data0=rr, data1=xcc, initial=hlast[:, ct:ct + 1], op0=ALU.mult, op1=ALU.add),
                   r=[B_rr4[ct], B_xc[ct], B_hlast[ct]], w=[B_hs[hk]])
                DV(lambda e: e.tensor_copy(out=hlast[:, ct:ct + 1], in_=hs[:, hk, L - 1:L]), r=[B_hs[hk]], w=[B_hlast[ct]])
            DV(lambda e: e.tensor_tensor(out=ee, in0=hs[:, hk, 0:T], in1=sh3[:, ct, 0:T], op=ALU.mult), r=[B_hs[hk]] + B_gga[ct], w=[B_e4[ct]])

        def postA_act(ct):
            ee = e4[:, ct, 0:T]
            sqv = ig4[:, ct, :].bitcast(BF16)[:, 0:T]
            AC(lambda e: e.activation(out=sqv, in_=ee, func=AF.Square), r=[B_e4[ct]], w=[B_ig4[ct]])
            stats_mm(sqv, B_ig4[ct], 0)
            AC(lambda e: e.activation(out=yabT[:, ct, 0:T], in_=ee, func=AF.Identity, scale=cvec[:, C_GOA + ct:C_GOA + ct + 1]),
               r=[B_e4[ct], B_const], w=[B_yab[ct]])

        for c in range(3):
            k = wget(g0 + c)
            for ct in range(4):
                bank = nbank()
                mm_group(bank, psum[bank][:, 0:T], [(ring[:, k, kt, ct * 128:(ct + 1) * 128], nT[:, kt, 0:T]) for kt in range(8)],
                         reads=[B_ring[k]] + B_nT)
                if c == 0:
                    xav = xa_ext[:, ct, 0:nb * EXT].rearrange("p (b l) -> p b l", l=EXT)
                    if sample:
                        DV(lambda e, xav=xav, ct=ct: e.tensor_copy(out=xav[:, :, 0:3], in_=st_cT[:, ct, :, :]), r=[B_const], w=[B_xa[ct]])
                    else:
                        DV(lambda e, xav=xav, ct=ct: e.tensor_copy(out=xav[:, :, 0:3], in_=xhist[:, ct:ct + 1, :]), r=[B_xhist[ct]], w=[B_xa[ct]])
                    DV(lambda e, xav=xav, bank=bank: e.tensor_copy(out=xav[:, :, 3:3 + L], in_=psum[bank][:, 0:T].rearrange("p (b l) -> p b l", l=L)),
                       r=[B_ps[bank]], w=[B_xa[ct]])
                    if not sample:
                        DV(lambda e, xav=xav, ct=ct: e.tensor_copy(out=xhist[:, ct:ct + 1, :], in_=xav[:, :, L:L + 3]), r=[B_xa[ct]], w=[B_xhist[ct]])
                    else:
                        DV(lambda e, xav=xav, ct=ct: e.tensor_copy(out=cs_out[:, ct, :, :], in_=xav[:, :, 4:7]), r=[B_xa[ct]], w=[B_csout])
                        if ct == 3:
                            outdma(o_cs[:, :], cs_out[:, :, :, :].rearrange("p a b c -> p (a b c)"), [B_csout])
                elif c == 1:
                    AC(lambda e, bank=bank, ct=ct: e.activation(out=sh3[:, ct, 0:T], in_=psum[bank][:, 0:T], func=AF.Gelu_apprx_tanh),
                       r=[B_ps[bank]], w=B_gga[ct])
                else:
                    AC(lambda e, bank=bank, ct=ct: e.activation(out=sh3[:, 4 + ct, 0:T], in_=psum[bank][:, 0:T], func=AF.Gelu_apprx_tanh),
                       r=[B_ps[bank]], w=B_gub[ct])
            wrel(g0 + c)
            if c == 0:
                phaseA1()
        k = wget(g0 + 3)

        def vb_mm(ts):
            bank = nbank()
            kk = ts % 2
            mm_group(bank, psum[bank][:TP, :], [(nT[:, kt, ts * 128:ts * 128 + TP], ring[:, k, kt, :]) for kt in range(8)],
                     reads=[B_ring[k]] + B_nT)
            gv = sh3[:TP, 8 + kk, :]
            AC(lambda e, bank=bank, gv=gv: e.activation(out=gv, in_=psum[bank][:TP, :], func=AF.Gelu_apprx_tanh), r=[B_ps[bank]], w=B_gvb[kk])

        def vb_ln(ts):
            kk = ts % 2
            gv = sh3[:TP, 8 + kk, :]
            DV(lambda e, gv=gv, kk=kk: e.bn_stats(out=bst[:TP, kk, :], in_=gv), r=B_gvb[kk], w=[B_bst[kk]])
            DV(lambda e, kk=kk: e.bn_aggr(out=mv[:TP, kk, :], in_=bst[:TP, kk, :]), r=[B_bst[kk]], w=[B_mv[kk]])
            DV(lambda e, kk=kk: e.tensor_scalar(out=rsv[:TP, kk:kk + 1], in0=mv[:TP, kk, 1:2], scalar1=1.0, scalar2=EPS, op0=ALU.mult, op1=ALU.add),
               r=[B_mv[kk]], w=[B_rsv[kk]])
            S.op("pool", lambda e, kk=kk: e.tensor_tensor(out=rsv[:TP, kk:kk + 1], in0=rsv[:TP, kk:kk + 1], in1=mhalf[:TP, 0:1], op=ALU.pow),
                 reads=[B_rsv[kk], B_const], writes=[B_rsv[kk]])
            DV(lambda e, gv=gv, kk=kk: e.tensor_scalar(out=gv, in0=gv, scalar1=mv[:TP, kk, 0:1], scalar2=rsv[:TP, kk:kk + 1], op0=ALU.subtract, op1=ALU.mult),
               r=B_gvb[kk] + [B_mv[kk], B_rsv[kk]], w=B_gvb[kk])
            S.op("pool", lambda e, gv=gv: e.tensor_tensor(out=gv, in0=gv, in1=lng_b[:TP, :], op=ALU.mult), reads=B_gvb[kk] + [B_const], writes=B_gvb[kk])
            if not sample:
                S.op("pool", lambda e, gv=gv, ts=ts: e.tensor_tensor(out=vnb[:TP, ts, :], in0=gv, in1=lnb_b[:TP, :], op=ALU.add), reads=B_gvb[kk] + [B_const], writes=[B_vnb[ts]])
            else:
                S.op("pool", lambda e, gv=gv: e.tensor_tensor(out=gv, in0=gv, in1=lnb_b[:TP, :], op=ALU.add), reads=B_gvb[kk] + [B_const], writes=B_gvb[kk])
                AC(lambda e, gv=gv, ts=ts: e.activation(out=vnb[:TP, ts, :], in_=gv, func=AF.Copy), r=B_gvb[kk], w=[B_vnb[ts]])
                pool_toks.append(S.dma("pool", S.chan("ovs"), lambda e, gv=gv: e.dma_start(out=o_vs[:, :], in_=gv), reads=B_gvb[kk]))
        for ts in range(min(2, NTS)):
            vb_mm(ts)
            vb_ln(ts)
        for ct in range(4):
            sigA(ct, *gatesA(ct))
        for ts in range(2, NTS):
            vb_mm(ts)
            vb_ln(ts)
        wrel(g0 + 3)
        for ct in range(4):
            expA(ct)
        for ct in range(4):
            sqrtA(ct)
        for ct in range(2):
            postA(ct)
            postA_act(ct)
        sgu_banks = []
        for h_ in range(4):
            k2 = h_ % 2
            bank = nbank()
            sgu_banks.append(bank)
            if not sample:
                nmm = NTS * 3
                idx = 0
                for ts in range(NTS):
                    oc = psum[bank][:, ts * 128:(ts + 1) * 128]
                    trip = [(vnb[:, ts, h_ * 128:(h_ + 1) * 128], wsgu[:, h_, :]),
                            (ones_bf[0:1, :], bsh[0:1, h_ * 128:(h_ + 1) * 128]),
                            (ones_bf[0:1, :], bsl[0:1, h_ * 128:(h_ + 1) * 128])]
                    for l_, r_ in trip:
                        S.op("pe", lambda e, oc=oc, l_=l_, r_=r_, idx=idx: e.matmul(oc, l_, r_, start=(idx == 0), stop=(idx == nmm - 1), skip_group_check=True),
                             reads=[B_vnb[ts], B_const], writes=[B_ps[bank]], inc=(idx == nmm - 1))
                        idx += 1
            else:
                oc = psum[bank][:, 0:64]
                trip = [(vnb[0:64, 0, h_ * 128:(h_ + 1) * 128], wsgu_s[0:64, h_, :]),
                        (ones_bf[0:1, :], bsh[0:1, 512 + h_ * 64:512 + (h_ + 1) * 64]),
                        (ones_bf[0:1, :], bsl[0:1, 512 + h_ * 64:512 + (h_ + 1) * 64])]
                for idx, (l_, r_) in enumerate(trip):
                    S.op("pe", lambda e, oc=oc, l_=l_, r_=r_, idx=idx: e.matmul(oc, l_, r_, start=(idx == 0), stop=(idx == 2), skip_group_check=True),
                         reads=[B_vnb[0], B_const], writes=[B_ps[bank]], inc=(idx == 2))
        for ct in range(2, 4):
            postA(ct)
            postA_act(ct)
        if sample:
            outdma(o_hs[:, :], hsout[:, :, :].rearrange("p a b -> p (a b)"), [B_hsout])
        if i == 3:
            outdma(o_hp[:, :], hlast[:, :], B_hlast)
            outdma(o_cp[:, :], xhist[:, :, :].rearrange("p a b -> p (a b)"), B_xhist)

        for h_ in range(4):
            k2 = h_
            bank = sgu_banks[h_]
            ee = e4[:, k2, 0:T]
            ig = ig4[:, k2, 0:T]
            DV(lambda e, ee=ee, bank=bank, h_=h_: e.tensor_tensor(out=ee, in0=psum[bank][:, 0:T], in1=sh3[:, 4 + h_, 0:T], op=ALU.mult),
               r=[B_ps[bank]] + B_gub[h_], w=[B_e4[k2]])
            sqv = ig4[:, k2, :].bitcast(BF16)[:, 0:T]
            AC(lambda e, ee=ee, sqv=sqv: e.activation(out=sqv, in_=ee, func=AF.Square), r=[B_e4[k2]], w=[B_ig4[k2]])
            stats_mm(sqv, B_ig4[k2], 8)
            AC(lambda e, ee=ee, h_=h_: e.activation(out=yabT[:, 4 + h_, 0:T], in_=ee, func=AF.Identity, scale=cvec[:, C_GOB + h_:C_GOB + h_ + 1]),
               r=[B_e4[k2], B_const], w=[B_yab[4 + h_]])
        src8 = psum[sbank][:TP, 0:16].rearrange("p (h t two) -> p h t two", h=2, two=2)[:, :, 0:NTS, 0]
        ms8v = ms8[:TP, :].rearrange("p (h t) -> p h t", h=2)[:, :, 0:NTS]
        rstd8v = rstd8[:TP, :].rearrange("p (h t) -> p h t", h=2)[:, :, 0:NTS]
        mh8v = mhalf[:TP, :].rearrange("p (h t) -> p h t", h=2)[:, :, 0:NTS]
        DV(lambda e: e.tensor_scalar(out=ms8v, in0=src8, scalar1=1.0 / 512, scalar2=EPS, op0=ALU.mult, op1=ALU.add), r=[B_ps[sbank]], w=[B_ms8])
        S.op("pool", lambda e: e.tensor_tensor(out=rstd8v, in0=ms8v, in1=mh8v, op=ALU.pow), reads=[B_ms8, B_const], writes=[B_rstd8])
        reserved.discard(sbank)

        for half_ in range(2):
            k = wget(g0 + 4 + half_)
            for ts in range(NTS):
                for grp in range(2):
                    bank = nbank()
                    mm_group(bank, psum[bank][:TP, :], [(yabT[:, ct, ts * 128:ts * 128 + TP], ring[:, k, ct, :]) for ct in range(grp * 4, grp * 4 + 4)],
                             reads=[B_ring[k]] + B_yab)
                    hsl = hbuf[:TP, hb, ts, half_ * 512:(half_ + 1) * 512]
                    DV(lambda e, bank=bank, hsl=hsl, ts=ts, grp=grp: e.scalar_tensor_tensor(out=hsl, in0=psum[bank][:TP, :], scalar=rstd8[:TP, grp * 4 + ts:grp * 4 + ts + 1],
                                                                                       in1=hsl, op0=ALU.mult, op1=ALU.add),
                       r=[B_ps[bank], B_rstd8, B_h[hb][ts]], w=[B_h[hb][ts]])
            wrel(g0 + 4 + half_)

        norm(hb, C_GFFN, ti)
        if not sample:
            fwv = cvec[:, C_FW:C_FW + 144].rearrange("p (f j) -> p f j", j=3)
            DV(lambda e: e.tensor_tensor(out=corr[:, :, 0], in0=fhist[:, :, 0], in1=fwv[:, :, 0], op=ALU.mult), r=B_fhist + [B_const], w=[B_corr])
            DV(lambda e: e.tensor_tensor(out=corr[:, :, 1], in0=fhist[:, :, 1], in1=fwv[:, :, 1], op=ALU.mult), r=B_fhist + [B_const], w=[B_corr])
            DV(lambda e: e.tensor_tensor(out=corr[:, :, 0], in0=corr[:, :, 0], in1=corr[:, :, 1], op=ALU.add), r=[B_corr], w=[B_corr])
            DV(lambda e: e.tensor_tensor(out=corr[:, :, 1], in0=fhist[:, :, 1], in1=fwv[:, :, 0], op=ALU.mult), r=B_fhist + [B_const], w=[B_corr])
        for hg in range(6):
            kg = wget(g0 + 6 + 2 * hg)
            kl = wget(g0 + 7 + 2 * hg)
            for jj in range(4):
                j = hg * 4 + jj
                s2 = j % 2
                for part, kw in ((0, kg), (1, kl)):
                    ft = part * 24 + j
                    bank = nbank()
                    mm_group(bank, psum[bank][:, 0:T], [(ring[:, kw, kt, jj * 128:(jj + 1) * 128], nT[:, kt, 0:T]) for kt in range(8)],
                             reads=[B_ring[kw]] + B_nT)
                    cc = e4[:, 2 * s2 + part, :]
                    Bc = B_fc[s2][part]
                    w0 = cvec[:, C_FW + ft * 3:C_FW + ft * 3 + 1]
                    w1 = cvec[:, C_FW + ft * 3 + 1:C_FW + ft * 3 + 2]
                    w2 = cvec[:, C_FW + ft * 3 + 2:C_FW + ft * 3 + 3]
                    bb = cvec[:, C_FB + ft:C_FB + ft + 1]
                    pb = psum[bank]
                    if not sample:
                        AC(lambda e, cc=cc, pb=pb, w2=w2, bb=bb: e.activation(out=cc[:, 0:L], in_=pb[:, 0:L], func=AF.Identity, scale=w2, bias=bb),
                           r=[B_ps[bank], B_const], w=[Bc])
                        DV(lambda e, cc=cc, pb=pb, w1=w1: e.scalar_tensor_tensor(out=cc[:, 1:L], in0=pb[:, 0:L - 1], scalar=w1, in1=cc[:, 1:L], op0=ALU.mult, op1=ALU.add),
                           r=[B_ps[bank], B_const, Bc], w=[Bc])
                        DV(lambda e, cc=cc, pb=pb, w0=w0: e.scalar_tensor_tensor(out=cc[:, 2:L], in0=pb[:, 0:L - 2], scalar=w0, in1=cc[:, 2:L], op0=ALU.mult, op1=ALU.add),
                           r=[B_ps[bank], B_const, Bc], w=[Bc])
                        DV(lambda e, cc=cc, ft=ft: e.tensor_tensor(out=cc[:, 0:2], in0=cc[:, 0:2], in1=corr[:, ft, :], op=ALU.add),
                           r=[B_corr, Bc], w=[Bc])
                        AC(lambda e, pb=pb, ft=ft: e.activation(out=fhist[:, ft, :], in_=pb[:, L - 2:L], func=AF.Copy), r=[B_ps[bank]], w=[B_fhist[ft]])
                    else:
                        fx = fext[:, part, :, :]
                        Bx = B_fext[part]
                        ccv = cc[:, 0:T].rearrange("p (b l) -> p b l", l=L)
                        AC(lambda e, fx=fx, ft=ft: e.activation(out=fx[:, :, 0:2], in_=st_fT[:, ft, :, :], func=AF.Copy), r=[B_stf[ft], B_const], w=[Bx])
                        AC(lambda e, fx=fx, pb=pb: e.activation(out=fx[:, :, 2:6], in_=pb[:, 0:T].rearrange("p (b l) -> p b l", l=L), func=AF.Copy),
                           r=[B_ps[bank]], w=[Bx])
                        DV(lambda e, fx=fx, ccv=ccv, w0=w0, bb=bb: e.tensor_scalar(out=ccv, in0=fx[:, :, 0:4], scalar1=w0, scalar2=bb, op0=ALU.mult, op1=ALU.add),
                           r=[Bx, B_const], w=[Bc])
                        DV(lambda e, fx=fx, ccv=ccv, w1=w1: e.scalar_tensor_tensor(out=ccv, in0=fx[:, :, 1:5], scalar=w1, in1=ccv, op0=ALU.mult, op1=ALU.add),
                           r=[Bx, B_const, Bc], w=[Bc])
                        DV(lambda e, fx=fx, ccv=ccv, w2=w2: e.scalar_tensor_tensor(out=ccv, in0=fx[:, :, 2:6], scalar=w2, in1=ccv, op0=ALU.mult, op1=ALU.add),
                           r=[Bx, B_const, Bc], w=[Bc])
                        AC(lambda e, fx=fx, ft=ft: e.activation(out=st_fT[:, ft, :, :], in_=fx[:, :, 4:6], func=AF.Copy), r=[Bx], w=[B_stf[ft]])
                AC(lambda e, s2=s2: e.activation(out=ig4[:, s2, 0:T], in_=e4[:, 2 * s2, 0:T], func=AF.Gelu_apprx_tanh), r=[B_fc[s2][0]], w=[B_fG[s2]])
                S.op("pool", lambda e, s2=s2, j=j: e.tensor_tensor(out=actT[:, j, 0:T], in0=ig4[:, s2, 0:T], in1=e4[:, 2 * s2 + 1, 0:T], op=ALU.mult),
                     reads=[B_fG[s2], B_fc[s2][1]], writes=B_act[j])
            wrel(g0 + 6 + 2 * hg)
            wrel(g0 + 7 + 2 * hg)
        if i + 1 < ntiles:
            emit_xload(i + 1)
        if i == 3:
            outdma(o_fp[:, :], fhist[:, :, :].rearrange("p a b -> p (a b)"), B_fhist)
        if sample:
            outdma(o_fs[:, :], st_fT[:, :, :, :].rearrange("p a b c -> p (a b c)"), B_stf)
        for half_ in range(2):
            banks = [nbank() for _ in range(NTS)]
            for jg in range(3):
                gslot = g0 + 18 + half_ * 3 + jg
                k = wget(gslot)
                for ts in range(NTS):
                    for j8 in range(8):
                        j = jg * 8 + j8
                        S.op("pe", lambda e, ts=ts, j=j, j8=j8, jg=jg, k=k: e.matmul(psum[banks[ts]][:TP, :], actT[:, j, ts * 128:ts * 128 + TP], ring[:, k, j8, :],
                                                                                 start=(jg == 0 and j8 == 0), stop=(jg == 2 and j8 == 7), skip_group_check=True),
                             reads=[B_ring[k]] + B_act[j], writes=[B_ps[banks[ts]]], inc=(j8 == 7))
                wrel(gslot)
            for ts in range(NTS):
                hsl = hbuf[:TP, hb, ts, half_ * 512:(half_ + 1) * 512]
                DV(lambda e, ts=ts, hsl=hsl: e.tensor_tensor(out=hsl, in0=psum[banks[ts]][:TP, :], in1=hsl, op=ALU.add),
                   r=[B_ps[banks[ts]], B_h[hb][ts]], w=[B_h[hb][ts]])

        norm(hb, C_GPLE, ti)
        if i + 1 < ntiles:
            norm((i + 1) % 2, C_GMIX, tile_info(i + 1), phase="pre")
        kp = wget(g0 + 24)
        for half_ in range(2):
            kgt = wget(g0 + 25 + half_)
            for ts in range(NTS):
                s2 = ts % 2
                bg = nbank()
                mm_group(bg, psum[bg][:TP, :], [(nT[:, kt, ts * 128:ts * 128 + TP], ring[:, kgt, kt, :]) for kt in range(8)], reads=[B_ring[kgt]] + B_nT)
                bp = nbank()
                mm_group(bp, psum[bp][:TP, :], [(pTb[:, hb, kt, ts * 128:ts * 128 + TP], ring[:, kp, kt * 2 + half_, :]) for kt in range(2)],
                         reads=[B_ring[kp], B_pTb[hb]])
                AC(lambda e, s2=s2, bg=bg: e.activation(out=ig4[:TP, s2, :], in_=psum[bg][:TP, :], func=AF.Sigmoid), r=[B_ps[bg]], w=[B_gt[s2]])
                DV(lambda e, s2=s2, bp=bp: e.tensor_tensor(out=e4[:TP, 2 * s2, :], in0=psum[bp][:TP, :], in1=ig4[:TP, s2, :], op=ALU.mult),
                   r=[B_ps[bp], B_gt[s2]], w=[B_pt[s2]])
                hsl = hbuf[:TP, hb, ts, half_ * 512:(half_ + 1) * 512]
                DV(lambda e, s2=s2, hsl=hsl: e.tensor_tensor(out=hsl, in0=hsl, in1=e4[:TP, 2 * s2, :], op=ALU.add), r=[B_pt[s2], B_h[hb][ts]], w=[B_h[hb][ts]])
            wrel(g0 + 25 + half_)
        wrel(g0 + 24)

        if i + 1 < ntiles:
            norm((i + 1) % 2, C_GMIX, tile_info(i + 1), phase="post")

        for ts in range(NTS):
            AC(lambda e, ts=ts: e.activation(out=junk[:TP, :], in_=hbuf[:TP, hb, ts, :], func=AF.Square, accum_out=ss[:TP, ts:ts + 1]),
               r=[B_h[hb][ts]], w=[B_junk, B_ss])
        rstd_chain(ss[:TP, 0:NTS], ms[:TP, 0:NTS], rstd[:TP, 0:NTS], 1.0 / D, [B_ss], B_ms, B_rstd)
        for ts in range(NTS):
            hv = hbuf[:TP, hb, ts, :]
            DV(lambda e, ts=ts, hv=hv: e.scalar_tensor_tensor(out=hv, in0=hv, scalar=rstd[:TP, ts:ts + 1], in1=gfin_b[:TP, :], op0=ALU.mult, op1=ALU.mult),
               r=[B_h[hb][ts], B_rstd, B_const], w=[B_h[hb][ts]])
        if not sample:
            dst = y_p[i * 512:(i + 1) * 512, :].rearrange("(t p) f -> p t f", p=128)
            out_toks.append(S.dma("sp", ch_y[hb], lambda e, dst=dst, hb=hb: e.dma_start(out=dst, in_=hbuf[:, hb, :, :]), reads=B_h[hb]))
        else:
            out_toks.append(S.dma("sp", ch_y[hb], lambda e, hb=hb: e.dma_start(out=y_s[:, :], in_=hbuf[0:64, hb, 0, :]), reads=[B_h[hb][0]]))

    for tok in out_toks:
        S.wait("sp", tok)
    for tok in pool_toks:
        S.wait("pool", tok)

    S.replay()
    stack.close()
    return nc


def _prep_shared(inp):
    f = np.float32
    w_in = np.asarray(inp["w_in"][0], f); w_out = np.asarray(inp["w_out"][0], f)
    w_up = np.asarray(inp["w_up"][0], f); w_down = np.asarray(inp["w_down"][0], f)
    w_pg = np.asarray(inp["w_ple_gate"][0], f); w_ple = np.asarray(inp["w_ple"][0], f)
    wall = np.zeros((NSLOT, 128, 4096), f)
    wall[0:4] = w_in.reshape(8, 128, 4, 512).transpose(2, 1, 0, 3).reshape(4, 128, 4096)
    wall[4:6] = w_out.reshape(8, 128, 2, 512).transpose(2, 1, 0, 3).reshape(2, 128, 4096)
    wall[6:18] = w_up.reshape(8, 128, 2, 6, 512).transpose(3, 2, 1, 0, 4).reshape(12, 128, 4096)
    wall[18:24] = w_down.reshape(3, 8, 128, 2, 512).transpose(3, 0, 2, 1, 4).reshape(6, 128, 4096)
    wall[24, :, 0:2048] = w_ple.reshape(2, 128, 2, 512).transpose(1, 0, 2, 3).reshape(128, 2048)
    wall[25:27] = w_pg.reshape(8, 128, 2, 512).transpose(2, 1, 0, 3).reshape(2, 128, 4096)

    def col(v, n):
        return np.asarray(v, f).reshape(n, 128).T

    cv = np.zeros((128, NCV), f)
    cv[:, C_GMIX:C_GMIX + 8] = col(inp["g_mix_norm"][0], 8)
    cv[:, C_GFFN:C_GFFN + 8] = col(inp["g_ffn_norm"][0], 8)
    cv[:, C_GPLE:C_GPLE + 8] = col(inp["g_ple_norm"][0], 8)
    caw = np.asarray(inp["conv_a_w"][0], f)
    cv[:, C_CAW:C_CAW + 16] = caw.reshape(4, 4, 128).transpose(2, 1, 0).reshape(128, 16)
    cv[:, C_CAB:C_CAB + 4] = col(inp["conv_a_b"][0], 4)
    cv[:, C_BA:C_BA + 4] = col(inp["lru_ba"][0], 4)
    cv[:, C_BX:C_BX + 4] = col(inp["lru_bx"][0], 4)
    cv[:, C_AP:C_AP + 4] = col(inp["lru_a_param"][0], 4)
    cv[:, C_GOA:C_GOA + 4] = col(inp["g_out_a"][0], 4)
    cv[:, C_GOB:C_GOB + 4] = col(inp["g_out_b"][0], 4)
    fw = np.asarray(inp["ffn_conv_w"][0], f)
    cv[:, C_FW:C_FW + 144] = fw.reshape(3, 48, 128).transpose(2, 1, 0).reshape(128, 144)
    cv[:, C_FB:C_FB + 48] = col(inp["ffn_conv_b"][0], 48)

    sgu_w = np.asarray(inp["sgu_w"][0], f)
    sgu_b = np.asarray(inp["sgu_b"][0], f)
    sguT = sgu_w.transpose(2, 0, 1).reshape(128, 512)
    maskT = (np.arange(128)[:, None] <= np.arange(128)[None, :]).astype(f)
    w4 = sgu_w[:, 0:4, 0:4]
    Rs = np.broadcast_to(w4.transpose(2, 0, 1)[None, :, :, None, :], (16, 4, 4, 16, 4)).reshape(64, 256).astype(f)
    bb = np.arange(16)
    mask_s = ((bb[:, None, None, None] == bb[None, None, :, None]) &
              (np.arange(4)[None, :, None, None] <= np.arange(4)[None, None, None, :])).astype(f).reshape(64, 64)
    bsrow = sgu_b.reshape(1, 512)
    bsrow_s = np.broadcast_to(sgu_b[:, None, 0:4], (4, 16, 4)).reshape(1, 256).astype(f)

    def bd(w):
        w = np.asarray(w, f)
        o = np.zeros((128, 4, 128), f)
        for ct in range(4):
            o[0:64, ct, 0:64] = w[2 * ct]
            o[64:128, ct, 64:128] = w[2 * ct + 1]
        return o.reshape(128, 512)

    return dict(
        wall=wall, cvec=cv,
        gfin_b=np.ascontiguousarray(np.broadcast_to(np.asarray(inp["g_final"], f)[None, :], (128, D))),
        lng_b=np.ascontiguousarray(np.broadcast_to(np.asarray(inp["ln_v_g"][0], f)[None, :], (128, 512))),
        lnb_b=np.ascontiguousarray(np.broadcast_to(np.asarray(inp["ln_v_b"][0], f)[None, :], (128, 512))),
        ident=np.eye(128, dtype=f), maskT=maskT, sguT=np.ascontiguousarray(sguT),
        wabd=bd(inp["lru_wa"][0]), wxbd=bd(inp["lru_wx"][0]),
        bsrow=np.ascontiguousarray(bsrow), Rs=np.ascontiguousarray(Rs), mask_s=mask_s, bsrow_s=np.ascontiguousarray(bsrow_s),
    )


_NC_CACHE = {}


def kernel(**inp):
    f = np.float32
    shared = _prep_shared(inp)
    x_prompt = np.asarray(inp["x_prompt"], f); x_sample = np.asarray(inp["x_sample"], f)
    p_prompt = np.asarray(inp["p_prompt"], f); p_sample = np.asarray(inp["p_sample"], f)
    st_h = np.asarray(inp["state_rglru_h"], f); st_c = np.asarray(inp["state_rglru_conv"], f)
    st_f = np.asarray(inp["state_ffn_conv"], f)
    in_maps = []
    for c in range(NCORES):
        sl = slice(16 * c, 16 * c + 16)
        m = dict(shared)
        m["xp"] = np.ascontiguousarray(x_prompt[c])
        m["xs"] = np.ascontiguousarray(x_sample[sl].reshape(64, D))
        m["ppT"] = np.ascontiguousarray(p_prompt[0, c].T)
        m["psT"] = np.ascontiguousarray(p_sample[0, sl].reshape(64, 256).T)
        m["st_hT"] = np.ascontiguousarray(st_h[0, sl].reshape(16, 4, 128).transpose(2, 1, 0).reshape(128, 64))
        m["st_cT"] = np.ascontiguousarray(st_c[0, sl].reshape(16, 3, 4, 128).transpose(3, 2, 0, 1).reshape(128, 192))
        m["st_fT"] = np.ascontiguousarray(st_f[0, sl].reshape(16, 2, 48, 128).transpose(3, 2, 0, 1).reshape(128, 1536))
        in_maps.append(m)
    if "nc" not in _NC_CACHE:
        _NC_CACHE["nc"] = build_nc()
    res = run_bass_kernel_spmd(_NC_CACHE["nc"], in_maps, core_ids=list(range(NCORES)))
    R = res.results
    y_prompt = np.stack([np.asarray(R[c]["y_p"], f) for c in range(NCORES)], 0)
    y_sample = np.concatenate([np.asarray(R[c]["y_s"], f).reshape(16, 4, D) for c in range(NCORES)], 0)
    h_p = np.stack([np.asarray(R[c]["o_hp"], f).T.reshape(512) for c in range(NCORES)], 0)[None]
    h_s = np.concatenate([np.asarray(R[c]["o_hs"], f).reshape(128, 4, 16).transpose(2, 1, 0).reshape(16, 512) for c in range(NCORES)], 0)[None]
    c_p = np.stack([np.asarray(R[c]["o_cp"], f).reshape(128, 4, 3).transpose(2, 1, 0).reshape(3, 512) for c in range(NCORES)], 0)[None]
    c_s = np.concatenate([np.asarray(R[c]["o_cs"], f).reshape(128, 4, 16, 3).transpose(2, 3, 1, 0).reshape(16, 3, 512) for c in range(NCORES)], 0)[None]
    v_s = np.concatenate([np.asarray(R[c]["o_vs"], f).reshape(16, 4, 512) for c in range(NCORES)], 0)[None]
    f_p = np.stack([np.asarray(R[c]["o_fp"], f).reshape(128, 48, 2).transpose(2, 1, 0).reshape(2, 6144) for c in range(NCORES)], 0)[None]
    f_s = np.concatenate([np.asarray(R[c]["o_fs"], f).reshape(128, 48, 16, 2).transpose(2, 3, 1, 0).reshape(16, 2, 6144) for c in range(NCORES)], 0)[None]
    return (y_prompt, y_sample, np.ascontiguousarray(h_p), np.ascontiguousarray(h_s), np.ascontiguousarray(c_p),
            np.ascontiguousarray(c_s), np.ascontiguousarray(v_s), np.ascontiguousarray(f_p), np.ascontiguousarray(f_s))
```
